# Optimizing a Trainium2 kernel written in Bass

```python
import math
import jax, jax.numpy as jnp
from jax import lax
import numpy as np

D_MODEL = 2048
BATCH = 8
SEQ = 4096
DEPTH = 2
DEC_BATCH = 16
DEC_SEQ = 2048
PAST_LEN = 128

HEAD_DIM = 128
ROPE_THETA = 500000.0
ROT_FRAC_DIV = 4
Q_BLOCK = 128
EPS = 1e-6
NEG = -1e30
GRID_W = 64
WIN_ROWS = 8
WIN_COLS = 16
NA_HEADS = 4
DIL_WINDOWS = (128, 512, 2048)
DILATIONS = (1, 4, 16)
DIL_GROUPS = 3
DIL_HEADS = 2
DIL_SPAN = DIL_WINDOWS[0] // (2 * DILATIONS[0])
DIFF_HEADS = 6
DIFF_QK_DIM = 64
DIFF_V_DIM = 2 * DIFF_QK_DIM
D_FF = 4 * D_MODEL
NA_W = NA_HEADS * HEAD_DIM
DIL_W = DIL_GROUPS * DIL_HEADS * HEAD_DIM
DIL_OUT_W = DIL_HEADS * HEAD_DIM
DIFF_QK_W = DIFF_HEADS * 2 * DIFF_QK_DIM
DIFF_V_W = DIFF_HEADS * DIFF_V_DIM
IN_COLS = 3 * NA_W + 3 * DIL_W + 2 * DIFF_QK_W + DIFF_V_W

kernel_name = "hybrid_na_dilated_diff_encoder"


def rms_norm(x, g):
    xf = x.astype(jnp.float32)
    y = xf * lax.rsqrt(jnp.mean(xf * xf, axis=-1, keepdims=True) + EPS)
    return (y * g.astype(jnp.float32)).astype(x.dtype)


def partial_rope(x, pos):
    rot = x.shape[-1] // ROT_FRAC_DIV
    half = rot // 2
    inv = ROPE_THETA ** (-jnp.arange(half, dtype=jnp.float32) / half)
    ang = pos.astype(jnp.float32)[:, None] * inv[None, :]
    cos, sin = jnp.cos(ang), jnp.sin(ang)
    xr = x[..., :rot].astype(jnp.float32)
    x1, x2 = xr[..., :half], xr[..., half:]
    xr = jnp.concatenate([x1 * cos - x2 * sin, x2 * cos + x1 * sin], axis=-1)
    return jnp.concatenate([xr.astype(x.dtype), x[..., rot:]], axis=-1)


def neighbourhood_attention(q, k, v, rpb):
    B, H, S, dh = q.shape
    rows = S // GRID_W
    kh, kw = min(WIN_ROWS, rows), WIN_COLS
    qg = q.reshape(B, H, rows, GRID_W, dh)
    kg = k.reshape(B, H, rows, GRID_W, dh)
    vg = v.reshape(B, H, rows, GRID_W, dh)
    r_ids = jnp.arange(rows)
    c_ids = jnp.arange(GRID_W)
    row_start = jnp.clip(r_ids - kh // 2, 0, rows - kh)
    col_start = jnp.clip(c_ids - kw // 2, 0, GRID_W - kw)
    col_idx = col_start[:, None] + jnp.arange(kw)[None, :]
    dc = col_idx - c_ids[:, None]
    bias_c = rpb[:, :, dc + WIN_COLS - 1]
    scale = dh ** -0.5

    def one_row(rr):
        rs = row_start[rr]
        q_row = lax.dynamic_index_in_dim(qg, rr, axis=2, keepdims=False)
        k_band = lax.dynamic_slice_in_dim(kg, rs, kh, axis=2)
        v_band = lax.dynamic_slice_in_dim(vg, rs, kh, axis=2)
        k_sel = k_band[:, :, :, col_idx]
        v_sel = v_band[:, :, :, col_idx]
        s = jnp.einsum('bhwd,bhrwkd->bhwrk', q_row, k_sel,
                       preferred_element_type=jnp.float32) * scale
        dr = rs + jnp.arange(kh) - rr
        bias = bias_c[:, dr + WIN_ROWS - 1].transpose(0, 2, 1, 3)
        s = s + bias[None].astype(jnp.float32)
        p = jax.nn.softmax(s.reshape(B, H, GRID_W, kh * kw), axis=-1)
        p = p.reshape(B, H, GRID_W, kh, kw).astype(v.dtype)
        return jnp.einsum('bhwrk,bhrwkd->bhwd', p, v_sel)

    out = lax.map(one_row, r_ids)
    return out.transpose(1, 2, 0, 3, 4).reshape(B, H, S, dh)


def dilated_attention(q, k, v):
    B, G, Hg, S, dh = q.shape
    nblk = S // Q_BLOCK
    offs = jnp.arange(-DIL_SPAN, DIL_SPAN + 1)
    dil = jnp.array(DILATIONS, dtype=jnp.int32)
    scale = dh ** -0.5
    take_g = jax.vmap(lambda kk, ii: jnp.take(kk, ii, axis=2), in_axes=(1, 0), out_axes=1)

    def one_block(i):
        t = i * Q_BLOCK + jnp.arange(Q_BLOCK)
        idx = t[None, :, None] + dil[:, None, None] * offs[None, None, :]
        valid = (idx >= 0) & (idx < S)
        idxc = jnp.clip(idx, 0, S - 1)
        qb = lax.dynamic_slice_in_dim(q, i * Q_BLOCK, Q_BLOCK, axis=3)
        kb = take_g(k, idxc)
        vb = take_g(v, idxc)
        s = jnp.einsum('bghqd,bghqnd->bghqn', qb, kb,
                       preferred_element_type=jnp.float32) * scale
        s = jnp.where(valid[None, :, None], s, NEG)
        lse = jax.nn.logsumexp(s, axis=-1)
        p = jnp.exp(s - lse[..., None])
        o = jnp.einsum('bghqn,bghqnd->bghqd', p.astype(v.dtype), vb,
                       preferred_element_type=jnp.float32)
        alpha = jax.nn.softmax(lse, axis=1)
        return jnp.sum(alpha[..., None] * o, axis=1).astype(v.dtype)

    out = lax.map(one_block, jnp.arange(nblk))
    return out.transpose(1, 2, 0, 3, 4).reshape(B, Hg, S, dh)


def diff_attention(q, k, v, lam):
    B, H, _, S, dc = q.shape
    nblk = S // Q_BLOCK
    scale = dc ** -0.5
    qblocks = jnp.moveaxis(q.reshape(B, H, 2, nblk, Q_BLOCK, dc), 3, 0)

    def one_block(qb):
        s = jnp.einsum('bhcqd,bhckd->bhcqk', qb, k,
                       preferred_element_type=jnp.float32) * scale
        p = jax.nn.softmax(s, axis=-1)
        a = p[:, :, 0] - lam * p[:, :, 1]
        return jnp.einsum('bhqk,bhkd->bhqd', a.astype(v.dtype), v)

    out = lax.map(one_block, qblocks)
    return out.transpose(1, 2, 0, 3, 4).reshape(B, H, S, v.shape[-1])


def encoder_layer(x, c, layer_idx, norm1_g, w_ada, b_ada, w_in, qn_a, kn_a, rpb_a,
                  qn_b, kn_b, qn_c, kn_c, lam_q1, lam_k1, lam_q2, lam_k2, subln_c,
                  w_br_a, w_br_b, w_br_c, w_gate, b_gate, w_out, norm2_g, w_ff1, w_ff2):
    B, S, _ = x.shape
    pos = jnp.arange(S)
    mod = (jax.nn.silu(c) @ w_ada + b_ada).astype(x.dtype)[:, None, :]
    sh1, sc1, g1, sh2, sc2, g2 = jnp.split(mod, 6, axis=-1)

    h = rms_norm(x, norm1_g) * (1 + sc1) + sh1
    z = h @ w_in
    widths = [NA_W, NA_W, NA_W, DIL_W, DIL_W, DIL_W, DIFF_QK_W, DIFF_QK_W]
    splits = [int(s) for s in np.cumsum(widths)]
    qa, ka, va, qb, kb, vb, qc, kc, vc = jnp.split(z, splits, axis=-1)

    to_na = lambda t: t.reshape(B, S, NA_HEADS, HEAD_DIM).transpose(0, 2, 1, 3)
    oa = neighbourhood_attention(rms_norm(to_na(qa), qn_a), rms_norm(to_na(ka), kn_a),
                                 to_na(va), rpb_a)

    to_dil = lambda t: t.reshape(B, S, DIL_GROUPS, DIL_HEADS, HEAD_DIM).transpose(0, 2, 3, 1, 4)
    ob = dilated_attention(partial_rope(rms_norm(to_dil(qb), qn_b), pos),
                           partial_rope(rms_norm(to_dil(kb), kn_b), pos),
                           to_dil(vb))

    to_diff = lambda t: t.reshape(B, S, DIFF_HEADS, 2, DIFF_QK_DIM).transpose(0, 2, 3, 1, 4)
    lam_init = 0.8 - 0.6 * math.exp(-0.3 * layer_idx)
    lam = (jnp.exp(jnp.sum(lam_q1.astype(jnp.float32) * lam_k1.astype(jnp.float32)))
           - jnp.exp(jnp.sum(lam_q2.astype(jnp.float32) * lam_k2.astype(jnp.float32)))
           + lam_init)
    vcs = vc.reshape(B, S, DIFF_HEADS, DIFF_V_DIM).transpose(0, 2, 1, 3)
    oc = diff_attention(partial_rope(rms_norm(to_diff(qc), qn_c), pos),
                        partial_rope(rms_norm(to_diff(kc), kn_c), pos), vcs, lam)
    oc = rms_norm(oc, subln_c) * (1.0 - lam_init)

    ya = oa.transpose(0, 2, 1, 3).reshape(B, S, NA_W) @ w_br_a
    yb = ob.transpose(0, 2, 1, 3).reshape(B, S, DIL_OUT_W) @ w_br_b
    yc = oc.transpose(0, 2, 1, 3).reshape(B, S, DIFF_V_W) @ w_br_c
    ga, gb, gc = jnp.split(jax.nn.sigmoid(h @ w_gate + b_gate), 3, axis=-1)
    x = x + g1 * ((ga * ya + gb * yb + gc * yc) @ w_out)

    h2 = rms_norm(x, norm2_g) * (1 + sc2) + sh2
    f = jnp.square(jax.nn.relu(h2 @ w_ff1)) @ w_ff2
    return x + g2 * f


def setup_inputs(seed: int = 0) -> dict:
    key = jax.random.key(seed)
    ks = jax.random.split(key, 29)

    def nrm(i, shape, scale):
        return jax.random.normal(ks[i], shape, jnp.float32) * scale

    D = D_MODEL
    return {
        "x_prompt": nrm(0, (BATCH, SEQ, D), 1.0),
        "x_sample": nrm(1, (DEC_BATCH, DEC_SEQ, D), 1.0),
        "c_prompt": nrm(2, (BATCH, D), 1.0),
        "c_sample": nrm(3, (DEC_BATCH, D), 1.0),
        "norm1_g": 1.0 + nrm(4, (DEPTH, D), 0.05),
        "w_ada": nrm(5, (DEPTH, D, 6 * D), 0.5 * D ** -0.5),
        "b_ada": nrm(6, (DEPTH, 6 * D), 0.02),
        "w_in": nrm(7, (DEPTH, D, IN_COLS), D ** -0.5),
        "qn_a": 1.0 + nrm(8, (DEPTH, HEAD_DIM), 0.05),
        "kn_a": 1.0 + nrm(9, (DEPTH, HEAD_DIM), 0.05),
        "rpb_a": nrm(10, (DEPTH, NA_HEADS, 2 * WIN_ROWS - 1, 2 * WIN_COLS - 1), 0.5),
        "qn_b": 1.0 + nrm(11, (DEPTH, HEAD_DIM), 0.05),
        "kn_b": 1.0 + nrm(12, (DEPTH, HEAD_DIM), 0.05),
        "qn_c": 1.0 + nrm(13, (DEPTH, DIFF_QK_DIM), 0.05),
        "kn_c": 1.0 + nrm(14, (DEPTH, DIFF_QK_DIM), 0.05),
        "lam_q1": nrm(15, (DEPTH, DIFF_QK_DIM), 0.1),
        "lam_k1": nrm(16, (DEPTH, DIFF_QK_DIM), 0.1),
        "lam_q2": nrm(17, (DEPTH, DIFF_QK_DIM), 0.1),
        "lam_k2": nrm(18, (DEPTH, DIFF_QK_DIM), 0.1),
        "subln_c": 1.0 + nrm(19, (DEPTH, DIFF_V_DIM), 0.05),
        "w_br_a": nrm(20, (DEPTH, NA_W, D), NA_W ** -0.5),
        "w_br_b": nrm(21, (DEPTH, DIL_OUT_W, D), DIL_OUT_W ** -0.5),
        "w_br_c": nrm(22, (DEPTH, DIFF_V_W, D), DIFF_V_W ** -0.5),
        "w_gate": nrm(23, (DEPTH, D, 3 * D), D ** -0.5),
        "b_gate": nrm(24, (DEPTH, 3 * D), 0.02),
        "w_out": nrm(25, (DEPTH, D, D), D ** -0.5),
        "norm2_g": 1.0 + nrm(26, (DEPTH, D), 0.05),
        "w_ff1": nrm(27, (DEPTH, D, D_FF), D ** -0.5),
        "w_ff2": nrm(28, (DEPTH, D_FF, D), D_FF ** -0.5),
    }


def reference(x_prompt, x_sample, c_prompt, c_sample, norm1_g, w_ada, b_ada, w_in,
              qn_a, kn_a, rpb_a, qn_b, kn_b, qn_c, kn_c, lam_q1, lam_k1, lam_q2, lam_k2,
              subln_c, w_br_a, w_br_b, w_br_c, w_gate, b_gate, w_out, norm2_g, w_ff1, w_ff2):
    y_prompt, y_sample = x_prompt, x_sample
    for l in range(DEPTH):
        p = (norm1_g[l], w_ada[l], b_ada[l], w_in[l], qn_a[l], kn_a[l], rpb_a[l],
             qn_b[l], kn_b[l], qn_c[l], kn_c[l], lam_q1[l], lam_k1[l], lam_q2[l], lam_k2[l],
             subln_c[l], w_br_a[l], w_br_b[l], w_br_c[l], w_gate[l], b_gate[l], w_out[l],
             norm2_g[l], w_ff1[l], w_ff2[l])
        y_prompt = encoder_layer(y_prompt, c_prompt, l, *p)
        y_sample = encoder_layer(y_sample, c_sample, l, *p)
    return (y_prompt, y_sample)
```

```python
import itertools
import math
import numpy as np
import ml_dtypes
import concourse.bass as bass
import concourse.mybir as mybir
from concourse.bass_utils import run_bass_kernel_spmd

F32 = mybir.dt.float32
BF16 = mybir.dt.bfloat16
AF = mybir.ActivationFunctionType
ALU = mybir.AluOpType
AX = mybir.AxisListType

D = 2048
KC = 16
DFF = 8192
INC = 6144
T = 512
EPS = 1e-6
PADK = 1024
NSEM = {"cast": 40, "sp": 28, "pool": 28}
CSTEP = 512
NCASE_TILES = None

V_N1G, V_N2G, V_BG, V_QNA, V_KNA, V_QNB, V_KNB, V_QNC, V_KNC, V_SUB = 0, 16, 32, 80, 81, 82, 83, 84, 85, 86
V_LAM = 87
V_BADA = V_LAM + 256
NV = V_BADA + 96


class Op:
    __slots__ = ("eng", "fn", "deps", "signal", "semkey", "val", "dma")

    def __init__(self, eng, fn, dma):
        self.eng = eng
        self.fn = fn
        self.dma = dma
        self.deps = []
        self.signal = False
        self.semkey = None
        self.val = 0


class Sched:
    ENGS = ("pe", "act", "dve", "pool", "sp")

    def __init__(self):
        self.q = {e: [] for e in self.ENGS}
        self.lastw = {}
        self.readers = {}
        self.last_op = {}
        self.dma_rr = {}
        self.dma_last = {}
        self.uid = 0

    def add(self, eng, fn, reads=(), writes=(), dma=False, pool=None):
        o = Op(eng, fn, dma)
        deps = {}
        psr = [r for r in reads if r[0] == "ps"]
        if psr:
            reads = [r for r in reads if r[0] != "ps"]
            writes = list(writes) + psr
        for r in reads:
            w = self.lastw.get(r)
            if w is not None:
                deps[id(w)] = (w, True)
        for r in writes:
            w = self.lastw.get(r)
            if w is not None:
                deps[id(w)] = (w, True)
            rd = self.readers.get(r)
            if rd:
                for x in rd.values():
                    if id(x) not in deps:
                        deps[id(x)] = (x, False)
        for d, is_w in deps.values():
            if (not dma) and (not d.dma) and d.eng == eng:
                if eng == "pe" or not is_w:
                    continue
            o.deps.append(d)
            d.signal = True
        if dma:
            pool = pool or eng
            n = self.dma_rr.get(pool, 0)
            self.dma_rr[pool] = n + 1
            k = (pool, n % NSEM[pool])
            prev = self.dma_last.get(k)
            if prev is not None:
                o.deps.append(prev)
            self.dma_last[k] = o
            o.semkey = k
            o.val = (prev.val if prev is not None else 0) + 16
            o.signal = True
        self.uid += 1
        for r in reads:
            self.readers.setdefault(r, {})[("d", self.uid) if dma else eng] = o
        for r in writes:
            self.lastw[r] = o
            self.readers[r] = {}
        self.q[eng].append(o)
        if not dma:
            self.last_op[eng] = o
        return o

    def barrier(self, include_cast=False):
        lasts = list(self.last_op.values())
        for k, o in self.dma_last.items():
            if k[0] == "cast" and not include_cast:
                continue
            lasts.append(o)
        for o in lasts:
            o.signal = True
        for e in self.ENGS:
            b = Op(e, None, False)
            b.deps = [o for o in lasts if o.dma or o.eng != e]
            self.q[e].append(b)
        keepw = {k: v for k, v in self.lastw.items() if k[0] == "W"}
        self.lastw = keepw
        self.readers = {}
        self.last_op = {}

    def sem_keys(self):
        keys = ["pe", "act", "dve"]
        for pool, n in self.dma_rr.items():
            for i in range(min(n, NSEM[pool])):
                keys.append((pool, i))
        return keys

    def finalize(self):
        for e in ("pe", "act", "dve"):
            cnt = 0
            for o in self.q[e]:
                if o.dma or o.fn is None:
                    continue
                if o.signal:
                    cnt += 1
                    o.semkey = e
                    o.val = cnt

    def run(self, e, eng, sems):
        seen = {}
        for o in self.q[e]:
            for d in o.deps:
                if seen.get(d.semkey, 0) >= d.val:
                    continue
                eng.wait_ge(sems[d.semkey], d.val)
                seen[d.semkey] = d.val
            if o.fn is None:
                continue
            ins = o.fn(eng)
            if o.signal:
                ins.then_inc(sems[o.semkey], 16 if o.dma else 1)


def _rs(r, R):
    return min(max(r - 4, 0), R - 8)


def na_plan(R):
    plan = []
    for i in range(R // 2):
        lo = _rs(2 * i, R)
        hi = _rs(2 * i + 1, R) + 7
        js = list(range(lo // 2, hi // 2 + 1))
        sig = []
        for j in js:
            s = []
            for kr in range(2):
                for qr in range(2):
                    Rk, Rq = 2 * j + kr, 2 * i + qr
                    valid = _rs(Rq, R) <= Rk <= _rs(Rq, R) + 7
                    s.append((Rk - Rq, valid))
            sig.append(tuple(s))
        plan.append((js, tuple(sig)))
    return plan


def na_cases():
    cases = {}
    off = 0
    for R in (64, 32):
        for js, sig in na_plan(R):
            if sig not in cases:
                cases[sig] = (off, len(js))
                off += len(js)
    return cases, off


def na_index_table():
    cases, nt = na_cases()
    tab = -np.ones((128, nt * 128), dtype=np.int64)
    kc = np.arange(64)
    qc = np.arange(64)
    cs = np.clip(qc - 8, 0, 48)
    colvalid = (kc[:, None] >= cs[None, :]) & (kc[:, None] < cs[None, :] + 16)
    colidx = kc[:, None] - qc[None, :] + 15
    for sig, (off, n) in cases.items():
        for t in range(n):
            s = sig[t]
            for kr in range(2):
                for qr in range(2):
                    dr, valid = s[kr * 2 + qr]
                    if not valid:
                        continue
                    blk = np.where(colvalid, (dr + 7) * 31 + colidx, -1)
                    tab[kr * 64:(kr + 1) * 64, (off + t) * 128 + qr * 64:(off + t) * 128 + (qr + 1) * 64] = blk
    return tab


def build(seqs=(4096, 2048, 2048), n_layers=2, debug=False, stop=99):
    nseq = len(seqs)
    TOK = sum(seqs)
    SMAX = max(seqs)
    SP = SMAX + 2 * PADK
    cases, NT = na_cases()
    nc = bass.Bass("TRN2", target_bir_lowering=False)

    def din(name, shape, dt=F32):
        return nc.dram_tensor(name, list(shape), dt, kind="ExternalInput").ap()

    DBG = ("mod_d", "qk_d", "va_d", "vb_d", "vc_d", "hT_d", "oT_d", "xmid_d")

    def dint(name, shape, dt=F32):
        kind = "ExternalOutput" if (debug and name in DBG) else "Internal"
        return nc.dram_tensor(name, list(shape), dt, kind=kind).ap()

    xin = din("xin", [TOK, D])
    cT = din("cT", [128, KC, nseq])
    w_ada = din("w_ada", [n_layers, D, 6 * D])
    b_ada3 = din("b_ada3", [n_layers, nseq, 6 * D])
    w_in = din("w_in", [n_layers, D, INC])
    w_gate = din("w_gate", [n_layers, D, 3 * D])
    w_bra = din("w_br_a", [n_layers, 512, D])
    w_brb = din("w_br_b", [n_layers, 256, D])
    w_brc = din("w_br_c", [n_layers, 768, D])
    w_out = din("w_out", [n_layers, D, D])
    w_ff1 = din("w_ff1", [n_layers, D, DFF])
    w_ff2 = din("w_ff2", [n_layers, DFF, D])
    vec_fm = din("vec_fm", [n_layers, 128, NV])
    cbf = din("cbf", [128, 8 * 128 + 256], BF16)
    ropeB = din("ropeB", [2, 128, SMAX])
    ropeC = din("ropeC", [2, 128, SMAX])
    rpbx = din("rpbx", [n_layers, 4, 128, NT * 128])
    zrow = din("zrow", [128, 1024], BF16)
    selc = din("selc", [4, 4 * 128])
    yout = nc.dram_tensor("yout", [TOK, D], F32, kind="ExternalOutput").ap()

    mod_d = dint("mod_d", [n_layers, nseq, 6 * D])
    win_d = dint("win_d", [n_layers, D, INC], BF16)
    wgate_d = dint("wgate_d", [n_layers, D, 3 * D], BF16)
    wbr_d = dint("wbr_d", [n_layers, 1536, D], BF16)
    wout_d = dint("wout_d", [n_layers, D, D], BF16)
    wff1_d = dint("wff1_d", [n_layers, D, DFF], BF16)
    wff2_d = dint("wff2_d", [n_layers, DFF, D], BF16)
    qk_d = dint("qk_d", [32, 128, SP], BF16)
    va_d = dint("va_d", [SMAX, 512], BF16)
    vb_d = dint("vb_d", [SMAX + 2 * PADK, 768], BF16)
    vc_d = dint("vc_d", [SMAX, 768], BF16)
    hT_d = dint("hT_d", [KC, 128, SMAX], BF16)
    oT_d = dint("oT_d", [12, 128, SMAX], BF16)
    xmid_d = dint("xmid_d", [SMAX, D])
    x1_d = dint("x1_d", [TOK, D])

    S = Sched()
    ARENA = 52000

    ctx = {}

    def emit_all():
        pass

    import contextlib
    stack = contextlib.ExitStack()
    arena = stack.enter_context(nc.sbuf_tensor("arena", [128, ARENA], F32))
    ps = stack.enter_context(nc.psum_tensor("ps", [128, 8, 512], F32))
    psflat = ps.rearrange("p b n -> p (b n)")

    class Bump:
        def __init__(self):
            self.off = 0
            self.n = 0

        def f32(self, cols):
            a = arena[:, self.off:self.off + cols]
            self.off += cols
            assert self.off <= ARENA, ("SBUF arena overflow", self.off)
            return a

        def bf(self, cols):
            w = (cols + 1) // 2
            a = arena[:, self.off:self.off + w].bitcast(BF16)
            self.off += w
            assert self.off <= ARENA, ("SBUF arena overflow", self.off)
            return a[:, 0:cols]

        def key(self, name):
            self.n += 1
            return (name, self.n)

    B = Bump()

    class Ring:
        def __init__(self, name, n, cols, dt):
            self.bufs = [(B.f32(cols) if dt == F32 else B.bf(cols), B.key(name)) for _ in range(n)]
            self.i = 0

        def next(self):
            r = self.bufs[self.i % len(self.bufs)]
            self.i += 1
            return r

    bank_rr = [0]

    def next_bank(lo=0, hi=8):
        b = lo + bank_rr[0] % (hi - lo)
        bank_rr[0] += 1
        return b

    def PSK(b):
        return ("ps", b)

    cb = B.bf(8 * 128 + 256)
    ident = cb[:, 0:128]
    ones128 = cb[:, 128:256]
    ones64 = cb[:, 256:384]
    ones_lo = cb[:, 384:512]
    ones_hi = cb[:, 512:640]
    permB = cb[:, 640:768]
    permC = cb[:, 768:896]
    band = cb[:, 1024:1280]
    vecs = [B.f32(NV) for _ in range(n_layers)]
    lamt = [B.f32(8) for _ in range(n_layers)]
    modfm = [B.f32(96 * nseq) for _ in range(n_layers)]
    sel = B.f32(4 * 128)
    S.add("sp", lambda e: e.dma_start(out=sel[0:4, :], in_=selc), writes=[("sel", 0)], dma=True)
    KCONST = ("const", 0)
    S.add("sp", lambda e: e.dma_start(out=cb, in_=cbf), writes=[KCONST], dma=True)
    for l in range(n_layers):
        S.add("sp", lambda e, l=l: e.dma_start(out=vecs[l], in_=vec_fm[l]), writes=[("vec", l)], dma=True)
    persist_mark = B.off

    def dma_ld(out, in_, reads, writes, slow=False):
        if slow:
            return S.add("sp", lambda e: e.dma_start(out=out, in_=in_, allow_slow_non_contiguous=True), reads=reads, writes=writes, dma=True)
        return S.add("sp", lambda e: e.dma_start(out=out, in_=in_), reads=reads, writes=writes, dma=True)

    def dma_st(out, in_, reads, writes):
        return S.add("pool", lambda e: e.dma_start(out=out, in_=in_), reads=reads, writes=writes, dma=True)

    def dma_cast(out, in_, reads, writes):
        return S.add("pool", lambda e: e.dma_start(out=out, in_=in_), reads=reads, writes=writes, dma=True, pool="cast")

    def pe_mm(out, lhsT, rhs, start, stop, reads, writes):
        return S.add("pe", lambda e: e.matmul(out, lhsT=lhsT, rhs=rhs, start=start, stop=stop), reads=reads, writes=writes)

    def act(out, in_, func, reads, writes, scale=1.0, bias=0.0, accum_out=None):
        if accum_out is None:
            fn = lambda e: e.activation(out=out, in_=in_, func=func, bias=bias, scale=scale)
        else:
            fn = lambda e: e.activation(out=out, in_=in_, func=func, bias=bias, scale=scale, accum_out=accum_out)
        return S.add("act", fn, reads=reads, writes=writes)

    def dve(fn, reads, writes):
        return S.add("dve", fn, reads=reads, writes=writes)

    def rstd_from(ssb_ap, n, out_f32, reads, wkey, tmp, tmpkey):
        act(tmp, ssb_ap, AF.Ln, reads=reads, writes=[tmpkey], scale=1.0 / n, bias=EPS)
        act(out_f32, tmp, AF.Exp, reads=[tmpkey], writes=[wkey], scale=-0.5)

    def cast_weight(src, dst, name, l, rows):
        step = CSTEP
        for r0 in range(0, rows, step):
            dma_cast(dst[r0:min(r0 + step, rows), :], src[r0:min(r0 + step, rows), :], reads=[], writes=[("W", l, name, r0 // step)])

    def wkeys(l, name, rows):
        return [("W", l, name, i) for i in range((rows + CSTEP - 1) // CSTEP)]

    zt = B.bf(1024)
    S.add("sp", lambda e: e.dma_start(out=zt, in_=zrow), writes=[("zt", 0)], dma=True)
    for j in range(14, 20):
        dma_st(qk_d[j, :, 0:PADK], zt, reads=[("zt", 0)], writes=[("padk", j, 0)])
        dma_st(qk_d[j, :, PADK + SMAX:PADK + SMAX + PADK], zt, reads=[("zt", 0)], writes=[("padk", j, 1)])
    for r0 in range(0, PADK, 128):
        dma_st(vb_d[r0:r0 + 128, :], zt[:, 0:768], reads=[("zt", 0)], writes=[("padv", r0, 0)])
        dma_st(vb_d[PADK + SMAX + r0:PADK + SMAX + r0 + 128, :], zt[:, 0:768], reads=[("zt", 0)], writes=[("padv", r0, 1)])

    persist_mark = B.off

    def issue_casts(l, first):
        if first:
            cast_weight(w_in[l], win_d[l], "win", l, D)
            return
        cast_weight(w_gate[l], wgate_d[l], "wgate", l, D)
        cast_weight(w_bra[l], wbr_d[l, 0:512, :], "wbra", l, 512)
        cast_weight(w_brb[l], wbr_d[l, 512:768, :], "wbrb", l, 256)
        cast_weight(w_brc[l], wbr_d[l, 768:1536, :], "wbrc", l, 768)
        cast_weight(w_out[l], wout_d[l], "wout", l, D)
        cast_weight(w_ff1[l], wff1_d[l], "wff1", l, D)
        cast_weight(w_ff2[l], wff2_d[l], "wff2", l, DFF)

    def prologue():
        B.off = persist_mark
        cs = B.f32(KC * nseq)
        csb = B.bf(KC * nseq)
        dma_ld(cs, cT.rearrange("p k s -> p (k s)"), reads=[], writes=[("cs", 0)])
        act(csb, cs, AF.Silu, reads=[("cs", 0)], writes=[("csb", 0)])
        csb3 = csb.rearrange("p (k s) -> p k s", s=nseq)
        wr = Ring("wada", 3, KC * 512 // 2, F32)
        br = Ring("bada", 2, 512, F32)
        orr = Ring("modo", 2, 512, F32)
        for l in range(n_layers):
            for cbk in range(24):
                wt, wk = wr.next()
                wtb = wt.bitcast(BF16).rearrange("p (k n) -> p k n", k=KC)
                c0 = cbk * 512
                S.add("pool", lambda e, wtb=wtb, l=l, c0=c0: e.dma_start(
                    out=wtb, in_=w_ada[l].rearrange("(k p) n -> p k n", p=128)[:, :, c0:c0 + 512]),
                    reads=[], writes=[wk], dma=True)
                bt, bk = br.next()
                dma_ld(bt[0:nseq, :], b_ada3[l, :, c0:c0 + 512], reads=[], writes=[bk])
                b = next_bank()
                for k in range(KC):
                    pe_mm(ps[0:nseq, b, :], csb3[:, k, :], wtb[:, k, :], k == 0, k == KC - 1,
                          reads=[wk, ("csb", 0)], writes=[PSK(b)])
                ot, ok = orr.next()
                dve(lambda e, ot=ot, b=b, bt=bt: e.tensor_tensor(out=ot[0:nseq, :], in0=ps[0:nseq, b, :], in1=bt[0:nseq, :], op=ALU.add),
                    reads=[PSK(b), bk], writes=[ok])
                dma_st(mod_d[l, :, c0:c0 + 512], ot[0:nseq, :], reads=[ok], writes=[("mod", l, cbk)])
                b2 = next_bank()
                mf3 = modfm[l].rearrange("p (c s) -> p c s", s=nseq)
                for cc in range(4):
                    for k in range(KC):
                        pe_mm(ps[:, b2, cc * nseq:(cc + 1) * nseq], wtb[:, k, cc * 128:(cc + 1) * 128], csb3[:, k, :], k == 0, k == KC - 1,
                              reads=[wk, ("csb", 0)], writes=[PSK(b2)])
                for cc in range(4):
                    c = cbk * 4 + cc
                    dve(lambda e, l=l, c=c, cc=cc, b2=b2, mf3=mf3: e.tensor_scalar(
                        out=mf3[:, c, :], in0=ps[:, b2, cc * nseq:(cc + 1) * nseq], scalar1=vecs[l][:, V_BADA + c:V_BADA + c + 1],
                        scalar2=None, op0=ALU.add), reads=[PSK(b2), ("vec", l)], writes=[("modfm", l, c)])
            v = vecs[l]
            lt = lamt[l]
            tmp = B.f32(128)
            tk = B.key("lamtmp")
            lam_init = 0.8 - 0.6 * math.exp(-0.3 * l)
            dve(lambda e, v=v, tmp=tmp: e.tensor_tensor(out=tmp[:, 0:64], in0=v[:, V_LAM:V_LAM + 64], in1=v[:, V_LAM + 64:V_LAM + 128], op=ALU.mult),
                reads=[("vec", l)], writes=[tk])
            dve(lambda e, v=v, tmp=tmp: e.tensor_tensor(out=tmp[:, 64:128], in0=v[:, V_LAM + 128:V_LAM + 192], in1=v[:, V_LAM + 192:V_LAM + 256], op=ALU.mult),
                reads=[("vec", l), tk], writes=[tk])
            dve(lambda e, lt=lt, tmp=tmp: e.reduce_sum(out=lt[:, 2:3], in_=tmp[:, 0:64], axis=AX.X), reads=[tk], writes=[("lam", l, 2)])
            dve(lambda e, lt=lt, tmp=tmp: e.reduce_sum(out=lt[:, 3:4], in_=tmp[:, 64:128], axis=AX.X), reads=[tk], writes=[("lam", l, 3)])
            act(lt[:, 4:6], lt[:, 2:4], AF.Exp, reads=[("lam", l, 2), ("lam", l, 3)], writes=[("lam", l, 4)])
            dve(lambda e, lt=lt: e.tensor_tensor(out=lt[:, 6:7], in0=lt[:, 5:6], in1=lt[:, 4:5], op=ALU.subtract),
                reads=[("lam", l, 4)], writes=[("lam", l, 6)])
            dve(lambda e, lt=lt, li=lam_init: e.tensor_scalar(out=lt[:, 0:1], in0=lt[:, 6:7], scalar1=-li, scalar2=None, op0=ALU.add),
                reads=[("lam", l, 6)], writes=[("lam", l, 0)])
            dve(lambda e, lt=lt, v=v, li=lam_init: e.tensor_scalar(out=lt[:, 1:2], in0=v[:, V_SUB:V_SUB + 1], scalar1=1.0 - li, scalar2=None, op0=ALU.mult),
                reads=[("vec", l)], writes=[("lam", l, 1)])
        S.barrier()

    def norm_transpose(xt, xk, a_fm, b_fm, abk, hT3, hkeys, st, R):
        import os
        NS = int(os.environ.get("KNS", "99"))
        junk, jk = R["junk"].next()
        ssq, sk = R["ss"].next()
        dve(lambda e: e.memset(ssq, 0.0), reads=[], writes=[sk])
        act(junk, xt, AF.Square, reads=[xk, sk], writes=[jk, sk], accum_out=ssq[:, 0:1])
        if NS < 2: return
        rstd_from(ssq[:, 0:1], D, ssq[:, 2:3], [sk], sk, ssq[:, 1:2], sk)
        if NS < 3: return
        xn, xnk = R["xn"].next()
        dve(lambda e: e.tensor_scalar(out=xn, in0=xt, scalar1=ssq[:, 2:3], scalar2=None, op0=ALU.mult),
            reads=[xk, sk], writes=[xnk])
        if NS < 4: return
        b0 = R["tb"][R["tbi"][0] % 2]
        R["tbi"][0] += 1
        pT = psflat[:, b0 * 512:(b0 + 2) * 512].bitcast(BF16).rearrange("p (k t) -> p k t", k=KC)
        for k in range(KC):
            S.add("pe", lambda e, k=k: e.transpose(pT[:, k, :], xn[:, k * 128:(k + 1) * 128], ident),
                  reads=[xnk, KCONST], writes=[PSK(b0 + k // 8)])
        if NS < 5: return
        for k in range(KC):
            o = hT3[:, k, st * 128:(st + 1) * 128]
            EV = os.environ.get("KEV", "mix")
            if (k % 2 == 0 and EV == "mix") or EV == "act":
                act(o, pT[:, k, :], AF.Identity, reads=[PSK(b0 + k // 8), abk], writes=[hkeys[st][0]],
                    scale=a_fm[:, k:k + 1], bias=b_fm[:, k:k + 1])
            else:
                dve(lambda e, o=o, k=k: e.tensor_scalar(out=o, in0=pT[:, k, :], scalar1=a_fm[:, k:k + 1], scalar2=b_fm[:, k:k + 1],
                                                         op0=ALU.mult, op1=ALU.add),
                    reads=[PSK(b0 + k // 8), abk], writes=[hkeys[st][1]])

    def load_mod_fm(l, s, idx_shift, idx_scale, gcol, a_fm, b_fm, abk):
        mf3 = modfm[l].rearrange("p (c s) -> p c s", s=nseq)
        v = vecs[l]
        dve(lambda e: e.scalar_tensor_tensor(out=a_fm, in0=mf3[:, idx_scale * 16:(idx_scale + 1) * 16, s], scalar=1.0,
                                             in1=v[:, gcol:gcol + KC], op0=ALU.add, op1=ALU.mult),
            reads=[("vec", l)], writes=[abk])
        dve(lambda e: e.tensor_copy(out=b_fm, in_=mf3[:, idx_shift * 16:(idx_shift + 1) * 16, s]), reads=[], writes=[abk])

    def gate_bcast(l, s, idx, gbc, kg, tmp, tmpk):
        dma_ld(tmp[0:nseq, :], mod_d[l, :, idx * D:(idx + 1) * D], reads=[], writes=[tmpk])
        for cbk in range(4):
            b = next_bank()
            pe_mm(ps[:, b, :], sel[0:nseq, s * 128:(s + 1) * 128], tmp[0:nseq, cbk * 512:(cbk + 1) * 512], True, True,
                  reads=[tmpk, ("sel", 0)], writes=[PSK(b)])
            dve(lambda e, b=b, cbk=cbk: e.tensor_copy(out=gbc[:, cbk * 512:(cbk + 1) * 512], in_=ps[:, b, :]), reads=[PSK(b)], writes=[kg])

    QK_MAP = {}
    for j in range(48):
        if j < 4: QK_MAP[j] = ("A", V_QNA, j)
        elif j < 8: QK_MAP[j] = ("A", V_KNA, j)
        elif j < 12: QK_MAP[j] = None
        elif j < 18: QK_MAP[j] = ("B", V_QNB, 8 + j - 12)
        elif j < 24: QK_MAP[j] = ("B", V_KNB, 14 + j - 18)
        elif j < 30: QK_MAP[j] = None
        elif j < 36: QK_MAP[j] = ("C", V_QNC, 20 + j - 30)
        elif j < 42: QK_MAP[j] = ("C", V_KNC, 26 + j - 36)
        else: QK_MAP[j] = None
    V_SEGS = {2: [(0, 512, va_d, 0)], 6: [(0, 512, vb_d, 0)], 7: [(0, 256, vb_d, 512)],
              10: [(256, 256, vc_d, 0)], 11: [(0, 512, vc_d, 256)]}

    def phase_A(l, s, row0, Sq, xsrc):
        B.off = persist_mark
        if Sq < SMAX:
            for j in range(14, 20):
                dma_st(qk_d[j, :, PADK + Sq:PADK + Sq + PADK], zt, reads=[("zt", 0)], writes=[("padk2", j)])
            for r0 in range(0, PADK, 128):
                dma_st(vb_d[PADK + Sq + r0:PADK + Sq + r0 + 128, :], zt[:, 0:768], reads=[("zt", 0)], writes=[("padv2", r0)])
        a_fm = B.f32(KC); b_fm = B.f32(KC); abk = B.key("ab1")
        load_mod_fm(l, s, 0, 1, V_N1G, a_fm, b_fm, abk)
        R = {"junk": Ring("junk", 1, D, BF16), "ss": Ring("ss", 3, 4, F32), "xn": Ring("xn", 2, D, BF16),
             "tb": (4, 6), "tbi": [0]}
        xr = Ring("xt", 2, D, F32)
        hr = [(B.bf(KC * T), [[B.key("hT"), B.key("hT")] for _ in range(4)]) for _ in range(2)]
        wr = Ring("wbig", 3, KC * 512 // 2, F32)
        rope = Ring("rope", 2, 4 * T, F32)
        sqr = Ring("sq", 2, T, BF16)
        xsr = Ring("xs", 2, T, BF16)
        rsr = Ring("rstd", 2, 2 * T, F32)
        t1r = Ring("t1", 2, T, F32)
        t2r = Ring("t2", 2, T, F32)
        qor = Ring("qo", 3, T, BF16)
        vor = Ring("vo", 3, T, BF16)
        v = vecs[l]
        import os
        SUB = int(os.environ.get("KSUB", "99"))
        for tt in range(Sq // T if SUB >= 5 else (1 if SUB >= 2 else 0)):
            t0 = tt * T
            hT, hkeys = hr[tt % 2]
            hT3 = hT.rearrange("p (k t) -> p k t", k=KC)
            hall = [k for st in range(4) for k in hkeys[st]]
            for st in range(4):
                xt, xk = xr.next()
                dma_ld(xt, xsrc[row0 + t0 + st * 128:row0 + t0 + (st + 1) * 128, :], reads=[], writes=[xk])
                norm_transpose(xt, xk, a_fm, b_fm, abk, hT3, hkeys, st, R)
            if SUB < 3:
                continue
            dma_st(hT_d.rearrange("k p s -> p k s")[:, :, t0:t0 + T], hT3, reads=hall, writes=[("hTd", tt)])
            rp, rpk = rope.next()
            rp4 = rp.rearrange("p (f t) -> p f t", f=4)
            dma_ld(rp4[:, 0:2, :], ropeB.rearrange("f p s -> p f s")[:, :, t0:t0 + T], reads=[], writes=[rpk])
            dma_ld(rp4[:, 2:4, :], ropeC.rearrange("f p s -> p f s")[:, :, t0:t0 + T], reads=[], writes=[rpk])
            for blk in range(12 if SUB >= 4 else 0):
                wt, wk = wr.next()
                wtb = wt.bitcast(BF16).rearrange("p (k n) -> p k n", k=KC)
                c0 = blk * 512
                dma_ld(wtb, win_d[l].rearrange("(k p) n -> p k n", p=128)[:, :, c0:c0 + 512],
                       reads=wkeys(l, "win", D), writes=[wk])
                for cc in range(4):
                    j = blk * 4 + cc
                    m = QK_MAP[j]
                    if m is None:
                        continue
                    typ, gcol, dj = m
                    b = next_bank(0, 4)
                    for k in range(KC):
                        pe_mm(ps[:, b, :], wtb[:, k, cc * 128:(cc + 1) * 128], hT3[:, k, :], k == 0, k == KC - 1,
                              reads=[wk] + hall, writes=[PSK(b)])
                    zp = ps[:, b, :]
                    g = v[:, gcol:gcol + 1]
                    sq, sqk = sqr.next()
                    act(sq, zp, AF.Square, reads=[PSK(b)], writes=[sqk])
                    b2 = next_bank(0, 4)
                    onesm = ones64 if typ == "C" else ones128
                    nn = 64 if typ == "C" else 128
                    pe_mm(ps[:, b2, :], onesm, sq, True, True, reads=[sqk, KCONST], writes=[PSK(b2)])
                    rs, rsk = rsr.next()
                    rstd_from(ps[:, b2, :], nn, rs[:, T:2 * T], [PSK(b2)], rsk, rs[:, 0:T], rsk)
                    rstd = rs[:, T:2 * T]
                    qo, qok = qor.next()
                    if typ == "A":
                        dve(lambda e, qo=qo, zp=zp, g=g, rstd=rstd: e.scalar_tensor_tensor(
                            out=qo, in0=zp, scalar=g, in1=rstd, op0=ALU.mult, op1=ALU.mult),
                            reads=[PSK(b), rsk, ("vec", l)], writes=[qok])
                    else:
                        xs, xsk = xsr.next()
                        act(xs, zp, AF.Identity, reads=[PSK(b), ("vec", l)], writes=[xsk], scale=g)
                        b3 = next_bank(0, 4)
                        pm = permB if typ == "B" else permC
                        pe_mm(ps[:, b3, :], pm, xs, True, True, reads=[xsk, KCONST], writes=[PSK(b3)])
                        cosT = rp4[:, 0, :] if typ == "B" else rp4[:, 2, :]
                        sinT = rp4[:, 1, :] if typ == "B" else rp4[:, 3, :]
                        t1, t1k = t1r.next()
                        t2, t2k = t2r.next()
                        dve(lambda e, t1=t1, zp=zp, g=g, cosT=cosT: e.scalar_tensor_tensor(
                            out=t1, in0=zp, scalar=g, in1=cosT, op0=ALU.mult, op1=ALU.mult),
                            reads=[PSK(b), rpk, ("vec", l)], writes=[t1k])
                        dve(lambda e, t2=t2, b3=b3, sinT=sinT: e.tensor_tensor(out=t2, in0=ps[:, b3, :], in1=sinT, op=ALU.mult),
                            reads=[PSK(b3), rpk], writes=[t2k])
                        dve(lambda e, t1=t1, t2=t2: e.tensor_tensor(out=t1, in0=t1, in1=t2, op=ALU.add),
                            reads=[t1k, t2k], writes=[t1k])
                        dve(lambda e, qo=qo, t1=t1, rstd=rstd: e.tensor_tensor(out=qo, in0=t1, in1=rstd, op=ALU.mult),
                            reads=[t1k, rsk], writes=[qok])
                    dma_st(qk_d[dj, :, PADK + t0:PADK + t0 + T], qo, reads=[qok], writes=[("qkd", dj, tt)])
                for (cin, ncol, dst, dcol) in V_SEGS.get(blk, []):
                    for st in range(4):
                        b = next_bank(0, 4)
                        for k in range(KC):
                            pe_mm(ps[:, b, 0:ncol], hT3[:, k, st * 128:(st + 1) * 128], wtb[:, k, cin:cin + ncol], k == 0, k == KC - 1,
                                  reads=[wk] + hkeys[st], writes=[PSK(b)])
                        vo, vok = vor.next()
                        act(vo[:, 0:ncol], ps[:, b, 0:ncol], AF.Identity, reads=[PSK(b)], writes=[vok])
                        roff = PADK if dst is vb_d else 0
                        dma_st(dst[roff + t0 + st * 128:roff + t0 + (st + 1) * 128, dcol:dcol + ncol], vo[:, 0:ncol],
                               reads=[vok], writes=[("vd", blk, dcol, tt, st)])
        S.barrier()

    def phase_B(l, Sq):
        B.off = persist_mark
        v = vecs[l]
        lt = lamt[l]
        sc128 = 128.0 ** -0.5
        sc64 = 64.0 ** -0.5
        mark = B.off
        R_rows = Sq // 64
        plan = na_plan(R_rows)
        QT = B.bf(Sq); KT = B.bf(Sq); Vt = B.bf(Sq); OT = B.bf(Sq)
        tab = B.f32(NT * 128)
        kQ, kK, kV, kO, kT = (B.key(n) for n in ("aQ", "aK", "aV", "aO", "aT"))
        Vt3 = Vt.rearrange("p (j c) -> p j c", c=128)
        sbr = Ring("asb", 2, 640, F32)
        ptr = Ring("apt", 2, 640, BF16)
        rzr = Ring("arz", 2, 512, F32)
        for h in range(4):
            dma_ld(QT, qk_d[h, :, PADK:PADK + Sq], reads=[], writes=[kQ])
            dma_ld(KT, qk_d[4 + h, :, PADK:PADK + Sq], reads=[], writes=[kK])
            dma_ld(Vt3, va_d[0:Sq, h * 128:(h + 1) * 128].rearrange("(j p) c -> p j c", p=128), reads=[], writes=[kV])
            dma_ld(tab, rpbx[l, h], reads=[], writes=[kT])
            npairs = R_rows // 2
            for i0 in range(0, npairs, 4):
                bO = 4 + (i0 // 4) % 2
                bZ = 6 + (i0 // 4) % 2
                for i in range(i0, i0 + 4):
                    js, sig = plan[i]
                    off, n = cases[sig]
                    bS = 2 * ((i) % 2)
                    for jj, j in enumerate(js):
                        pe_mm(ps[:, bS + jj // 4, (jj % 4) * 128:(jj % 4 + 1) * 128], KT[:, j * 128:(j + 1) * 128], QT[:, i * 128:(i + 1) * 128],
                              True, True, reads=[kK, kQ], writes=[PSK(bS + jj // 4)])
                    sb, sbk = sbr.next()
                    pt, ptk = ptr.next()
                    dve(lambda e, sb=sb, bS=bS, n=n, off=off: e.scalar_tensor_tensor(
                        out=sb[:, 0:n * 128], in0=psflat[:, bS * 512:bS * 512 + n * 128], scalar=sc128,
                        in1=tab[:, off * 128:(off + n) * 128], op0=ALU.mult, op1=ALU.add),
                        reads=[PSK(bS), PSK(bS + 1), kT], writes=[sbk])
                    act(pt[:, 0:n * 128], sb[:, 0:n * 128], AF.Exp, reads=[sbk], writes=[ptk])
                    sl = i - i0
                    for jj, j in enumerate(js):
                        pe_mm(ps[:, bO, sl * 128:(sl + 1) * 128], Vt3[:, j, :], pt[:, jj * 128:(jj + 1) * 128], jj == 0, jj == n - 1,
                              reads=[kV, ptk], writes=[PSK(bO)])
                    for jj, j in enumerate(js):
                        pe_mm(ps[:, bZ, sl * 128:(sl + 1) * 128], ones128, pt[:, jj * 128:(jj + 1) * 128], jj == 0, jj == n - 1,
                              reads=[KCONST, ptk], writes=[PSK(bZ)])
                rz, rzk = rzr.next()
                dve(lambda e, rz=rz, bZ=bZ: e.reciprocal(out=rz, in_=ps[:, bZ, :]), reads=[PSK(bZ)], writes=[rzk])
                dve(lambda e, rz=rz, bO=bO, i0=i0, OT=OT: e.tensor_tensor(out=OT[:, i0 * 128:(i0 + 4) * 128], in0=ps[:, bO, :], in1=rz, op=ALU.mult),
                    reads=[PSK(bO), rzk], writes=[kO])
            dma_st(oT_d[h, :, 0:Sq], OT, reads=[kO], writes=[("oTd", h)])
        S.barrier()
        B.off = mark
        QT = B.bf(Sq); KT = B.bf(Sq + 2 * PADK); OT = B.bf(Sq)
        QTr = B.bf(Sq); KTr = B.bf(Sq + 2 * PADK)
        accO = B.f32(Sq); accZ = B.f32(Sq)
        kQ, kK, kO, kaO, kaZ = (B.key(n) for n in ("bQ", "bK", "bO", "baO", "baZ"))
        kQr, kKr = B.key("bQr"), B.key("bKr")
        vtr = Ring("bvt", 2, 33 * 128, BF16)
        er = Ring("be", 2, 256, BF16)
        ptr = Ring("bpt", 2, 256, BF16)
        for hg in range(2):
            for g, d in enumerate((1, 4, 16)):
                cj = g * 2 + hg
                dma_ld(QT, qk_d[8 + cj, :, PADK:PADK + Sq], reads=[], writes=[kQ])
                dma_ld(KT, qk_d[14 + cj, :, 0:Sq + 2 * PADK], reads=[], writes=[kK])
                L = Sq // d
                nq = L // 128
                if d == 1:
                    Qv = QT.rearrange("p (d m) -> p d m", d=1)
                    Kv = KT[:, PADK - 64:PADK - 64 + L + 128].rearrange("p (d m) -> p d m", d=1)
                    kQu, kKu = kQ, kK
                else:
                    Qv = QTr.rearrange("p (d m) -> p d m", d=d)
                    Kv = KTr[:, 0:(L + 128) * d].rearrange("p (d m) -> p d m", d=d)
                    dve(lambda e, Qv=Qv, QT=QT, d=d: e.tensor_copy(out=Qv, in_=QT.rearrange("p (m d) -> p d m", d=d)),
                        reads=[kQ], writes=[kQr])
                    act(Kv, KT[:, PADK - 64 * d:PADK - 64 * d + (L + 128) * d].rearrange("p (m d) -> p d m", d=d), AF.Identity,
                        reads=[kK], writes=[kKr])
                    kQu, kKu = kQr, kKr
                for r in range(d):
                    vt, vtk = vtr.next()
                    vt3 = vt[:, 0:(nq + 1) * 128].rearrange("p (j c) -> p j c", c=128)
                    base = PADK - 64 * d + r
                    nrow = (nq + 1) * 128
                    vsrc = vb_d[base:base + (nrow - 1) * d + 1:d, cj * 128:(cj + 1) * 128].rearrange("(j p) c -> p j c", p=128)
                    dma_ld(vt3, vsrc, reads=[], writes=[vtk])
                    for i0 in range(0, nq, 4):
                        w = min(4, nq - i0)
                        bO = 4 + (i0 // 4) % 2
                        bZ = 6 + (i0 // 4) % 2
                        for i in range(i0, i0 + w):
                            bS = next_bank(0, 4)
                            rhs = Qv[:, r, 128 * i:128 * (i + 1)]
                            for t in range(2):
                                pe_mm(ps[:, bS, t * 128:(t + 1) * 128], Kv[:, r, 128 * (i + t):128 * (i + t + 1)], rhs, True, True,
                                      reads=[kKu, kQu], writes=[PSK(bS)])
                            ee, ek = er.next()
                            pt, ptk = ptr.next()
                            act(ee, ps[:, bS, 0:256], AF.Exp, reads=[PSK(bS)], writes=[ek], scale=sc128)
                            dve(lambda e, pt=pt, ee=ee: e.tensor_tensor(out=pt, in0=ee, in1=band, op=ALU.mult),
                                reads=[ek, KCONST], writes=[ptk])
                            sl = i - i0
                            for t in range(2):
                                pe_mm(ps[:, bO, sl * 128:(sl + 1) * 128], vt3[:, i + t, :], pt[:, t * 128:(t + 1) * 128], t == 0, t == 1,
                                      reads=[vtk, ptk], writes=[PSK(bO)])
                            for t in range(2):
                                jp = i + t
                                om = ones_hi if jp == 0 else (ones_lo if jp == nq else ones128)
                                pe_mm(ps[:, bZ, sl * 128:(sl + 1) * 128], om, pt[:, t * 128:(t + 1) * 128], t == 0, t == 1,
                                      reads=[KCONST, ptk], writes=[PSK(bZ)])
                        p0 = r + 128 * i0 * d
                        dO = accO[:, p0:p0 + (w * 128 - 1) * d + 1:d]
                        dZ = accZ[:, p0:p0 + (w * 128 - 1) * d + 1:d]
                        if g == 0:
                            act(dO, ps[:, bO, 0:w * 128], AF.Identity, reads=[PSK(bO)], writes=[kaO])
                            act(dZ, ps[:, bZ, 0:w * 128], AF.Identity, reads=[PSK(bZ)], writes=[kaZ])
                        else:
                            dve(lambda e, dO=dO, bO=bO, w=w: e.tensor_tensor(out=dO, in0=dO, in1=ps[:, bO, 0:w * 128], op=ALU.add),
                                reads=[PSK(bO), kaO], writes=[kaO])
                            dve(lambda e, dZ=dZ, bZ=bZ, w=w: e.tensor_tensor(out=dZ, in0=dZ, in1=ps[:, bZ, 0:w * 128], op=ALU.add),
                                reads=[PSK(bZ), kaZ], writes=[kaZ])
            dve(lambda e, accZ=accZ: e.reciprocal(out=accZ, in_=accZ), reads=[kaZ], writes=[kaZ])
            dve(lambda e, OT=OT, accO=accO, accZ=accZ: e.tensor_tensor(out=OT, in0=accO, in1=accZ, op=ALU.mult), reads=[kaO, kaZ], writes=[kO])
            dma_st(oT_d[4 + hg, :, 0:Sq], OT, reads=[kO], writes=[("oTd", 4 + hg)])
        S.barrier()
        B.off = mark
        Qx = B.bf(2 * Sq); KT = B.bf(Sq); Vt = B.bf(Sq); OT = B.bf(Sq)
        Qx3 = Qx.rearrange("p (c s) -> p c s", c=2)
        Vt3 = Vt.rearrange("p (j c) -> p j c", c=128)
        kQ, kK, kV, kO = (B.key(n) for n in ("cQ", "cK", "cV", "cO"))
        ptr = Ring("cpt", 3, 512, BF16)
        rzr = Ring("crz", 2, 512, F32)
        tr = Ring("ct", 2, 512, F32)
        ar = Ring("ca", 2, 256, F32)
        sqr = Ring("csq", 2, 256, BF16)
        rsr = Ring("crs", 2, 512, F32)
        dve(lambda e, Qx=Qx: e.memset(Qx, 0.0), reads=[], writes=[kQ])
        for hc in range(6):
            dma_ld(Qx3[0:64, 0, :], qk_d[20 + hc, 0:64, PADK:PADK + Sq], reads=[], writes=[kQ])
            dma_ld(Qx3[64:128, 1, :], qk_d[20 + hc, 64:128, PADK:PADK + Sq], reads=[], writes=[kQ])
            dma_ld(KT, qk_d[26 + hc, :, PADK:PADK + Sq], reads=[], writes=[kK])
            dma_ld(Vt3, vc_d[0:Sq, hc * 128:(hc + 1) * 128].rearrange("(j p) c -> p j c", p=128), reads=[], writes=[kV])
            nkt = Sq // 128
            for qb in range(Sq // 256):
                bO = 3 + qb % 2
                bZ = 5 + qb % 2
                for kt in range(nkt):
                    bS = next_bank(0, 3)
                    pe_mm(ps[:, bS, :].rearrange("p (c q) -> p c q", c=2), KT[:, kt * 128:(kt + 1) * 128], Qx3[:, :, qb * 256:(qb + 1) * 256],
                          True, True, reads=[kK, kQ], writes=[PSK(bS)])
                    pt, ptk = ptr.next()
                    act(pt, ps[:, bS, :], AF.Exp, reads=[PSK(bS)], writes=[ptk], scale=sc64)
                    pe_mm(ps[:, bO, :], Vt3[:, kt, :], pt, kt == 0, kt == nkt - 1, reads=[kV, ptk], writes=[PSK(bO)])
                    pe_mm(ps[:, bZ, :], ones128, pt, kt == 0, kt == nkt - 1, reads=[KCONST, ptk], writes=[PSK(bZ)])
                rz, rzk = rzr.next()
                tt_, ttk = tr.next()
                aa, aak = ar.next()
                sq, sqk = sqr.next()
                rs, rsk = rsr.next()
                dve(lambda e, rz=rz, bZ=bZ: e.reciprocal(out=rz, in_=ps[:, bZ, :]), reads=[PSK(bZ)], writes=[rzk])
                dve(lambda e, tt_=tt_, rz=rz, bO=bO: e.tensor_tensor(out=tt_, in0=ps[:, bO, :], in1=rz, op=ALU.mult),
                    reads=[PSK(bO), rzk], writes=[ttk])
                dve(lambda e, aa=aa, tt_=tt_: e.scalar_tensor_tensor(out=aa, in0=tt_[:, 256:512], scalar=lt[:, 0:1], in1=tt_[:, 0:256],
                                                                      op0=ALU.mult, op1=ALU.add),
                    reads=[ttk, ("lam", l, 0)], writes=[aak])
                act(sq, aa, AF.Square, reads=[aak], writes=[sqk])
                pe_mm(ps[:, 7, 0:256], ones128, sq, True, True, reads=[KCONST, sqk], writes=[PSK(7)])
                rstd_from(ps[:, 7, 0:256], 128, rs[:, 256:512], [PSK(7)], rsk, rs[:, 0:256], rsk)
                dve(lambda e, aa=aa, rs=rs, qb=qb, OT=OT: e.scalar_tensor_tensor(out=OT[:, qb * 256:(qb + 1) * 256], in0=aa, scalar=lt[:, 1:2],
                                                                           in1=rs[:, 256:512], op0=ALU.mult, op1=ALU.mult),
                    reads=[aak, rsk, ("lam", l, 1)], writes=[kO])
            dma_st(oT_d[6 + hc, :, 0:Sq], OT, reads=[kO], writes=[("oTd", 6 + hc)])
        S.barrier()

    def phase_C1(l, s, row0, Sq, xsrc):
        B.off = persist_mark
        v = vecs[l]
        g1bc = B.f32(D); kg1 = B.key("g1bc")
        GATE1 = True
        OTt = B.bf(12 * T); kOT = B.key("OTt")
        OT3 = OTt.rearrange("p (c t) -> p c t", c=12)
        hT = B.bf(KC * T); khT = B.key("hTt")
        hT3 = hT.rearrange("p (k t) -> p k t", k=KC)
        mT = B.bf(KC * T)
        mT3 = mT.rearrange("p (k t) -> p k t", k=KC)
        kmT = [B.key("mT") for _ in range(KC)]
        xts = [(B.f32(D), B.key("xt")) for _ in range(4)]
        gate_bcast(l, s, 2, g1bc, kg1, xts[0][0], xts[0][1])
        wsr = Ring("wsm", 3, (48 + 12) * 128 // 2, F32)
        wbr = Ring("wbig", 2, KC * 512 // 2, F32)
        sgr = Ring("sg", 4, T, F32)
        mr = Ring("m", 3, T, F32)
        tr = Ring("tmp", 2, T, F32)
        for tt in range(Sq // T):
            t0 = tt * T
            dma_ld(OT3, oT_d.rearrange("c p s -> p c s")[:, :, t0:t0 + T], reads=[], writes=[kOT])
            dma_ld(hT3, hT_d.rearrange("k p s -> p k s")[:, :, t0:t0 + T], reads=[], writes=[khT])
            for st in range(4):
                dma_ld(xts[st][0], xsrc[row0 + t0 + st * 128:row0 + t0 + (st + 1) * 128, :], reads=[], writes=[xts[st][1]])
            for dc in range(KC):
                wt, wk = wsr.next()
                wtb = wt.bitcast(BF16)
                wg = wtb[:, 0:48 * 128].rearrange("p (g k n) -> p g k n", g=3, k=KC)
                wb = wtb[:, 48 * 128:60 * 128].rearrange("p (k n) -> p k n", k=12)
                dma_ld(wg, wgate_d[l].rearrange("(k p) (g n) -> p g k n", p=128, g=3)[:, :, :, dc * 128:(dc + 1) * 128],
                       reads=wkeys(l, "wgate", D), writes=[wk])
                dma_ld(wb, wbr_d[l].rearrange("(k p) n -> p k n", p=128)[:, :, dc * 128:(dc + 1) * 128],
                       reads=wkeys(l, "wbra", 512) + wkeys(l, "wbrb", 256) + wkeys(l, "wbrc", 768), writes=[wk])
                kcs = [(0, 4), (4, 6), (6, 12)]
                mcur = None
                for gi in range(3):
                    bg = next_bank()
                    for k in range(KC):
                        pe_mm(ps[:, bg, :], wg[:, gi, k, :], hT3[:, k, :], k == 0, k == KC - 1, reads=[wk, khT], writes=[PSK(bg)])
                    by = next_bank()
                    k0, k1 = kcs[gi]
                    for k in range(k0, k1):
                        pe_mm(ps[:, by, :], wb[:, k, :], OT3[:, k, :], k == k0, k == k1 - 1, reads=[wk, kOT], writes=[PSK(by)])
                    sg, sgk = sgr.next()
                    act(sg, ps[:, bg, :], AF.Sigmoid, reads=[PSK(bg), ("vec", l)], writes=[sgk],
                        bias=v[:, V_BG + gi * 16 + dc:V_BG + gi * 16 + dc + 1])
                    mm_, mk = mr.next()
                    dve(lambda e, mm_=mm_, sg=sg, by=by: e.tensor_tensor(out=mm_, in0=ps[:, by, :], in1=sg, op=ALU.mult),
                        reads=[PSK(by), sgk], writes=[mk])
                    if gi == 0:
                        mcur = (mm_, mk)
                    elif gi == 1:
                        dve(lambda e, a=mcur[0], b_=mm_: e.tensor_tensor(out=a, in0=a, in1=b_, op=ALU.add),
                            reads=[mcur[1], mk], writes=[mcur[1]])
                    else:
                        dve(lambda e, a=mcur[0], b_=mm_, dc=dc: e.tensor_tensor(out=mT3[:, dc, :], in0=a, in1=b_, op=ALU.add),
                            reads=[mcur[1], mk], writes=[kmT[dc]])
            for cbk in range(4):
                wt, wk = wbr.next()
                wtb = wt.bitcast(BF16).rearrange("p (k n) -> p k n", k=KC)
                dma_ld(wtb, wout_d[l].rearrange("(k p) n -> p k n", p=128)[:, :, cbk * 512:(cbk + 1) * 512],
                       reads=wkeys(l, "wout", D), writes=[wk])
                for st in range(4):
                    b = next_bank()
                    for k in range(KC):
                        pe_mm(ps[:, b, :], mT3[:, k, st * 128:(st + 1) * 128], wtb[:, k, :], k == 0, k == KC - 1,
                              reads=[wk, kmT[k]], writes=[PSK(b)])
                    tm, tk = tr.next()
                    xt, xk = xts[st]
                    dve(lambda e, tm=tm, b=b, cbk=cbk: e.tensor_tensor(out=tm, in0=ps[:, b, :], in1=g1bc[:, cbk * 512:(cbk + 1) * 512], op=ALU.mult),
                        reads=[PSK(b), kg1], writes=[tk])
                    dve(lambda e, tm=tm, xt=xt, cbk=cbk: e.tensor_tensor(out=xt[:, cbk * 512:(cbk + 1) * 512], in0=xt[:, cbk * 512:(cbk + 1) * 512],
                                                                          in1=tm, op=ALU.add),
                        reads=[tk, xk], writes=[xk])
            for st in range(4):
                dma_st(xmid_d[t0 + st * 128:t0 + (st + 1) * 128, :], xts[st][0], reads=[xts[st][1]], writes=[("xmid", tt, st)])
        S.barrier()

    def phase_C2(l, s, row0, Sq, xdst):
        B.off = persist_mark
        a_fm = B.f32(KC); b_fm = B.f32(KC); abk = B.key("ab2")
        load_mod_fm(l, s, 3, 4, V_N2G, a_fm, b_fm, abk)
        g2bc = B.f32(D); kg2 = B.key("g2bc")
        GATE2 = True
        R = {"junk": Ring("junk", 1, D, BF16), "ss": Ring("ss", 3, 4, F32), "xn": Ring("xn", 2, D, BF16),
             "tb": (0, 2), "tbi": [0]}
        xts = [(B.f32(D), B.key("xt")) for _ in range(4)]
        gate_bcast(l, s, 5, g2bc, kg2, xts[0][0], xts[0][1])
        hT = B.bf(KC * T)
        hT3 = hT.rearrange("p (k t) -> p k t", k=KC)
        hkeys = [[B.key("h2T"), B.key("h2T")] for _ in range(4)]
        hall = [k for st in range(4) for k in hkeys[st]]
        f1 = B.bf(64 * T)
        f13 = f1.rearrange("p (c t) -> p c t", c=64)
        kf1 = [B.key("f1") for _ in range(64)]
        wbr = Ring("wbig", 3, KC * 512 // 2, F32)
        rr = Ring("relu", 3, T, BF16)
        tr = Ring("tmp", 2, T, F32)
        for tt in range(Sq // T):
            t0 = tt * T
            for st in range(4):
                xt, xk = xts[st]
                dma_ld(xt, xmid_d[t0 + st * 128:t0 + (st + 1) * 128, :], reads=[], writes=[xk])
                norm_transpose(xt, xk, a_fm, b_fm, abk, hT3, hkeys, st, R)
            for blk in range(16):
                wt, wk = wbr.next()
                wtb = wt.bitcast(BF16).rearrange("p (k n) -> p k n", k=KC)
                dma_ld(wtb, wff1_d[l].rearrange("(k p) n -> p k n", p=128)[:, :, blk * 512:(blk + 1) * 512],
                       reads=wkeys(l, "wff1", D), writes=[wk])
                for cc in range(4):
                    c = blk * 4 + cc
                    b = next_bank(0, 4)
                    for k in range(KC):
                        pe_mm(ps[:, b, :], wtb[:, k, cc * 128:(cc + 1) * 128], hT3[:, k, :], k == 0, k == KC - 1,
                              reads=[wk] + hall, writes=[PSK(b)])
                    rl, rk = rr.next()
                    act(rl, ps[:, b, :], AF.Relu, reads=[PSK(b)], writes=[rk])
                    dve(lambda e, rl=rl, c=c: e.tensor_tensor(out=f13[:, c, :], in0=rl, in1=rl, op=ALU.mult),
                        reads=[rk], writes=[kf1[c]])
            for cbk in range(4):
                for kg in range(4):
                    wt, wk = wbr.next()
                    wtb = wt.bitcast(BF16).rearrange("p (k n) -> p k n", k=KC)
                    dma_ld(wtb, wff2_d[l, kg * 2048:(kg + 1) * 2048, :].rearrange("(k p) n -> p k n", p=128)[:, :, cbk * 512:(cbk + 1) * 512],
                           reads=wkeys(l, "wff2", DFF), writes=[wk])
                    for st in range(4):
                        b = 4 + st
                        for k in range(KC):
                            c = kg * 16 + k
                            pe_mm(ps[:, b, :], f13[:, c, st * 128:(st + 1) * 128], wtb[:, k, :], c == 0, c == 63,
                                  reads=[wk, kf1[c]], writes=[PSK(b)])
                for st in range(4):
                    b = 4 + st
                    tm, tk = tr.next()
                    xt, xk = xts[st]
                    dve(lambda e, tm=tm, b=b, cbk=cbk: e.tensor_tensor(out=tm, in0=ps[:, b, :], in1=g2bc[:, cbk * 512:(cbk + 1) * 512], op=ALU.mult),
                        reads=[PSK(b), kg2], writes=[tk])
                    dve(lambda e, tm=tm, xt=xt, cbk=cbk: e.tensor_tensor(out=xt[:, cbk * 512:(cbk + 1) * 512], in0=xt[:, cbk * 512:(cbk + 1) * 512],
                                                                          in1=tm, op=ALU.add),
                        reads=[tk, xk], writes=[xk])
            for st in range(4):
                dma_st(xdst[row0 + t0 + st * 128:row0 + t0 + (st + 1) * 128, :], xts[st][0], reads=[xts[st][1]], writes=[("xout", tt, st)])
        S.barrier()

    issue_casts(0, True)
    if stop >= 1:
        prologue()
    if stop >= 2:
        issue_casts(0, False)
    for l in range(n_layers if stop >= 2 else 0):
        xsrc = xin if l == 0 else x1_d
        xdst = yout if l == n_layers - 1 else x1_d
        row0 = 0
        for s, Sq in enumerate(seqs):
            phase_A(l, s, row0, Sq, xsrc)
            if l == 0 and s == 0 and n_layers > 1:
                issue_casts(1, True)
                issue_casts(1, False)
            if stop >= 3:
                phase_B(l, Sq)
            if stop >= 4:
                phase_C1(l, s, row0, Sq, xsrc)
            if stop >= 5:
                phase_C2(l, s, row0, Sq, xdst)
            row0 += Sq
    S.barrier(include_cast=True)
    S.finalize()

    keys = S.sem_keys()
    sems = {}
    for i, k in enumerate(keys):
        sems[k] = stack.enter_context(nc.semaphore("s%d" % i))
    block = stack.enter_context(nc.Block())

    @block.tensor
    def _(e):
        S.run("pe", e, sems)

    @block.scalar
    def _(e):
        S.run("act", e, sems)

    @block.vector
    def _(e):
        S.run("dve", e, sems)

    @block.gpsimd
    def _(e):
        S.run("pool", e, sems)

    @block.sync
    def _(e):
        S.run("sp", e, sems)

    stack.close()
    return nc


def host_consts(smax):
    bf = ml_dtypes.bfloat16
    cb = np.zeros((128, 8 * 128 + 256), np.float32)
    cb[:, 0:128] = np.eye(128)
    cb[:, 128:256] = 1.0
    cb[0:64, 256:320] = 1.0
    cb[64:128, 320:384] = 1.0
    cb[0:64, 384:512] = 1.0
    cb[64:128, 512:640] = 1.0
    pB = np.zeros((128, 128), np.float32)
    for d in range(16):
        pB[d + 16, d] = 1.0
        pB[d, d + 16] = 1.0
    cb[:, 640:768] = pB
    pC = np.zeros((128, 128), np.float32)
    for blk in range(2):
        for d in range(8):
            pC[blk * 64 + d + 8, blk * 64 + d] = 1.0
            pC[blk * 64 + d, blk * 64 + d + 8] = 1.0
    cb[:, 768:896] = pC
    p = np.arange(128)[:, None]
    c = np.arange(128)[None, :]
    cb[:, 1024:1152] = (p >= c)
    cb[:, 1152:1280] = (p <= c)
    pos = np.arange(smax, dtype=np.float32)

    def rope_tab(dh, nblk):
        rot = dh // 4
        half = rot // 2
        inv = (500000.0 ** (-np.arange(half, dtype=np.float32) / half)).astype(np.float32)
        ang = pos[None, :] * inv[:, None]
        cos, sin = np.cos(ang).astype(np.float32), np.sin(ang).astype(np.float32)
        t = np.zeros((2, 128, smax), np.float32)
        t[0] = 1.0
        for b in range(nblk):
            o = b * dh
            t[0, o:o + half] = cos
            t[0, o + half:o + rot] = cos
            t[1, o:o + half] = -sin
            t[1, o + half:o + rot] = sin
        return t

    return cb.astype(bf), rope_tab(128, 1), rope_tab(64, 2)


def host_inputs(inp, seq_sel, n_layers, smax):
    cbf, ropeB, ropeC = host_consts(smax)
    xs, cs = [], []
    for kind, idx in seq_sel:
        if kind == "p":
            xs.append(np.asarray(inp["x_prompt"][idx]))
            cs.append(np.asarray(inp["c_prompt"][idx]))
        else:
            xs.append(np.asarray(inp["x_sample"][idx]))
            cs.append(np.asarray(inp["c_sample"][idx]))
    xin = np.ascontiguousarray(np.concatenate(xs, axis=0), dtype=np.float32)
    cmat = np.stack(cs, 0)
    cT = np.ascontiguousarray(cmat.reshape(len(cs), KC, 128).transpose(2, 1, 0), dtype=np.float32)
    return xin, cT, cbf, ropeB, ropeC


def shared_inputs(inp, n_layers, nseq):
    L = n_layers
    f = lambda k: np.ascontiguousarray(np.asarray(inp[k])[:L], dtype=np.float32)
    vec = np.zeros((L, 128, NV), np.float32)
    for l in range(L):
        vec[l, :, V_N1G:V_N1G + 16] = np.asarray(inp["norm1_g"][l]).reshape(16, 128).T
        vec[l, :, V_N2G:V_N2G + 16] = np.asarray(inp["norm2_g"][l]).reshape(16, 128).T
        vec[l, :, V_BG:V_BG + 48] = np.asarray(inp["b_gate"][l]).reshape(48, 128).T
        vec[l, :, V_QNA] = np.asarray(inp["qn_a"][l])
        vec[l, :, V_KNA] = np.asarray(inp["kn_a"][l])
        vec[l, :, V_QNB] = np.asarray(inp["qn_b"][l])
        vec[l, :, V_KNB] = np.asarray(inp["kn_b"][l])
        vec[l, :, V_QNC] = np.tile(np.asarray(inp["qn_c"][l]), 2)
        vec[l, :, V_KNC] = np.tile(np.asarray(inp["kn_c"][l]), 2)
        vec[l, :, V_SUB] = np.asarray(inp["subln_c"][l])
        vec[l, :, V_BADA:V_BADA + 96] = np.asarray(inp["b_ada"][l]).reshape(96, 128).T
        for i, k in enumerate(("lam_q1", "lam_k1", "lam_q2", "lam_k2")):
            vec[l, :, V_LAM + 64 * i:V_LAM + 64 * (i + 1)] = np.asarray(inp[k][l])[None, :]
    idx = na_index_table()
    rpb = np.asarray(inp["rpb_a"])[:L].astype(np.float32)
    flat = rpb.reshape(L, 4, 15 * 31)
    rpbx = np.where(idx[None, None] >= 0, flat[:, :, np.clip(idx, 0, None)], np.float32(-1.0e4)).astype(np.float32)
    b_ada3 = np.ascontiguousarray(np.broadcast_to(np.asarray(inp["b_ada"])[:L, None, :], (L, nseq, 6 * D)), dtype=np.float32)
    sh = {"w_ada": f("w_ada"), "b_ada3": b_ada3, "w_in": f("w_in"), "w_gate": f("w_gate"), "w_br_a": f("w_br_a"),
          "w_br_b": f("w_br_b"), "w_br_c": f("w_br_c"), "w_out": f("w_out"), "w_ff1": f("w_ff1"), "w_ff2": f("w_ff2"),
          "vec_fm": vec, "rpbx": np.ascontiguousarray(rpbx),
          "zrow": np.zeros((128, 1024), ml_dtypes.bfloat16),
          "selc": np.ascontiguousarray(np.repeat(np.eye(4, dtype=np.float32), 128, axis=1))}
    return sh


_NC_CACHE = {}


def kernel(**inp):
    n_cores = 8
    seqs = (4096, 2048, 2048)
    key = ("full",)
    if key not in _NC_CACHE:
        _NC_CACHE[key] = build(seqs, 2)
    nc = _NC_CACHE[key]
    sh = shared_inputs(inp, 2, 3)
    in_maps = []
    for c in range(n_cores):
        xin, cT, cbf, ropeB, ropeC = host_inputs(inp, [("p", c), ("s", 2 * c), ("s", 2 * c + 1)], 2, 4096)
        m = dict(sh)
        m.update({"xin": xin, "cT": cT, "cbf": cbf, "ropeB": ropeB, "ropeC": ropeC})
        in_maps.append(m)
    res = run_bass_kernel_spmd(nc, in_maps, core_ids=list(range(n_cores)))
    yp = np.empty((8, 4096, D), np.float32)
    ys = np.empty((16, 2048, D), np.float32)
    for c in range(n_cores):
        y = res.results[c]["yout"]
        yp[c] = y[0:4096]
        ys[2 * c] = y[4096:6144]
        ys[2 * c + 1] = y[6144:8192]
    return (yp, ys)
```

```python
import itertools
import math
import numpy as np
import ml_dtypes
import concourse.bass as bass
import concourse.mybir as mybir
from concourse.bass_utils import run_bass_kernel_spmd

F32 = mybir.dt.float32
BF16 = mybir.dt.bfloat16
AF = mybir.ActivationFunctionType
ALU = mybir.AluOpType
AX = mybir.AxisListType

D = 2048
KC = 16
DFF = 8192
INC = 6144
T = 512
EPS = 1e-6
PADK = 1024
NSEM = {"cast": 40, "sp": 28, "pool": 28}
CSTEP = 512
NCASE_TILES = None

V_N1G, V_N2G, V_BG, V_QNA, V_KNA, V_QNB, V_KNB, V_QNC, V_KNC, V_SUB = 0, 16, 32, 80, 81, 82, 83, 84, 85, 86
V_LAM = 87
V_BADA = V_LAM + 256
NV = V_BADA + 96


class Op:
    __slots__ = ("eng", "fn", "deps", "signal", "semkey", "val", "dma")

    def __init__(self, eng, fn, dma):
        self.eng = eng
        self.fn = fn
        self.dma = dma
        self.deps = []
        self.signal = False
        self.semkey = None
        self.val = 0


class Sched:
    ENGS = ("pe", "act", "dve", "pool", "sp")

    def __init__(self):
        self.q = {e: [] for e in self.ENGS}
        self.lastw = {}
        self.readers = {}
        self.last_op = {}
        self.dma_rr = {}
        self.dma_last = {}
        self.uid = 0

    def add(self, eng, fn, reads=(), writes=(), dma=False, pool=None):
        o = Op(eng, fn, dma)
        deps = {}
        psr = [r for r in reads if r[0] == "ps"]
        if psr:
            reads = [r for r in reads if r[0] != "ps"]
            writes = list(writes) + psr
        for r in reads:
            w = self.lastw.get(r)
            if w is not None:
                deps[id(w)] = (w, True)
        for r in writes:
            w = self.lastw.get(r)
            if w is not None:
                deps[id(w)] = (w, True)
            rd = self.readers.get(r)
            if rd:
                for x in rd.values():
                    if id(x) not in deps:
                        deps[id(x)] = (x, False)
        for d, is_w in deps.values():
            if (not dma) and (not d.dma) and d.eng == eng:
                if eng == "pe" or not is_w:
                    continue
            o.deps.append(d)
            d.signal = True
        if dma:
            pool = pool or eng
            n = self.dma_rr.get(pool, 0)
            self.dma_rr[pool] = n + 1
            k = (pool, n % NSEM[pool])
            prev = self.dma_last.get(k)
            if prev is not None:
                o.deps.append(prev)
            self.dma_last[k] = o
            o.semkey = k
            o.val = (prev.val if prev is not None else 0) + 16
            o.signal = True
        self.uid += 1
        for r in reads:
            self.readers.setdefault(r, {})[("d", self.uid) if dma else eng] = o
        for r in writes:
            self.lastw[r] = o
            self.readers[r] = {}
        self.q[eng].append(o)
        if not dma:
            self.last_op[eng] = o
        return o

    def barrier(self, include_cast=False):
        lasts = list(self.last_op.values())
        for k, o in self.dma_last.items():
            if k[0] == "cast" and not include_cast:
                continue
            lasts.append(o)
        for o in lasts:
            o.signal = True
        for e in self.ENGS:
            b = Op(e, None, False)
            b.deps = [o for o in lasts if o.dma or o.eng != e]
            self.q[e].append(b)
        keepw = {k: v for k, v in self.lastw.items() if k[0] == "W"}
        self.lastw = keepw
        self.readers = {}
        self.last_op = {}

    def sem_keys(self):
        keys = ["pe", "act", "dve"]
        for pool, n in self.dma_rr.items():
            for i in range(min(n, NSEM[pool])):
                keys.append((pool, i))
        return keys

    def finalize(self):
        for e in ("pe", "act", "dve"):
            cnt = 0
            for o in self.q[e]:
                if o.dma or o.fn is None:
                    continue
                if o.signal:
                    cnt += 1
                    o.semkey = e
                    o.val = cnt

    def run(self, e, eng, sems):
        seen = {}
        for o in self.q[e]:
            for d in o.deps:
                if seen.get(d.semkey, 0) >= d.val:
                    continue
                eng.wait_ge(sems[d.semkey], d.val)
                seen[d.semkey] = d.val
            if o.fn is None:
                continue
            ins = o.fn(eng)
            if o.signal:
                ins.then_inc(sems[o.semkey], 16 if o.dma else 1)


def _rs(r, R):
    return min(max(r - 4, 0), R - 8)


def na_plan(R):
    plan = []
    for i in range(R // 2):
        lo = _rs(2 * i, R)
        hi = _rs(2 * i + 1, R) + 7
        js = list(range(lo // 2, hi // 2 + 1))
        sig = []
        for j in js:
            s = []
            for kr in range(2):
                for qr in range(2):
                    Rk, Rq = 2 * j + kr, 2 * i + qr
                    valid = _rs(Rq, R) <= Rk <= _rs(Rq, R) + 7
                    s.append((Rk - Rq, valid))
            sig.append(tuple(s))
        plan.append((js, tuple(sig)))
    return plan


def na_cases():
    cases = {}
    off = 0
    for R in (64, 32):
        for js, sig in na_plan(R):
            if sig not in cases:
                cases[sig] = (off, len(js))
                off += len(js)
    return cases, off


def na_index_table():
    cases, nt = na_cases()
    tab = -np.ones((128, nt * 128), dtype=np.int64)
    kc = np.arange(64)
    qc = np.arange(64)
    cs = np.clip(qc - 8, 0, 48)
    colvalid = (kc[:, None] >= cs[None, :]) & (kc[:, None] < cs[None, :] + 16)
    colidx = kc[:, None] - qc[None, :] + 15
    for sig, (off, n) in cases.items():
        for t in range(n):
            s = sig[t]
            for kr in range(2):
                for qr in range(2):
                    dr, valid = s[kr * 2 + qr]
                    if not valid:
                        continue
                    blk = np.where(colvalid, (dr + 7) * 31 + colidx, -1)
                    tab[kr * 64:(kr + 1) * 64, (off + t) * 128 + qr * 64:(off + t) * 128 + (qr + 1) * 64] = blk
    return tab


def build(seqs=(4096, 2048, 2048), n_layers=2, debug=False, stop=99):
    nseq = len(seqs)
    TOK = sum(seqs)
    SMAX = max(seqs)
    SP = SMAX + 2 * PADK
    cases, NT = na_cases()
    nc = bass.Bass("TRN2", target_bir_lowering=False)

    def din(name, shape, dt=F32):
        return nc.dram_tensor(name, list(shape), dt, kind="ExternalInput").ap()

    DBG = ("mod_d", "qk_d", "va_d", "vb_d", "vc_d", "hT_d", "oT_d", "xmid_d")

    def dint(name, shape, dt=F32):
        kind = "ExternalOutput" if (debug and name in DBG) else "Internal"
        return nc.dram_tensor(name, list(shape), dt, kind=kind).ap()

    xin = din("xin", [TOK, D])
    cT = din("cT", [128, KC, nseq])
    w_ada = din("w_ada", [n_layers, D, 6 * D])
    b_ada3 = din("b_ada3", [n_layers, nseq, 6 * D])
    w_in = din("w_in", [n_layers, D, INC])
    w_gate = din("w_gate", [n_layers, D, 3 * D])
    w_bra = din("w_br_a", [n_layers, 512, D])
    w_brb = din("w_br_b", [n_layers, 256, D])
    w_brc = din("w_br_c", [n_layers, 768, D])
    w_out = din("w_out", [n_layers, D, D])
    w_ff1 = din("w_ff1", [n_layers, D, DFF])
    w_ff2 = din("w_ff2", [n_layers, DFF, D])
    vec_fm = din("vec_fm", [n_layers, 128, NV])
    cbf = din("cbf", [128, 8 * 128 + 256], BF16)
    ropeB = din("ropeB", [2, 128, SMAX])
    ropeC = din("ropeC", [2, 128, SMAX])
    rpbx = din("rpbx", [n_layers, 4, 128, NT * 128])
    zrow = din("zrow", [128, 1024], BF16)
    selc = din("selc", [4, 4 * 128])
    yout = nc.dram_tensor("yout", [TOK, D], F32, kind="ExternalOutput").ap()

    mod_d = dint("mod_d", [n_layers, nseq, 6 * D])
    win_d = dint("win_d", [n_layers, D, INC], BF16)
    wgate_d = dint("wgate_d", [n_layers, D, 3 * D], BF16)
    wbr_d = dint("wbr_d", [n_layers, 1536, D], BF16)
    wout_d = dint("wout_d", [n_layers, D, D], BF16)
    wff1_d = dint("wff1_d", [n_layers, D, DFF], BF16)
    wff2_d = dint("wff2_d", [n_layers, DFF, D], BF16)
    qk_d = dint("qk_d", [32, 128, SP], BF16)
    va_d = dint("va_d", [SMAX, 512], BF16)
    vb_d = dint("vb_d", [SMAX + 2 * PADK, 768], BF16)
    vc_d = dint("vc_d", [SMAX, 768], BF16)
    hT_d = dint("hT_d", [KC, 128, SMAX], BF16)
    oT_d = dint("oT_d", [12, 128, SMAX], BF16)
    xmid_d = dint("xmid_d", [SMAX, D])
    x1_d = dint("x1_d", [TOK, D])

    S = Sched()
    ARENA = 52000

    ctx = {}

    def emit_all():
        pass

    import contextlib
    stack = contextlib.ExitStack()
    arena = stack.enter_context(nc.sbuf_tensor("arena", [128, ARENA], F32))
    ps = stack.enter_context(nc.psum_tensor("ps", [128, 8, 512], F32))
    psflat = ps.rearrange("p b n -> p (b n)")

    class Bump:
        def __init__(self):
            self.off = 0
            self.n = 0

        def f32(self, cols):
            a = arena[:, self.off:self.off + cols]
            self.off += cols
            assert self.off <= ARENA, ("SBUF arena overflow", self.off)
            return a

        def bf(self, cols):
            w = (cols + 1) // 2
            a = arena[:, self.off:self.off + w].bitcast(BF16)
            self.off += w
            assert self.off <= ARENA, ("SBUF arena overflow", self.off)
            return a[:, 0:cols]

        def key(self, name):
            self.n += 1
            return (name, self.n)

    B = Bump()

    class Ring:
        def __init__(self, name, n, cols, dt):
            self.bufs = [(B.f32(cols) if dt == F32 else B.bf(cols), B.key(name)) for _ in range(n)]
            self.i = 0

        def next(self):
            r = self.bufs[self.i % len(self.bufs)]
            self.i += 1
            return r

    bank_rr = [0]

    def next_bank(lo=0, hi=8):
        b = lo + bank_rr[0] % (hi - lo)
        bank_rr[0] += 1
        return b

    def PSK(b):
        return ("ps", b)

    cb = B.bf(8 * 128 + 256)
    ident = cb[:, 0:128]
    ones128 = cb[:, 128:256]
    ones64 = cb[:, 256:384]
    ones_lo = cb[:, 384:512]
    ones_hi = cb[:, 512:640]
    permB = cb[:, 640:768]
    permC = cb[:, 768:896]
    band = cb[:, 1024:1280]
    vecs = [B.f32(NV) for _ in range(n_layers)]
    lamt = [B.f32(8) for _ in range(n_layers)]
    modfm = [B.f32(96 * nseq) for _ in range(n_layers)]
    sel = B.f32(4 * 128)
    S.add("sp", lambda e: e.dma_start(out=sel[0:4, :], in_=selc), writes=[("sel", 0)], dma=True)
    KCONST = ("const", 0)
    S.add("sp", lambda e: e.dma_start(out=cb, in_=cbf), writes=[KCONST], dma=True)
    for l in range(n_layers):
        S.add("sp", lambda e, l=l: e.dma_start(out=vecs[l], in_=vec_fm[l]), writes=[("vec", l)], dma=True)
    persist_mark = B.off

    def dma_ld(out, in_, reads, writes, slow=False):
        if slow:
            return S.add("sp", lambda e: e.dma_start(out=out, in_=in_, allow_slow_non_contiguous=True), reads=reads, writes=writes, dma=True)
        return S.add("sp", lambda e: e.dma_start(out=out, in_=in_), reads=reads, writes=writes, dma=True)

    def dma_st(out, in_, reads, writes):
        return S.add("pool", lambda e: e.dma_start(out=out, in_=in_), reads=reads, writes=writes, dma=True)

    def dma_cast(out, in_, reads, writes):
        return S.add("pool", lambda e: e.dma_start(out=out, in_=in_), reads=reads, writes=writes, dma=True, pool="cast")

    def pe_mm(out, lhsT, rhs, start, stop, reads, writes):
        return S.add("pe", lambda e: e.matmul(out, lhsT=lhsT, rhs=rhs, start=start, stop=stop), reads=reads, writes=writes)

    def act(out, in_, func, reads, writes, scale=1.0, bias=0.0, accum_out=None):
        if accum_out is None:
            fn = lambda e: e.activation(out=out, in_=in_, func=func, bias=bias, scale=scale)
        else:
            fn = lambda e: e.activation(out=out, in_=in_, func=func, bias=bias, scale=scale, accum_out=accum_out)
        return S.add("act", fn, reads=reads, writes=writes)

    def dve(fn, reads, writes):
        return S.add("dve", fn, reads=reads, writes=writes)

    def rstd_from(ssb_ap, n, out_f32, reads, wkey, tmp, tmpkey):
        act(tmp, ssb_ap, AF.Ln, reads=reads, writes=[tmpkey], scale=1.0 / n, bias=EPS)
        act(out_f32, tmp, AF.Exp, reads=[tmpkey], writes=[wkey], scale=-0.5)

    def cast_weight(src, dst, name, l, rows):
        step = CSTEP
        for r0 in range(0, rows, step):
            dma_cast(dst[r0:min(r0 + step, rows), :], src[r0:min(r0 + step, rows), :], reads=[], writes=[("W", l, name, r0 // step)])

    def wkeys(l, name, rows):
        return [("W", l, name, i) for i in range((rows + CSTEP - 1) // CSTEP)]

    zt = B.bf(1024)
    S.add("sp", lambda e: e.dma_start(out=zt, in_=zrow), writes=[("zt", 0)], dma=True)
    for j in range(14, 20):
        dma_st(qk_d[j, :, 0:PADK], zt, reads=[("zt", 0)], writes=[("padk", j, 0)])
        dma_st(qk_d[j, :, PADK + SMAX:PADK + SMAX + PADK], zt, reads=[("zt", 0)], writes=[("padk", j, 1)])
    for r0 in range(0, PADK, 128):
        dma_st(vb_d[r0:r0 + 128, :], zt[:, 0:768], reads=[("zt", 0)], writes=[("padv", r0, 0)])
        dma_st(vb_d[PADK + SMAX + r0:PADK + SMAX + r0 + 128, :], zt[:, 0:768], reads=[("zt", 0)], writes=[("padv", r0, 1)])

    persist_mark = B.off

    def issue_casts(l, first):
        if first:
            cast_weight(w_in[l], win_d[l], "win", l, D)
            return
        cast_weight(w_gate[l], wgate_d[l], "wgate", l, D)
        cast_weight(w_bra[l], wbr_d[l, 0:512, :], "wbra", l, 512)
        cast_weight(w_brb[l], wbr_d[l, 512:768, :], "wbrb", l, 256)
        cast_weight(w_brc[l], wbr_d[l, 768:1536, :], "wbrc", l, 768)
        cast_weight(w_out[l], wout_d[l], "wout", l, D)
        cast_weight(w_ff1[l], wff1_d[l], "wff1", l, D)
        cast_weight(w_ff2[l], wff2_d[l], "wff2", l, DFF)

    def prologue():
        B.off = persist_mark
        cs = B.f32(KC * nseq)
        csb = B.bf(KC * nseq)
        dma_ld(cs, cT.rearrange("p k s -> p (k s)"), reads=[], writes=[("cs", 0)])
        act(csb, cs, AF.Silu, reads=[("cs", 0)], writes=[("csb", 0)])
        csb3 = csb.rearrange("p (k s) -> p k s", s=nseq)
        wr = Ring("wada", 3, KC * 512 // 2, F32)
        br = Ring("bada", 2, 512, F32)
        orr = Ring("modo", 2, 512, F32)
        for l in range(n_layers):
            for cbk in range(24):
                wt, wk = wr.next()
                wtb = wt.bitcast(BF16).rearrange("p (k n) -> p k n", k=KC)
                c0 = cbk * 512
                S.add("pool", lambda e, wtb=wtb, l=l, c0=c0: e.dma_start(
                    out=wtb, in_=w_ada[l].rearrange("(k p) n -> p k n", p=128)[:, :, c0:c0 + 512]),
                    reads=[], writes=[wk], dma=True)
                bt, bk = br.next()
                dma_ld(bt[0:nseq, :], b_ada3[l, :, c0:c0 + 512], reads=[], writes=[bk])
                b = next_bank()
                for k in range(KC):
                    pe_mm(ps[0:nseq, b, :], csb3[:, k, :], wtb[:, k, :], k == 0, k == KC - 1,
                          reads=[wk, ("csb", 0)], writes=[PSK(b)])
                ot, ok = orr.next()
                dve(lambda e, ot=ot, b=b, bt=bt: e.tensor_tensor(out=ot[0:nseq, :], in0=ps[0:nseq, b, :], in1=bt[0:nseq, :], op=ALU.add),
                    reads=[PSK(b), bk], writes=[ok])
                dma_st(mod_d[l, :, c0:c0 + 512], ot[0:nseq, :], reads=[ok], writes=[("mod", l, cbk)])
                b2 = next_bank()
                mf3 = modfm[l].rearrange("p (c s) -> p c s", s=nseq)
                for cc in range(4):
                    for k in range(KC):
                        pe_mm(ps[:, b2, cc * nseq:(cc + 1) * nseq], wtb[:, k, cc * 128:(cc + 1) * 128], csb3[:, k, :], k == 0, k == KC - 1,
                              reads=[wk, ("csb", 0)], writes=[PSK(b2)])
                for cc in range(4):
                    c = cbk * 4 + cc
                    dve(lambda e, l=l, c=c, cc=cc, b2=b2, mf3=mf3: e.tensor_scalar(
                        out=mf3[:, c, :], in0=ps[:, b2, cc * nseq:(cc + 1) * nseq], scalar1=vecs[l][:, V_BADA + c:V_BADA + c + 1],
                        scalar2=None, op0=ALU.add), reads=[PSK(b2), ("vec", l)], writes=[("modfm", l, c)])
            v = vecs[l]
            lt = lamt[l]
            tmp = B.f32(128)
            tk = B.key("lamtmp")
            lam_init = 0.8 - 0.6 * math.exp(-0.3 * l)
            dve(lambda e, v=v, tmp=tmp: e.tensor_tensor(out=tmp[:, 0:64], in0=v[:, V_LAM:V_LAM + 64], in1=v[:, V_LAM + 64:V_LAM + 128], op=ALU.mult),
                reads=[("vec", l)], writes=[tk])
            dve(lambda e, v=v, tmp=tmp: e.tensor_tensor(out=tmp[:, 64:128], in0=v[:, V_LAM + 128:V_LAM + 192], in1=v[:, V_LAM + 192:V_LAM + 256], op=ALU.mult),
                reads=[("vec", l), tk], writes=[tk])
            dve(lambda e, lt=lt, tmp=tmp: e.reduce_sum(out=lt[:, 2:3], in_=tmp[:, 0:64], axis=AX.X), reads=[tk], writes=[("lam", l, 2)])
            dve(lambda e, lt=lt, tmp=tmp: e.reduce_sum(out=lt[:, 3:4], in_=tmp[:, 64:128], axis=AX.X), reads=[tk], writes=[("lam", l, 3)])
            act(lt[:, 4:6], lt[:, 2:4], AF.Exp, reads=[("lam", l, 2), ("lam", l, 3)], writes=[("lam", l, 4)])
            dve(lambda e, lt=lt: e.tensor_tensor(out=lt[:, 6:7], in0=lt[:, 5:6], in1=lt[:, 4:5], op=ALU.subtract),
                reads=[("lam", l, 4)], writes=[("lam", l, 6)])
            dve(lambda e, lt=lt, li=lam_init: e.tensor_scalar(out=lt[:, 0:1], in0=lt[:, 6:7], scalar1=-li, scalar2=None, op0=ALU.add),
                reads=[("lam", l, 6)], writes=[("lam", l, 0)])
            dve(lambda e, lt=lt, v=v, li=lam_init: e.tensor_scalar(out=lt[:, 1:2], in0=v[:, V_SUB:V_SUB + 1], scalar1=1.0 - li, scalar2=None, op0=ALU.mult),
                reads=[("vec", l)], writes=[("lam", l, 1)])
        S.barrier()

    def norm_transpose(xt, xk, a_fm, b_fm, abk, hT3, hkeys, st, R):
        import os
        NS = int(os.environ.get("KNS", "99"))
        junk, jk = R["junk"].next()
        ssq, sk = R["ss"].next()
        dve(lambda e: e.memset(ssq, 0.0), reads=[], writes=[sk])
        act(junk, xt, AF.Square, reads=[xk, sk], writes=[jk, sk], accum_out=ssq[:, 0:1])
        if NS < 2: return
        rstd_from(ssq[:, 0:1], D, ssq[:, 2:3], [sk], sk, ssq[:, 1:2], sk)
        if NS < 3: return
        xn, xnk = R["xn"].next()
        dve(lambda e: e.tensor_scalar(out=xn, in0=xt, scalar1=ssq[:, 2:3], scalar2=None, op0=ALU.mult),
            reads=[xk, sk], writes=[xnk])
        if NS < 4: return
        b0 = R["tb"][R["tbi"][0] % 2]
        R["tbi"][0] += 1
        pT = psflat[:, b0 * 512:(b0 + 2) * 512].bitcast(BF16).rearrange("p (k t) -> p k t", k=KC)
        for k in range(KC):
            S.add("pe", lambda e, k=k: e.transpose(pT[:, k, :], xn[:, k * 128:(k + 1) * 128], ident),
                  reads=[xnk, KCONST], writes=[PSK(b0 + k // 8)])
        if NS < 5: return
        for k in range(KC):
            o = hT3[:, k, st * 128:(st + 1) * 128]
            EV = os.environ.get("KEV", "mix")
            if (k % 2 == 0 and EV == "mix") or EV == "act":
                act(o, pT[:, k, :], AF.Identity, reads=[PSK(b0 + k // 8), abk], writes=[hkeys[st][0]],
                    scale=a_fm[:, k:k + 1], bias=b_fm[:, k:k + 1])
            else:
                dve(lambda e, o=o, k=k: e.tensor_scalar(out=o, in0=pT[:, k, :], scalar1=a_fm[:, k:k + 1], scalar2=b_fm[:, k:k + 1],
                                                         op0=ALU.mult, op1=ALU.add),
                    reads=[PSK(b0 + k // 8), abk], writes=[hkeys[st][1]])

    def load_mod_fm(l, s, idx_shift, idx_scale, gcol, a_fm, b_fm, abk):
        mf3 = modfm[l].rearrange("p (c s) -> p c s", s=nseq)
        v = vecs[l]
        dve(lambda e: e.scalar_tensor_tensor(out=a_fm, in0=mf3[:, idx_scale * 16:(idx_scale + 1) * 16, s], scalar=1.0,
                                             in1=v[:, gcol:gcol + KC], op0=ALU.add, op1=ALU.mult),
            reads=[("vec", l)], writes=[abk])
        dve(lambda e: e.tensor_copy(out=b_fm, in_=mf3[:, idx_shift * 16:(idx_shift + 1) * 16, s]), reads=[], writes=[abk])

    def gate_bcast(l, s, idx, gbc, kg, tmp, tmpk):
        dma_ld(tmp[0:nseq, :], mod_d[l, :, idx * D:(idx + 1) * D], reads=[], writes=[tmpk])
        for cbk in range(4):
            b = next_bank()
            pe_mm(ps[:, b, :], sel[0:nseq, s * 128:(s + 1) * 128], tmp[0:nseq, cbk * 512:(cbk + 1) * 512], True, True,
                  reads=[tmpk, ("sel", 0)], writes=[PSK(b)])
            dve(lambda e, b=b, cbk=cbk: e.tensor_copy(out=gbc[:, cbk * 512:(cbk + 1) * 512], in_=ps[:, b, :]), reads=[PSK(b)], writes=[kg])

    QK_MAP = {}
    for j in range(48):
        if j < 4: QK_MAP[j] = ("A", V_QNA, j)
        elif j < 8: QK_MAP[j] = ("A", V_KNA, j)
        elif j < 12: QK_MAP[j] = None
        elif j < 18: QK_MAP[j] = ("B", V_QNB, 8 + j - 12)
        elif j < 24: QK_MAP[j] = ("B", V_KNB, 14 + j - 18)
        elif j < 30: QK_MAP[j] = None
        elif j < 36: QK_MAP[j] = ("C", V_QNC, 20 + j - 30)
        elif j < 42: QK_MAP[j] = ("C", V_KNC, 26 + j - 36)
        else: QK_MAP[j] = None
    V_SEGS = {2: [(0, 512, va_d, 0)], 6: [(0, 512, vb_d, 0)], 7: [(0, 256, vb_d, 512)],
              10: [(256, 256, vc_d, 0)], 11: [(0, 512, vc_d, 256)]}

    def phase_A(l, s, row0, Sq, xsrc):
        B.off = persist_mark
        if Sq < SMAX:
            for j in range(14, 20):
                dma_st(qk_d[j, :, PADK + Sq:PADK + Sq + PADK], zt, reads=[("zt", 0)], writes=[("padk2", j)])
            for r0 in range(0, PADK, 128):
                dma_st(vb_d[PADK + Sq + r0:PADK + Sq + r0 + 128, :], zt[:, 0:768], reads=[("zt", 0)], writes=[("padv2", r0)])
        a_fm = B.f32(KC); b_fm = B.f32(KC); abk = B.key("ab1")
        load_mod_fm(l, s, 0, 1, V_N1G, a_fm, b_fm, abk)
        R = {"junk": Ring("junk", 1, D, BF16), "ss": Ring("ss", 3, 4, F32), "xn": Ring("xn", 2, D, BF16),
             "tb": (6, 6), "tbi": [0]}
        xr = Ring("xt", 2, D, F32)
        hr = [(B.bf(KC * T), [[B.key("hT"), B.key("hT")] for _ in range(4)]) for _ in range(2)]
        wr = Ring("wbig", 3, KC * 512 // 2, F32)
        rope = Ring("rope", 2, 4 * T, F32)
        sqr = Ring("sq", 3, T, BF16)
        xsr = Ring("xs", 3, T, BF16)
        rsr = Ring("rstd", 2, 2 * T, F32)
        t1r = Ring("t1", 2, T, F32)
        t2r = Ring("t2", 2, T, F32)
        qor = Ring("qo", 3, T, BF16)
        vor = Ring("vo", 3, T, BF16)
        v = vecs[l]
        ntl = Sq // T

        def prep(tt):
            t0 = tt * T
            hT, hkeys = hr[tt % 2]
            hT3 = hT.rearrange("p (k t) -> p k t", k=KC)
            hall = [k for st in range(4) for k in hkeys[st]]
            for st in range(4):
                xt, xk = xr.next()
                dma_ld(xt, xsrc[row0 + t0 + st * 128:row0 + t0 + (st + 1) * 128, :], reads=[], writes=[xk])
                norm_transpose(xt, xk, a_fm, b_fm, abk, hT3, hkeys, st, R)
            dma_st(hT_d.rearrange("k p s -> p k s")[:, :, t0:t0 + T], hT3, reads=hall, writes=[("hTd", tt)])
            rp, rpk = rope.next()
            rp4 = rp.rearrange("p (f t) -> p f t", f=4)
            dma_ld(rp4[:, 0:2, :], ropeB.rearrange("f p s -> p f s")[:, :, t0:t0 + T], reads=[], writes=[rpk])
            dma_ld(rp4[:, 2:4, :], ropeC.rearrange("f p s -> p f s")[:, :, t0:t0 + T], reads=[], writes=[rpk])
            return hT3, hkeys, hall, rp4, rpk

        cur = prep(0)
        for tt in range(ntl):
            t0 = tt * T
            hT3, hkeys, hall, rp4, rpk = cur
            nxt = None
            pending = None
            for blk in range(12):
                wt, wk = wr.next()
                wtb = wt.bitcast(BF16).rearrange("p (k n) -> p k n", k=KC)
                c0 = blk * 512
                dma_ld(wtb, win_d[l].rearrange("(k p) n -> p k n", p=128)[:, :, c0:c0 + 512],
                       reads=wkeys(l, "win", D), writes=[wk])
                for cc in range(4):
                    j = blk * 4 + cc
                    m = QK_MAP[j]
                    if m is None:
                        continue
                    typ, gcol, dj = m
                    b = next_bank(0, 6)
                    for k in range(KC):
                        pe_mm(ps[:, b, :], wtb[:, k, cc * 128:(cc + 1) * 128], hT3[:, k, :], k == 0, k == KC - 1,
                              reads=[wk] + hall, writes=[PSK(b)])
                    zp = ps[:, b, :]
                    g = v[:, gcol:gcol + 1]
                    sq, sqk = sqr.next()
                    act(sq, zp, AF.Square, reads=[PSK(b)], writes=[sqk])
                    xs = xsk = None
                    if typ != "A":
                        xs, xsk = xsr.next()
                        act(xs, zp, AF.Identity, reads=[PSK(b), ("vec", l)], writes=[xsk], scale=g)
                    if pending is not None:
                        pending()

                    def rest(typ=typ, dj=dj, b=b, zp=zp, g=g, sq=sq, sqk=sqk, xs=xs, xsk=xsk, rp4=rp4, rpk=rpk, t0=t0, tt=tt):
                        b2 = next_bank(0, 6)
                        onesm = ones64 if typ == "C" else ones128
                        nn = 64 if typ == "C" else 128
                        pe_mm(ps[:, b2, :], onesm, sq, True, True, reads=[sqk, KCONST], writes=[PSK(b2)])
                        if typ != "A":
                            b3 = next_bank(0, 6)
                            pm = permB if typ == "B" else permC
                            pe_mm(ps[:, b3, :], pm, xs, True, True, reads=[xsk, KCONST], writes=[PSK(b3)])
                        rs, rsk = rsr.next()
                        rstd_from(ps[:, b2, :], nn, rs[:, T:2 * T], [PSK(b2)], rsk, rs[:, 0:T], rsk)
                        rstd = rs[:, T:2 * T]
                        qo, qok = qor.next()
                        if typ == "A":
                            dve(lambda e: e.scalar_tensor_tensor(out=qo, in0=zp, scalar=g, in1=rstd, op0=ALU.mult, op1=ALU.mult),
                                reads=[PSK(b), rsk, ("vec", l)], writes=[qok])
                        else:
                            cosT = rp4[:, 0, :] if typ == "B" else rp4[:, 2, :]
                            sinT = rp4[:, 1, :] if typ == "B" else rp4[:, 3, :]
                            t1, t1k = t1r.next()
                            t2, t2k = t2r.next()
                            dve(lambda e: e.scalar_tensor_tensor(out=t1, in0=zp, scalar=g, in1=cosT, op0=ALU.mult, op1=ALU.mult),
                                reads=[PSK(b), rpk, ("vec", l)], writes=[t1k])
                            dve(lambda e: e.tensor_tensor(out=t2, in0=ps[:, b3, :], in1=sinT, op=ALU.mult),
                                reads=[PSK(b3), rpk], writes=[t2k])
                            dve(lambda e: e.tensor_tensor(out=t1, in0=t1, in1=t2, op=ALU.add), reads=[t1k, t2k], writes=[t1k])
                            dve(lambda e: e.tensor_tensor(out=qo, in0=t1, in1=rstd, op=ALU.mult), reads=[t1k, rsk], writes=[qok])
                        dma_st(qk_d[dj, :, PADK + t0:PADK + t0 + T], qo, reads=[qok], writes=[("qkd", dj, tt)])
                    pending = rest
                for (cin, ncol, dst, dcol) in V_SEGS.get(blk, []):
                    for st in range(4):
                        b = next_bank(0, 6)
                        for k in range(KC):
                            pe_mm(ps[:, b, 0:ncol], hT3[:, k, st * 128:(st + 1) * 128], wtb[:, k, cin:cin + ncol], k == 0, k == KC - 1,
                                  reads=[wk] + hkeys[st], writes=[PSK(b)])
                        if pending is not None:
                            pending()
                            pending = None
                        vo, vok = vor.next()
                        act(vo[:, 0:ncol], ps[:, b, 0:ncol], AF.Identity, reads=[PSK(b)], writes=[vok])
                        roff = PADK if dst is vb_d else 0
                        dma_st(dst[roff + t0 + st * 128:roff + t0 + (st + 1) * 128, dcol:dcol + ncol], vo[:, 0:ncol],
                               reads=[vok], writes=[("vd", blk, dcol, tt, st)])
                if blk == 6 and tt + 1 < ntl:
                    nxt = prep(tt + 1)
            if pending is not None:
                pending()
                pending = None
            cur = nxt
        S.barrier()

    def phase_B(l, Sq):
        B.off = persist_mark
        v = vecs[l]
        lt = lamt[l]
        sc128 = 128.0 ** -0.5
        sc64 = 64.0 ** -0.5
        mark = B.off
        R_rows = Sq // 64
        plan = na_plan(R_rows)
        QT = B.bf(Sq); KT = B.bf(Sq); Vt = B.bf(Sq); OT = B.bf(Sq)
        tab = B.f32(NT * 128)
        kQ, kK, kV, kO, kT = (B.key(n) for n in ("aQ", "aK", "aV", "aO", "aT"))
        Vt3 = Vt.rearrange("p (j c) -> p j c", c=128)
        sbr = Ring("asb", 2, 640, F32)
        ptr = Ring("apt", 2, 640, BF16)
        rzr = Ring("arz", 2, 512, F32)
        for h in range(4):
            dma_ld(QT, qk_d[h, :, PADK:PADK + Sq], reads=[], writes=[kQ])
            dma_ld(KT, qk_d[4 + h, :, PADK:PADK + Sq], reads=[], writes=[kK])
            dma_ld(Vt3, va_d[0:Sq, h * 128:(h + 1) * 128].rearrange("(j p) c -> p j c", p=128), reads=[], writes=[kV])
            dma_ld(tab, rpbx[l, h], reads=[], writes=[kT])
            npairs = R_rows // 2
            for i0 in range(0, npairs, 4):
                bO = 4 + (i0 // 4) % 2
                bZ = 6 + (i0 // 4) % 2
                for i in range(i0, i0 + 4):
                    js, sig = plan[i]
                    off, n = cases[sig]
                    bS = 2 * ((i) % 2)
                    for jj, j in enumerate(js):
                        pe_mm(ps[:, bS + jj // 4, (jj % 4) * 128:(jj % 4 + 1) * 128], KT[:, j * 128:(j + 1) * 128], QT[:, i * 128:(i + 1) * 128],
                              True, True, reads=[kK, kQ], writes=[PSK(bS + jj // 4)])
                    sb, sbk = sbr.next()
                    pt, ptk = ptr.next()
                    dve(lambda e, sb=sb, bS=bS, n=n, off=off: e.scalar_tensor_tensor(
                        out=sb[:, 0:n * 128], in0=psflat[:, bS * 512:bS * 512 + n * 128], scalar=sc128,
                        in1=tab[:, off * 128:(off + n) * 128], op0=ALU.mult, op1=ALU.add),
                        reads=[PSK(bS), PSK(bS + 1), kT], writes=[sbk])
                    act(pt[:, 0:n * 128], sb[:, 0:n * 128], AF.Exp, reads=[sbk], writes=[ptk])
                    sl = i - i0
                    for jj, j in enumerate(js):
                        pe_mm(ps[:, bO, sl * 128:(sl + 1) * 128], Vt3[:, j, :], pt[:, jj * 128:(jj + 1) * 128], jj == 0, jj == n - 1,
                              reads=[kV, ptk], writes=[PSK(bO)])
                    for jj, j in enumerate(js):
                        pe_mm(ps[:, bZ, sl * 128:(sl + 1) * 128], ones128, pt[:, jj * 128:(jj + 1) * 128], jj == 0, jj == n - 1,
                              reads=[KCONST, ptk], writes=[PSK(bZ)])
                rz, rzk = rzr.next()
                dve(lambda e, rz=rz, bZ=bZ: e.reciprocal(out=rz, in_=ps[:, bZ, :]), reads=[PSK(bZ)], writes=[rzk])
                dve(lambda e, rz=rz, bO=bO, i0=i0, OT=OT: e.tensor_tensor(out=OT[:, i0 * 128:(i0 + 4) * 128], in0=ps[:, bO, :], in1=rz, op=ALU.mult),
                    reads=[PSK(bO), rzk], writes=[kO])
            dma_st(oT_d[h, :, 0:Sq], OT, reads=[kO], writes=[("oTd", h)])
        S.barrier()
        B.off = mark
        QT = B.bf(Sq); KT = B.bf(Sq + 2 * PADK); OT = B.bf(Sq)
        QTr = B.bf(Sq); KTr = B.bf(Sq + 2 * PADK)
        accO = B.f32(Sq); accZ = B.f32(Sq)
        kQ, kK, kO, kaO, kaZ = (B.key(n) for n in ("bQ", "bK", "bO", "baO", "baZ"))
        kQr, kKr = B.key("bQr"), B.key("bKr")
        vtr = Ring("bvt", 2, 33 * 128, BF16)
        er = Ring("be", 2, 256, BF16)
        ptr = Ring("bpt", 2, 256, BF16)
        for hg in range(2):
            for g, d in enumerate((1, 4, 16)):
                cj = g * 2 + hg
                dma_ld(QT, qk_d[8 + cj, :, PADK:PADK + Sq], reads=[], writes=[kQ])
                dma_ld(KT, qk_d[14 + cj, :, 0:Sq + 2 * PADK], reads=[], writes=[kK])
                L = Sq // d
                nq = L // 128
                if d == 1:
                    Qv = QT.rearrange("p (d m) -> p d m", d=1)
                    Kv = KT[:, PADK - 64:PADK - 64 + L + 128].rearrange("p (d m) -> p d m", d=1)
                    kQu, kKu = kQ, kK
                else:
                    Qv = QTr.rearrange("p (d m) -> p d m", d=d)
                    Kv = KTr[:, 0:(L + 128) * d].rearrange("p (d m) -> p d m", d=d)
                    dve(lambda e, Qv=Qv, QT=QT, d=d: e.tensor_copy(out=Qv, in_=QT.rearrange("p (m d) -> p d m", d=d)),
                        reads=[kQ], writes=[kQr])
                    act(Kv, KT[:, PADK - 64 * d:PADK - 64 * d + (L + 128) * d].rearrange("p (m d) -> p d m", d=d), AF.Identity,
                        reads=[kK], writes=[kKr])
                    kQu, kKu = kQr, kKr
                for r in range(d):
                    vt, vtk = vtr.next()
                    vt3 = vt[:, 0:(nq + 1) * 128].rearrange("p (j c) -> p j c", c=128)
                    base = PADK - 64 * d + r
                    nrow = (nq + 1) * 128
                    vsrc = vb_d[base:base + (nrow - 1) * d + 1:d, cj * 128:(cj + 1) * 128].rearrange("(j p) c -> p j c", p=128)
                    dma_ld(vt3, vsrc, reads=[], writes=[vtk])
                    for i0 in range(0, nq, 4):
                        w = min(4, nq - i0)
                        bO = 4 + (i0 // 4) % 2
                        bZ = 6 + (i0 // 4) % 2
                        for i in range(i0, i0 + w):
                            bS = next_bank(0, 4)
                            rhs = Qv[:, r, 128 * i:128 * (i + 1)]
                            for t in range(2):
                                pe_mm(ps[:, bS, t * 128:(t + 1) * 128], Kv[:, r, 128 * (i + t):128 * (i + t + 1)], rhs, True, True,
                                      reads=[kKu, kQu], writes=[PSK(bS)])
                            ee, ek = er.next()
                            pt, ptk = ptr.next()
                            act(ee, ps[:, bS, 0:256], AF.Exp, reads=[PSK(bS)], writes=[ek], scale=sc128)
                            dve(lambda e, pt=pt, ee=ee: e.tensor_tensor(out=pt, in0=ee, in1=band, op=ALU.mult),
                                reads=[ek, KCONST], writes=[ptk])
                            sl = i - i0
                            for t in range(2):
                                pe_mm(ps[:, bO, sl * 128:(sl + 1) * 128], vt3[:, i + t, :], pt[:, t * 128:(t + 1) * 128], t == 0, t == 1,
                                      reads=[vtk, ptk], writes=[PSK(bO)])
                            for t in range(2):
                                jp = i + t
                                om = ones_hi if jp == 0 else (ones_lo if jp == nq else ones128)
                                pe_mm(ps[:, bZ, sl * 128:(sl + 1) * 128], om, pt[:, t * 128:(t + 1) * 128], t == 0, t == 1,
                                      reads=[KCONST, ptk], writes=[PSK(bZ)])
                        p0 = r + 128 * i0 * d
                        dO = accO[:, p0:p0 + (w * 128 - 1) * d + 1:d]
                        dZ = accZ[:, p0:p0 + (w * 128 - 1) * d + 1:d]
                        if g == 0:
                            act(dO, ps[:, bO, 0:w * 128], AF.Identity, reads=[PSK(bO)], writes=[kaO])
                            act(dZ, ps[:, bZ, 0:w * 128], AF.Identity, reads=[PSK(bZ)], writes=[kaZ])
                        else:
                            dve(lambda e, dO=dO, bO=bO, w=w: e.tensor_tensor(out=dO, in0=dO, in1=ps[:, bO, 0:w * 128], op=ALU.add),
                                reads=[PSK(bO), kaO], writes=[kaO])
                            dve(lambda e, dZ=dZ, bZ=bZ, w=w: e.tensor_tensor(out=dZ, in0=dZ, in1=ps[:, bZ, 0:w * 128], op=ALU.add),
                                reads=[PSK(bZ), kaZ], writes=[kaZ])
            dve(lambda e, accZ=accZ: e.reciprocal(out=accZ, in_=accZ), reads=[kaZ], writes=[kaZ])
            dve(lambda e, OT=OT, accO=accO, accZ=accZ: e.tensor_tensor(out=OT, in0=accO, in1=accZ, op=ALU.mult), reads=[kaO, kaZ], writes=[kO])
            dma_st(oT_d[4 + hg, :, 0:Sq], OT, reads=[kO], writes=[("oTd", 4 + hg)])
        S.barrier()
        B.off = mark
        csets = []
        for _ in range(2):
            Qx = B.bf(2 * Sq); KT = B.bf(Sq); Vt = B.bf(Sq); OT = B.bf(Sq)
            ks = tuple(B.key(n) for n in ("cQ", "cK", "cV", "cO"))
            dve(lambda e, Qx=Qx: e.memset(Qx, 0.0), reads=[], writes=[ks[0]])
            csets.append((Qx, KT, Vt, OT, ks))
        ptr = Ring("cpt", 5, 512, BF16)
        rzr = Ring("crz", 2, 512, F32)
        tr = Ring("ct", 2, 512, F32)
        ar = Ring("ca", 3, 256, F32)
        sqr = Ring("csq", 3, 256, BF16)
        rsr = Ring("crs", 2, 512, F32)
        PRE = 2
        nkt = Sq // 128
        nqb = Sq // 256

        def c_load(hc):
            Qx, KT, Vt, OT, (kQ, kK, kV, kO) = csets[hc % 2]
            Qx3 = Qx.rearrange("p (c s) -> p c s", c=2)
            Vt3 = Vt.rearrange("p (j c) -> p j c", c=128)
            dma_ld(Qx3[0:64, 0, :], qk_d[20 + hc, 0:64, PADK:PADK + Sq], reads=[], writes=[kQ])
            dma_ld(Qx3[64:128, 1, :], qk_d[20 + hc, 64:128, PADK:PADK + Sq], reads=[], writes=[kQ])
            dma_ld(KT, qk_d[26 + hc, :, PADK:PADK + Sq], reads=[], writes=[kK])
            dma_ld(Vt3, vc_d[0:Sq, hc * 128:(hc + 1) * 128].rearrange("(j p) c -> p j c", p=128), reads=[], writes=[kV])

        c_load(0)
        for hc in range(6):
            if hc + 1 < 6:
                c_load(hc + 1)
            Qx, KT, Vt, OT, (kQ, kK, kV, kO) = csets[hc % 2]
            Qx3 = Qx.rearrange("p (c s) -> p c s", c=2)
            Vt3 = Vt.rearrange("p (j c) -> p j c", c=128)
            items = [(qb, kt) for qb in range(nqb) for kt in range(nkt)]
            stash = {}
            pend = []
            for n in range(len(items) + PRE):
                if n < len(items):
                    qb, kt = items[n]
                    bS = next_bank(0, 3)
                    pe_mm(ps[:, bS, :].rearrange("p (c q) -> p c q", c=2), KT[:, kt * 128:(kt + 1) * 128], Qx3[:, :, qb * 256:(qb + 1) * 256],
                          True, True, reads=[kK, kQ], writes=[PSK(bS)])
                    pt, ptk = ptr.next()
                    act(pt, ps[:, bS, :], AF.Exp, reads=[PSK(bS)], writes=[ptk], scale=sc64)
                    stash[n] = (pt, ptk)
                m = n - PRE
                if m >= 0:
                    qb, kt = items[m]
                    pt, ptk = stash.pop(m)
                    bO = 3 + qb % 2
                    bZ = 5 + qb % 2
                    pe_mm(ps[:, bO, :], Vt3[:, kt, :], pt, kt == 0, kt == nkt - 1, reads=[kV, ptk], writes=[PSK(bO)])
                    pe_mm(ps[:, bZ, :], ones128, pt, kt == 0, kt == nkt - 1, reads=[KCONST, ptk], writes=[PSK(bZ)])
                    if kt == nkt - 1:
                        rz, rzk = rzr.next()
                        tt_, ttk = tr.next()
                        aa, aak = ar.next()
                        sq, sqk = sqr.next()
                        dve(lambda e, rz=rz, bZ=bZ: e.reciprocal(out=rz, in_=ps[:, bZ, :]), reads=[PSK(bZ)], writes=[rzk])
                        dve(lambda e, tt_=tt_, rz=rz, bO=bO: e.tensor_tensor(out=tt_, in0=ps[:, bO, :], in1=rz, op=ALU.mult),
                            reads=[PSK(bO), rzk], writes=[ttk])
                        dve(lambda e, aa=aa, tt_=tt_: e.scalar_tensor_tensor(out=aa, in0=tt_[:, 256:512], scalar=lt[:, 0:1], in1=tt_[:, 0:256],
                                                                              op0=ALU.mult, op1=ALU.add),
                            reads=[ttk, ("lam", l, 0)], writes=[aak])
                        act(sq, aa, AF.Square, reads=[aak], writes=[sqk])

                        def part2(aa=aa, aak=aak, sq=sq, sqk=sqk, qb=qb, OT=OT, kO=kO):
                            rs, rsk = rsr.next()
                            pe_mm(ps[:, 7, 0:256], ones128, sq, True, True, reads=[KCONST, sqk], writes=[PSK(7)])
                            rstd_from(ps[:, 7, 0:256], 128, rs[:, 256:512], [PSK(7)], rsk, rs[:, 0:256], rsk)
                            dve(lambda e: e.scalar_tensor_tensor(out=OT[:, qb * 256:(qb + 1) * 256], in0=aa, scalar=lt[:, 1:2],
                                                                 in1=rs[:, 256:512], op0=ALU.mult, op1=ALU.mult),
                                reads=[aak, rsk, ("lam", l, 1)], writes=[kO])
                        pend.append((n + 6, part2))
                while pend and pend[0][0] <= n:
                    pend.pop(0)[1]()
            for _, fn in pend:
                fn()
            dma_st(oT_d[6 + hc, :, 0:Sq], OT, reads=[kO], writes=[("oTd", 6 + hc)])
        S.barrier()

    def phase_C1(l, s, row0, Sq, xsrc):
        B.off = persist_mark
        v = vecs[l]
        g1bc = B.f32(D); kg1 = B.key("g1bc")
        GATE1 = True
        OTt = B.bf(12 * T); kOT = B.key("OTt")
        OT3 = OTt.rearrange("p (c t) -> p c t", c=12)
        hT = B.bf(KC * T); khT = B.key("hTt")
        hT3 = hT.rearrange("p (k t) -> p k t", k=KC)
        mT = B.bf(KC * T)
        mT3 = mT.rearrange("p (k t) -> p k t", k=KC)
        kmT = [B.key("mT") for _ in range(KC)]
        xts = [(B.f32(D), B.key("xt")) for _ in range(4)]
        gate_bcast(l, s, 2, g1bc, kg1, xts[0][0], xts[0][1])
        wsr = Ring("wsm", 3, (48 + 12) * 128 // 2, F32)
        wbr = Ring("wbig", 2, KC * 512 // 2, F32)
        sgr = Ring("sg", 4, T, F32)
        mr = Ring("m", 3, T, F32)
        tr = Ring("tmp", 2, T, F32)
        for tt in range(Sq // T):
            t0 = tt * T
            dma_ld(OT3, oT_d.rearrange("c p s -> p c s")[:, :, t0:t0 + T], reads=[], writes=[kOT])
            dma_ld(hT3, hT_d.rearrange("k p s -> p k s")[:, :, t0:t0 + T], reads=[], writes=[khT])
            for st in range(4):
                dma_ld(xts[st][0], xsrc[row0 + t0 + st * 128:row0 + t0 + (st + 1) * 128, :], reads=[], writes=[xts[st][1]])
            for dc in range(KC):
                wt, wk = wsr.next()
                wtb = wt.bitcast(BF16)
                wg = wtb[:, 0:48 * 128].rearrange("p (g k n) -> p g k n", g=3, k=KC)
                wb = wtb[:, 48 * 128:60 * 128].rearrange("p (k n) -> p k n", k=12)
                dma_ld(wg, wgate_d[l].rearrange("(k p) (g n) -> p g k n", p=128, g=3)[:, :, :, dc * 128:(dc + 1) * 128],
                       reads=wkeys(l, "wgate", D), writes=[wk])
                dma_ld(wb, wbr_d[l].rearrange("(k p) n -> p k n", p=128)[:, :, dc * 128:(dc + 1) * 128],
                       reads=wkeys(l, "wbra", 512) + wkeys(l, "wbrb", 256) + wkeys(l, "wbrc", 768), writes=[wk])
                kcs = [(0, 4), (4, 6), (6, 12)]
                mcur = None
                for gi in range(3):
                    bg = next_bank()
                    for k in range(KC):
                        pe_mm(ps[:, bg, :], wg[:, gi, k, :], hT3[:, k, :], k == 0, k == KC - 1, reads=[wk, khT], writes=[PSK(bg)])
                    by = next_bank()
                    k0, k1 = kcs[gi]
                    for k in range(k0, k1):
                        pe_mm(ps[:, by, :], wb[:, k, :], OT3[:, k, :], k == k0, k == k1 - 1, reads=[wk, kOT], writes=[PSK(by)])
                    sg, sgk = sgr.next()
                    act(sg, ps[:, bg, :], AF.Sigmoid, reads=[PSK(bg), ("vec", l)], writes=[sgk],
                        bias=v[:, V_BG + gi * 16 + dc:V_BG + gi * 16 + dc + 1])
                    mm_, mk = mr.next()
                    dve(lambda e, mm_=mm_, sg=sg, by=by: e.tensor_tensor(out=mm_, in0=ps[:, by, :], in1=sg, op=ALU.mult),
                        reads=[PSK(by), sgk], writes=[mk])
                    if gi == 0:
                        mcur = (mm_, mk)
                    elif gi == 1:
                        dve(lambda e, a=mcur[0], b_=mm_: e.tensor_tensor(out=a, in0=a, in1=b_, op=ALU.add),
                            reads=[mcur[1], mk], writes=[mcur[1]])
                    else:
                        dve(lambda e, a=mcur[0], b_=mm_, dc=dc: e.tensor_tensor(out=mT3[:, dc, :], in0=a, in1=b_, op=ALU.add),
                            reads=[mcur[1], mk], writes=[kmT[dc]])
            for cbk in range(4):
                wt, wk = wbr.next()
                wtb = wt.bitcast(BF16).rearrange("p (k n) -> p k n", k=KC)
                dma_ld(wtb, wout_d[l].rearrange("(k p) n -> p k n", p=128)[:, :, cbk * 512:(cbk + 1) * 512],
                       reads=wkeys(l, "wout", D), writes=[wk])
                for st in range(4):
                    b = next_bank()
                    for k in range(KC):
                        pe_mm(ps[:, b, :], mT3[:, k, st * 128:(st + 1) * 128], wtb[:, k, :], k == 0, k == KC - 1,
                              reads=[wk, kmT[k]], writes=[PSK(b)])
                    tm, tk = tr.next()
                    xt, xk = xts[st]
                    dve(lambda e, tm=tm, b=b, cbk=cbk: e.tensor_tensor(out=tm, in0=ps[:, b, :], in1=g1bc[:, cbk * 512:(cbk + 1) * 512], op=ALU.mult),
                        reads=[PSK(b), kg1], writes=[tk])
                    dve(lambda e, tm=tm, xt=xt, cbk=cbk: e.tensor_tensor(out=xt[:, cbk * 512:(cbk + 1) * 512], in0=xt[:, cbk * 512:(cbk + 1) * 512],
                                                                          in1=tm, op=ALU.add),
                        reads=[tk, xk], writes=[xk])
            for st in range(4):
                dma_st(xmid_d[t0 + st * 128:t0 + (st + 1) * 128, :], xts[st][0], reads=[xts[st][1]], writes=[("xmid", tt, st)])
        S.barrier()

    def phase_C2(l, s, row0, Sq, xdst):
        B.off = persist_mark
        a_fm = B.f32(KC); b_fm = B.f32(KC); abk = B.key("ab2")
        load_mod_fm(l, s, 3, 4, V_N2G, a_fm, b_fm, abk)
        g2bc = B.f32(D); kg2 = B.key("g2bc")
        GATE2 = True
        R = {"junk": Ring("junk", 1, D, BF16), "ss": Ring("ss", 3, 4, F32), "xn": Ring("xn", 2, D, BF16),
             "tb": (0, 2), "tbi": [0]}
        xts = [(B.f32(D), B.key("xt")) for _ in range(4)]
        gate_bcast(l, s, 5, g2bc, kg2, xts[0][0], xts[0][1])
        hT = B.bf(KC * T)
        hT3 = hT.rearrange("p (k t) -> p k t", k=KC)
        hkeys = [[B.key("h2T"), B.key("h2T")] for _ in range(4)]
        hall = [k for st in range(4) for k in hkeys[st]]
        f1 = B.bf(64 * T)
        f13 = f1.rearrange("p (c t) -> p c t", c=64)
        kf1 = [B.key("f1") for _ in range(64)]
        wbr = Ring("wbig", 3, KC * 512 // 2, F32)
        rr = Ring("relu", 3, T, BF16)
        tr = Ring("tmp", 2, T, F32)
        for tt in range(Sq // T):
            t0 = tt * T
            for st in range(4):
                xt, xk = xts[st]
                dma_ld(xt, xmid_d[t0 + st * 128:t0 + (st + 1) * 128, :], reads=[], writes=[xk])
                norm_transpose(xt, xk, a_fm, b_fm, abk, hT3, hkeys, st, R)
            for blk in range(16):
                wt, wk = wbr.next()
                wtb = wt.bitcast(BF16).rearrange("p (k n) -> p k n", k=KC)
                dma_ld(wtb, wff1_d[l].rearrange("(k p) n -> p k n", p=128)[:, :, blk * 512:(blk + 1) * 512],
                       reads=wkeys(l, "wff1", D), writes=[wk])
                for cc in range(4):
                    c = blk * 4 + cc
                    b = next_bank(0, 4)
                    for k in range(KC):
                        pe_mm(ps[:, b, :], wtb[:, k, cc * 128:(cc + 1) * 128], hT3[:, k, :], k == 0, k == KC - 1,
                              reads=[wk] + hall, writes=[PSK(b)])
                    rl, rk = rr.next()
                    act(rl, ps[:, b, :], AF.Relu, reads=[PSK(b)], writes=[rk])
                    dve(lambda e, rl=rl, c=c: e.tensor_tensor(out=f13[:, c, :], in0=rl, in1=rl, op=ALU.mult),
                        reads=[rk], writes=[kf1[c]])
            for cbk in range(4):
                for kg in range(4):
                    wt, wk = wbr.next()
                    wtb = wt.bitcast(BF16).rearrange("p (k n) -> p k n", k=KC)
                    dma_ld(wtb, wff2_d[l, kg * 2048:(kg + 1) * 2048, :].rearrange("(k p) n -> p k n", p=128)[:, :, cbk * 512:(cbk + 1) * 512],
                           reads=wkeys(l, "wff2", DFF), writes=[wk])
                    for st in range(4):
                        b = 4 + st
                        for k in range(KC):
                            c = kg * 16 + k
                            pe_mm(ps[:, b, :], f13[:, c, st * 128:(st + 1) * 128], wtb[:, k, :], c == 0, c == 63,
                                  reads=[wk, kf1[c]], writes=[PSK(b)])
                for st in range(4):
                    b = 4 + st
                    tm, tk = tr.next()
                    xt, xk = xts[st]
                    dve(lambda e, tm=tm, b=b, cbk=cbk: e.tensor_tensor(out=tm, in0=ps[:, b, :], in1=g2bc[:, cbk * 512:(cbk + 1) * 512], op=ALU.mult),
                        reads=[PSK(b), kg2], writes=[tk])
                    dve(lambda e, tm=tm, xt=xt, cbk=cbk: e.tensor_tensor(out=xt[:, cbk * 512:(cbk + 1) * 512], in0=xt[:, cbk * 512:(cbk + 1) * 512],
                                                                          in1=tm, op=ALU.add),
                        reads=[tk, xk], writes=[xk])
            for st in range(4):
                dma_st(xdst[row0 + t0 + st * 128:row0 + t0 + (st + 1) * 128, :], xts[st][0], reads=[xts[st][1]], writes=[("xout", tt, st)])
        S.barrier()

    issue_casts(0, True)
    if stop >= 1:
        prologue()
    if stop >= 2:
        issue_casts(0, False)
    for l in range(n_layers if stop >= 2 else 0):
        xsrc = xin if l == 0 else x1_d
        xdst = yout if l == n_layers - 1 else x1_d
        row0 = 0
        for s, Sq in enumerate(seqs):
            phase_A(l, s, row0, Sq, xsrc)
            if l == 0 and s == 0 and n_layers > 1:
                issue_casts(1, True)
                issue_casts(1, False)
            if stop >= 3:
                phase_B(l, Sq)
            if stop >= 4:
                phase_C1(l, s, row0, Sq, xsrc)
            if stop >= 5:
                phase_C2(l, s, row0, Sq, xdst)
            row0 += Sq
    S.barrier(include_cast=True)
    S.finalize()

    keys = S.sem_keys()
    sems = {}
    for i, k in enumerate(keys):
        sems[k] = stack.enter_context(nc.semaphore("s%d" % i))
    block = stack.enter_context(nc.Block())

    @block.tensor
    def _(e):
        S.run("pe", e, sems)

    @block.scalar
    def _(e):
        S.run("act", e, sems)

    @block.vector
    def _(e):
        S.run("dve", e, sems)

    @block.gpsimd
    def _(e):
        S.run("pool", e, sems)

    @block.sync
    def _(e):
        S.run("sp", e, sems)

    stack.close()
    return nc


def host_consts(smax):
    bf = ml_dtypes.bfloat16
    cb = np.zeros((128, 8 * 128 + 256), np.float32)
    cb[:, 0:128] = np.eye(128)
    cb[:, 128:256] = 1.0
    cb[0:64, 256:320] = 1.0
    cb[64:128, 320:384] = 1.0
    cb[0:64, 384:512] = 1.0
    cb[64:128, 512:640] = 1.0
    pB = np.zeros((128, 128), np.float32)
    for d in range(16):
        pB[d + 16, d] = 1.0
        pB[d, d + 16] = 1.0
    cb[:, 640:768] = pB
    pC = np.zeros((128, 128), np.float32)
    for blk in range(2):
        for d in range(8):
            pC[blk * 64 + d + 8, blk * 64 + d] = 1.0
            pC[blk * 64 + d, blk * 64 + d + 8] = 1.0
    cb[:, 768:896] = pC
    p = np.arange(128)[:, None]
    c = np.arange(128)[None, :]
    cb[:, 1024:1152] = (p >= c)
    cb[:, 1152:1280] = (p <= c)
    pos = np.arange(smax, dtype=np.float32)

    def rope_tab(dh, nblk):
        rot = dh // 4
        half = rot // 2
        inv = (500000.0 ** (-np.arange(half, dtype=np.float32) / half)).astype(np.float32)
        ang = pos[None, :] * inv[:, None]
        cos, sin = np.cos(ang).astype(np.float32), np.sin(ang).astype(np.float32)
        t = np.zeros((2, 128, smax), np.float32)
        t[0] = 1.0
        for b in range(nblk):
            o = b * dh
            t[0, o:o + half] = cos
            t[0, o + half:o + rot] = cos
            t[1, o:o + half] = -sin
            t[1, o + half:o + rot] = sin
        return t

    return cb.astype(bf), rope_tab(128, 1), rope_tab(64, 2)


def host_inputs(inp, seq_sel, n_layers, smax):
    cbf, ropeB, ropeC = host_consts(smax)
    xs, cs = [], []
    for kind, idx in seq_sel:
        if kind == "p":
            xs.append(np.asarray(inp["x_prompt"][idx]))
            cs.append(np.asarray(inp["c_prompt"][idx]))
        else:
            xs.append(np.asarray(inp["x_sample"][idx]))
            cs.append(np.asarray(inp["c_sample"][idx]))
    xin = np.ascontiguousarray(np.concatenate(xs, axis=0), dtype=np.float32)
    cmat = np.stack(cs, 0)
    cT = np.ascontiguousarray(cmat.reshape(len(cs), KC, 128).transpose(2, 1, 0), dtype=np.float32)
    return xin, cT, cbf, ropeB, ropeC


def shared_inputs(inp, n_layers, nseq):
    L = n_layers
    f = lambda k: np.ascontiguousarray(np.asarray(inp[k])[:L], dtype=np.float32)
    vec = np.zeros((L, 128, NV), np.float32)
    for l in range(L):
        vec[l, :, V_N1G:V_N1G + 16] = np.asarray(inp["norm1_g"][l]).reshape(16, 128).T
        vec[l, :, V_N2G:V_N2G + 16] = np.asarray(inp["norm2_g"][l]).reshape(16, 128).T
        vec[l, :, V_BG:V_BG + 48] = np.asarray(inp["b_gate"][l]).reshape(48, 128).T
        vec[l, :, V_QNA] = np.asarray(inp["qn_a"][l])
        vec[l, :, V_KNA] = np.asarray(inp["kn_a"][l])
        vec[l, :, V_QNB] = np.asarray(inp["qn_b"][l])
        vec[l, :, V_KNB] = np.asarray(inp["kn_b"][l])
        vec[l, :, V_QNC] = np.tile(np.asarray(inp["qn_c"][l]), 2)
        vec[l, :, V_KNC] = np.tile(np.asarray(inp["kn_c"][l]), 2)
        vec[l, :, V_SUB] = np.asarray(inp["subln_c"][l])
        vec[l, :, V_BADA:V_BADA + 96] = np.asarray(inp["b_ada"][l]).reshape(96, 128).T
        for i, k in enumerate(("lam_q1", "lam_k1", "lam_q2", "lam_k2")):
            vec[l, :, V_LAM + 64 * i:V_LAM + 64 * (i + 1)] = np.asarray(inp[k][l])[None, :]
    idx = na_index_table()
    rpb = np.asarray(inp["rpb_a"])[:L].astype(np.float32)
    flat = rpb.reshape(L, 4, 15 * 31)
    rpbx = np.where(idx[None, None] >= 0, flat[:, :, np.clip(idx, 0, None)], np.float32(-1.0e4)).astype(np.float32)
    b_ada3 = np.ascontiguousarray(np.broadcast_to(np.asarray(inp["b_ada"])[:L, None, :], (L, nseq, 6 * D)), dtype=np.float32)
    sh = {"w_ada": f("w_ada"), "b_ada3": b_ada3, "w_in": f("w_in"), "w_gate": f("w_gate"), "w_br_a": f("w_br_a"),
          "w_br_b": f("w_br_b"), "w_br_c": f("w_br_c"), "w_out": f("w_out"), "w_ff1": f("w_ff1"), "w_ff2": f("w_ff2"),
          "vec_fm": vec, "rpbx": np.ascontiguousarray(rpbx),
          "zrow": np.zeros((128, 1024), ml_dtypes.bfloat16),
          "selc": np.ascontiguousarray(np.repeat(np.eye(4, dtype=np.float32), 128, axis=1))}
    return sh


_NC_CACHE = {}


def kernel(**inp):
    n_cores = 8
    seqs = (4096, 2048, 2048)
    key = ("full",)
    if key not in _NC_CACHE:
        _NC_CACHE[key] = build(seqs, 2)
    nc = _NC_CACHE[key]
    sh = shared_inputs(inp, 2, 3)
    in_maps = []
    for c in range(n_cores):
        xin, cT, cbf, ropeB, ropeC = host_inputs(inp, [("p", c), ("s", 2 * c), ("s", 2 * c + 1)], 2, 4096)
        m = dict(sh)
        m.update({"xin": xin, "cT": cT, "cbf": cbf, "ropeB": ropeB, "ropeC": ropeC})
        in_maps.append(m)
    res = run_bass_kernel_spmd(nc, in_maps, core_ids=list(range(n_cores)))
    yp = np.empty((8, 4096, D), np.float32)
    ys = np.empty((16, 2048, D), np.float32)
    for c in range(n_cores):
        y = res.results[c]["yout"]
        yp[c] = y[0:4096]
        ys[2 * c] = y[4096:6144]
        ys[2 * c + 1] = y[6144:8192]
    return (yp, ys)
```

```python
import itertools
import math
import numpy as np
import ml_dtypes
import concourse.bass as bass
import concourse.mybir as mybir
from concourse.bass_utils import run_bass_kernel_spmd

F32 = mybir.dt.float32
BF16 = mybir.dt.bfloat16
AF = mybir.ActivationFunctionType
ALU = mybir.AluOpType
AX = mybir.AxisListType

D = 2048
KC = 16
DFF = 8192
INC = 6144
T = 512
EPS = 1e-6
PADK = 1024
NSEM = {"cast": 40, "sp": 28, "pool": 28}
CSTEP = 512
NCASE_TILES = None

V_N1G, V_N2G, V_BG, V_QNA, V_KNA, V_QNB, V_KNB, V_QNC, V_KNC, V_SUB = 0, 16, 32, 80, 81, 82, 83, 84, 85, 86
V_LAM = 87
V_BADA = V_LAM + 256
NV = V_BADA + 96


class Op:
    __slots__ = ("eng", "fn", "deps", "signal", "semkey", "val", "dma", "psread")

    def __init__(self, eng, fn, dma):
        self.eng = eng
        self.fn = fn
        self.dma = dma
        self.deps = []
        self.signal = False
        self.semkey = None
        self.val = 0
        self.psread = ()


class Sched:
    ENGS = ("pe", "act", "dve", "pool", "sp")

    def __init__(self):
        self.q = {e: [] for e in self.ENGS}
        self.lastw = {}
        self.readers = {}
        self.last_op = {}
        self.dma_rr = {}
        self.dma_last = {}
        self.uid = 0

    def add(self, eng, fn, reads=(), writes=(), dma=False, pool=None):
        o = Op(eng, fn, dma)
        deps = {}
        psr = [r for r in reads if r[0] == "ps"]
        if psr:
            reads = [r for r in reads if r[0] != "ps"]
            writes = list(writes)
            o.psread = set(psr)
            for r in psr:
                w = self.lastw.get(r)
                if w is not None and (not dma) and w.eng == eng and r in w.psread:
                    for d in w.deps:
                        if id(d) not in deps and d.eng != eng:
                            deps[id(d)] = (d, True)
                    self.lastw[r] = o
                    rd = self.readers.get(r)
                    continue
                writes.append(r)
        for r in reads:
            w = self.lastw.get(r)
            if w is not None:
                deps[id(w)] = (w, True)
        for r in writes:
            w = self.lastw.get(r)
            if w is not None:
                deps[id(w)] = (w, True)
            rd = self.readers.get(r)
            if rd:
                for x in rd.values():
                    if id(x) not in deps:
                        deps[id(x)] = (x, False)
        for d, is_w in deps.values():
            if (not dma) and (not d.dma) and d.eng == eng:
                if eng == "pe" or not is_w:
                    continue
            o.deps.append(d)
            d.signal = True
        if dma:
            pool = pool or eng
            n = self.dma_rr.get(pool, 0)
            self.dma_rr[pool] = n + 1
            k = (pool, n % NSEM[pool])
            prev = self.dma_last.get(k)
            if prev is not None:
                o.deps.append(prev)
            self.dma_last[k] = o
            o.semkey = k
            o.val = (prev.val if prev is not None else 0) + 16
            o.signal = True
        self.uid += 1
        for r in reads:
            self.readers.setdefault(r, {})[("d", self.uid) if dma else eng] = o
        for r in writes:
            self.lastw[r] = o
            self.readers[r] = {}
        self.q[eng].append(o)
        if not dma:
            self.last_op[eng] = o
        return o

    def barrier(self, include_cast=False):
        lasts = list(self.last_op.values())
        for k, o in self.dma_last.items():
            if k[0] == "cast" and not include_cast:
                continue
            lasts.append(o)
        for o in lasts:
            o.signal = True
        for e in self.ENGS:
            b = Op(e, None, False)
            b.deps = [o for o in lasts if o.dma or o.eng != e]
            self.q[e].append(b)
        keepw = {k: v for k, v in self.lastw.items() if k[0] == "W"}
        self.lastw = keepw
        self.readers = {}
        self.last_op = {}

    def sem_keys(self):
        keys = ["pe", "act", "dve"]
        for pool, n in self.dma_rr.items():
            for i in range(min(n, NSEM[pool])):
                keys.append((pool, i))
        return keys

    def finalize(self):
        for e in ("pe", "act", "dve"):
            cnt = 0
            for o in self.q[e]:
                if o.dma or o.fn is None:
                    continue
                if o.signal:
                    cnt += 1
                    o.semkey = e
                    o.val = cnt

    def run(self, e, eng, sems):
        seen = {}
        for o in self.q[e]:
            for d in o.deps:
                if seen.get(d.semkey, 0) >= d.val:
                    continue
                eng.wait_ge(sems[d.semkey], d.val)
                seen[d.semkey] = d.val
            if o.fn is None:
                continue
            ins = o.fn(eng)
            if o.signal:
                ins.then_inc(sems[o.semkey], 16 if o.dma else 1)


def _rs(r, R):
    return min(max(r - 4, 0), R - 8)


def na_plan(R):
    plan = []
    for i in range(R // 2):
        lo = _rs(2 * i, R)
        hi = _rs(2 * i + 1, R) + 7
        js = list(range(lo // 2, hi // 2 + 1))
        sig = []
        for j in js:
            s = []
            for kr in range(2):
                for qr in range(2):
                    Rk, Rq = 2 * j + kr, 2 * i + qr
                    valid = _rs(Rq, R) <= Rk <= _rs(Rq, R) + 7
                    s.append((Rk - Rq, valid))
            sig.append(tuple(s))
        plan.append((js, tuple(sig)))
    return plan


def na_cases():
    cases = {}
    off = 0
    for R in (64, 32):
        for js, sig in na_plan(R):
            if sig not in cases:
                cases[sig] = (off, len(js))
                off += len(js)
    return cases, off


def na_index_table():
    cases, nt = na_cases()
    tab = -np.ones((128, nt * 128), dtype=np.int64)
    kc = np.arange(64)
    qc = np.arange(64)
    cs = np.clip(qc - 8, 0, 48)
    colvalid = (kc[:, None] >= cs[None, :]) & (kc[:, None] < cs[None, :] + 16)
    colidx = kc[:, None] - qc[None, :] + 15
    for sig, (off, n) in cases.items():
        for t in range(n):
            s = sig[t]
            for kr in range(2):
                for qr in range(2):
                    dr, valid = s[kr * 2 + qr]
                    if not valid:
                        continue
                    blk = np.where(colvalid, (dr + 7) * 31 + colidx, -1)
                    tab[kr * 64:(kr + 1) * 64, (off + t) * 128 + qr * 64:(off + t) * 128 + (qr + 1) * 64] = blk
    return tab


def build(seqs=(4096, 2048, 2048), n_layers=2, debug=False, stop=99):
    nseq = len(seqs)
    TOK = sum(seqs)
    SMAX = max(seqs)
    SP = SMAX + 2 * PADK
    cases, NT = na_cases()
    nc = bass.Bass("TRN2", target_bir_lowering=False)

    def din(name, shape, dt=F32):
        return nc.dram_tensor(name, list(shape), dt, kind="ExternalInput").ap()

    DBG = ("mod_d", "qk_d", "va_d", "vb_d", "vc_d", "hT_d", "oT_d", "xmid_d")

    def dint(name, shape, dt=F32):
        kind = "ExternalOutput" if (debug and name in DBG) else "Internal"
        return nc.dram_tensor(name, list(shape), dt, kind=kind).ap()

    xin = din("xin", [TOK, D])
    cT = din("cT", [128, KC, nseq])
    w_ada = din("w_ada", [n_layers, D, 6 * D])
    b_ada3 = din("b_ada3", [n_layers, nseq, 6 * D])
    w_in = din("w_in", [n_layers, D, INC])
    w_gate = din("w_gate", [n_layers, D, 3 * D])
    w_bra = din("w_br_a", [n_layers, 512, D])
    w_brb = din("w_br_b", [n_layers, 256, D])
    w_brc = din("w_br_c", [n_layers, 768, D])
    w_out = din("w_out", [n_layers, D, D])
    w_ff1 = din("w_ff1", [n_layers, D, DFF])
    w_ff2 = din("w_ff2", [n_layers, DFF, D])
    vec_fm = din("vec_fm", [n_layers, 128, NV])
    cbf = din("cbf", [128, 8 * 128 + 256], BF16)
    ropeB = din("ropeB", [2, 128, SMAX])
    ropeC = din("ropeC", [2, 128, SMAX])
    rpbx = din("rpbx", [n_layers, 4, 128, NT * 128])
    zrow = din("zrow", [128, 1024], BF16)
    selc = din("selc", [4, 4 * 128])
    yout = nc.dram_tensor("yout", [TOK, D], F32, kind="ExternalOutput").ap()

    mod_d = dint("mod_d", [n_layers, nseq, 6 * D])
    win_d = dint("win_d", [n_layers, D, INC], BF16)
    wgate_d = dint("wgate_d", [n_layers, D, 3 * D], BF16)
    wbr_d = dint("wbr_d", [n_layers, 1536, D], BF16)
    wout_d = dint("wout_d", [n_layers, D, D], BF16)
    wff1_d = dint("wff1_d", [n_layers, D, DFF], BF16)
    wff2_d = dint("wff2_d", [n_layers, DFF, D], BF16)
    qk_d = dint("qk_d", [32, 128, SP], BF16)
    va_d = dint("va_d", [SMAX, 512], BF16)
    vb_d = dint("vb_d", [SMAX + 2 * PADK, 768], BF16)
    vc_d = dint("vc_d", [SMAX, 768], BF16)
    hT_d = dint("hT_d", [KC, 128, SMAX], BF16)
    oT_d = dint("oT_d", [12, 128, SMAX], BF16)
    xmid_d = dint("xmid_d", [SMAX, D])
    x1_d = dint("x1_d", [TOK, D])

    S = Sched()
    ARENA = 52000

    ctx = {}

    def emit_all():
        pass

    import contextlib
    stack = contextlib.ExitStack()
    arena = stack.enter_context(nc.sbuf_tensor("arena", [128, ARENA], F32))
    ps = stack.enter_context(nc.psum_tensor("ps", [128, 8, 512], F32))
    psflat = ps.rearrange("p b n -> p (b n)")

    class Bump:
        def __init__(self):
            self.off = 0
            self.n = 0

        def f32(self, cols):
            a = arena[:, self.off:self.off + cols]
            self.off += cols
            assert self.off <= ARENA, ("SBUF arena overflow", self.off)
            return a

        def bf(self, cols):
            w = (cols + 1) // 2
            a = arena[:, self.off:self.off + w].bitcast(BF16)
            self.off += w
            assert self.off <= ARENA, ("SBUF arena overflow", self.off)
            return a[:, 0:cols]

        def key(self, name):
            self.n += 1
            return (name, self.n)

    B = Bump()

    class Ring:
        def __init__(self, name, n, cols, dt):
            self.bufs = [(B.f32(cols) if dt == F32 else B.bf(cols), B.key(name)) for _ in range(n)]
            self.i = 0

        def next(self):
            r = self.bufs[self.i % len(self.bufs)]
            self.i += 1
            return r

    bank_rr = [0]

    def next_bank(lo=0, hi=8):
        b = lo + bank_rr[0] % (hi - lo)
        bank_rr[0] += 1
        return b

    def PSK(b):
        return ("ps", b)

    cb = B.bf(8 * 128 + 256)
    ident = cb[:, 0:128]
    ones128 = cb[:, 128:256]
    ones64 = cb[:, 256:384]
    ones_lo = cb[:, 384:512]
    ones_hi = cb[:, 512:640]
    permB = cb[:, 640:768]
    permC = cb[:, 768:896]
    band = cb[:, 1024:1280]
    vecs = [B.f32(NV) for _ in range(n_layers)]
    lamt = [B.f32(8) for _ in range(n_layers)]
    modfm = [B.f32(96 * nseq) for _ in range(n_layers)]
    sel = B.f32(4 * 128)
    S.add("sp", lambda e: e.dma_start(out=sel[0:4, :], in_=selc), writes=[("sel", 0)], dma=True)
    KCONST = ("const", 0)
    S.add("sp", lambda e: e.dma_start(out=cb, in_=cbf), writes=[KCONST], dma=True)
    for l in range(n_layers):
        S.add("sp", lambda e, l=l: e.dma_start(out=vecs[l], in_=vec_fm[l]), writes=[("vec", l)], dma=True)
    persist_mark = B.off

    def dma_ld(out, in_, reads, writes, slow=False):
        if slow:
            return S.add("sp", lambda e: e.dma_start(out=out, in_=in_, allow_slow_non_contiguous=True), reads=reads, writes=writes, dma=True)
        return S.add("sp", lambda e: e.dma_start(out=out, in_=in_), reads=reads, writes=writes, dma=True)

    def dma_st(out, in_, reads, writes):
        return S.add("pool", lambda e: e.dma_start(out=out, in_=in_), reads=reads, writes=writes, dma=True)

    def dma_cast(out, in_, reads, writes):
        return S.add("pool", lambda e: e.dma_start(out=out, in_=in_), reads=reads, writes=writes, dma=True, pool="cast")

    def pe_mm(out, lhsT, rhs, start, stop, reads, writes):
        return S.add("pe", lambda e: e.matmul(out, lhsT=lhsT, rhs=rhs, start=start, stop=stop), reads=reads, writes=writes)

    def act(out, in_, func, reads, writes, scale=1.0, bias=0.0, accum_out=None):
        if accum_out is None:
            fn = lambda e: e.activation(out=out, in_=in_, func=func, bias=bias, scale=scale)
        else:
            fn = lambda e: e.activation(out=out, in_=in_, func=func, bias=bias, scale=scale, accum_out=accum_out)
        return S.add("act", fn, reads=reads, writes=writes)

    def dve(fn, reads, writes):
        return S.add("dve", fn, reads=reads, writes=writes)

    def rstd_from(ssb_ap, n, out_f32, reads, wkey, tmp, tmpkey):
        act(tmp, ssb_ap, AF.Ln, reads=reads, writes=[tmpkey], scale=1.0 / n, bias=EPS)
        act(out_f32, tmp, AF.Exp, reads=[tmpkey], writes=[wkey], scale=-0.5)

    def cast_weight(src, dst, name, l, rows):
        step = CSTEP
        for r0 in range(0, rows, step):
            dma_cast(dst[r0:min(r0 + step, rows), :], src[r0:min(r0 + step, rows), :], reads=[], writes=[("W", l, name, r0 // step)])

    def wkeys(l, name, rows):
        return [("W", l, name, i) for i in range((rows + CSTEP - 1) // CSTEP)]

    zt = B.bf(1024)
    S.add("sp", lambda e: e.dma_start(out=zt, in_=zrow), writes=[("zt", 0)], dma=True)
    for j in range(14, 20):
        dma_st(qk_d[j, :, 0:PADK], zt, reads=[("zt", 0)], writes=[("padk", j, 0)])
        dma_st(qk_d[j, :, PADK + SMAX:PADK + SMAX + PADK], zt, reads=[("zt", 0)], writes=[("padk", j, 1)])
    for r0 in range(0, PADK, 128):
        dma_st(vb_d[r0:r0 + 128, :], zt[:, 0:768], reads=[("zt", 0)], writes=[("padv", r0, 0)])
        dma_st(vb_d[PADK + SMAX + r0:PADK + SMAX + r0 + 128, :], zt[:, 0:768], reads=[("zt", 0)], writes=[("padv", r0, 1)])

    persist_mark = B.off

    def issue_casts(l, first):
        if first:
            cast_weight(w_in[l], win_d[l], "win", l, D)
            return
        cast_weight(w_gate[l], wgate_d[l], "wgate", l, D)
        cast_weight(w_bra[l], wbr_d[l, 0:512, :], "wbra", l, 512)
        cast_weight(w_brb[l], wbr_d[l, 512:768, :], "wbrb", l, 256)
        cast_weight(w_brc[l], wbr_d[l, 768:1536, :], "wbrc", l, 768)
        cast_weight(w_out[l], wout_d[l], "wout", l, D)
        cast_weight(w_ff1[l], wff1_d[l], "wff1", l, D)
        cast_weight(w_ff2[l], wff2_d[l], "wff2", l, DFF)

    def prologue():
        B.off = persist_mark
        cs = B.f32(KC * nseq)
        csb = B.bf(KC * nseq)
        dma_ld(cs, cT.rearrange("p k s -> p (k s)"), reads=[], writes=[("cs", 0)])
        act(csb, cs, AF.Silu, reads=[("cs", 0)], writes=[("csb", 0)])
        csb3 = csb.rearrange("p (k s) -> p k s", s=nseq)
        wr = Ring("wada", 8, KC * 512 // 2, F32)
        br = Ring("bada", 2, 512, F32)
        orr = Ring("modo", 2, 512, F32)
        for l in range(n_layers):
            for cbk in range(24):
                wt, wk = wr.next()
                wtb = wt.bitcast(BF16).rearrange("p (k n) -> p k n", k=KC)
                c0 = cbk * 512
                S.add("pool", lambda e, wtb=wtb, l=l, c0=c0: e.dma_start(
                    out=wtb, in_=w_ada[l].rearrange("(k p) n -> p k n", p=128)[:, :, c0:c0 + 512]),
                    reads=[], writes=[wk], dma=True)
                bt, bk = br.next()
                dma_ld(bt[0:nseq, :], b_ada3[l, :, c0:c0 + 512], reads=[], writes=[bk])
                b = next_bank()
                for k in range(KC):
                    pe_mm(ps[0:nseq, b, :], csb3[:, k, :], wtb[:, k, :], k == 0, k == KC - 1,
                          reads=[wk, ("csb", 0)], writes=[PSK(b)])
                ot, ok = orr.next()
                dve(lambda e, ot=ot, b=b, bt=bt: e.tensor_tensor(out=ot[0:nseq, :], in0=ps[0:nseq, b, :], in1=bt[0:nseq, :], op=ALU.add),
                    reads=[PSK(b), bk], writes=[ok])
                dma_st(mod_d[l, :, c0:c0 + 512], ot[0:nseq, :], reads=[ok], writes=[("mod", l, cbk)])
                b2 = next_bank()
                mf3 = modfm[l].rearrange("p (c s) -> p c s", s=nseq)
                for cc in range(4):
                    for k in range(KC):
                        pe_mm(ps[:, b2, cc * nseq:(cc + 1) * nseq], wtb[:, k, cc * 128:(cc + 1) * 128], csb3[:, k, :], k == 0, k == KC - 1,
                              reads=[wk, ("csb", 0)], writes=[PSK(b2)])
                for cc in range(4):
                    c = cbk * 4 + cc
                    dve(lambda e, l=l, c=c, cc=cc, b2=b2, mf3=mf3: e.tensor_scalar(
                        out=mf3[:, c, :], in0=ps[:, b2, cc * nseq:(cc + 1) * nseq], scalar1=vecs[l][:, V_BADA + c:V_BADA + c + 1],
                        scalar2=None, op0=ALU.add), reads=[PSK(b2), ("vec", l)], writes=[("modfm", l, c)])
            v = vecs[l]
            lt = lamt[l]
            tmp = B.f32(128)
            tk = B.key("lamtmp")
            lam_init = 0.8 - 0.6 * math.exp(-0.3 * l)
            dve(lambda e, v=v, tmp=tmp: e.tensor_tensor(out=tmp[:, 0:64], in0=v[:, V_LAM:V_LAM + 64], in1=v[:, V_LAM + 64:V_LAM + 128], op=ALU.mult),
                reads=[("vec", l)], writes=[tk])
            dve(lambda e, v=v, tmp=tmp: e.tensor_tensor(out=tmp[:, 64:128], in0=v[:, V_LAM + 128:V_LAM + 192], in1=v[:, V_LAM + 192:V_LAM + 256], op=ALU.mult),
                reads=[("vec", l), tk], writes=[tk])
            dve(lambda e, lt=lt, tmp=tmp: e.reduce_sum(out=lt[:, 2:3], in_=tmp[:, 0:64], axis=AX.X), reads=[tk], writes=[("lam", l, 2)])
            dve(lambda e, lt=lt, tmp=tmp: e.reduce_sum(out=lt[:, 3:4], in_=tmp[:, 64:128], axis=AX.X), reads=[tk], writes=[("lam", l, 3)])
            act(lt[:, 4:6], lt[:, 2:4], AF.Exp, reads=[("lam", l, 2), ("lam", l, 3)], writes=[("lam", l, 4)])
            dve(lambda e, lt=lt: e.tensor_tensor(out=lt[:, 6:7], in0=lt[:, 5:6], in1=lt[:, 4:5], op=ALU.subtract),
                reads=[("lam", l, 4)], writes=[("lam", l, 6)])
            dve(lambda e, lt=lt, li=lam_init: e.tensor_scalar(out=lt[:, 0:1], in0=lt[:, 6:7], scalar1=-li, scalar2=None, op0=ALU.add),
                reads=[("lam", l, 6)], writes=[("lam", l, 0)])
            dve(lambda e, lt=lt, v=v, li=lam_init: e.tensor_scalar(out=lt[:, 1:2], in0=v[:, V_SUB:V_SUB + 1], scalar1=1.0 - li, scalar2=None, op0=ALU.mult),
                reads=[("vec", l)], writes=[("lam", l, 1)])
        S.barrier()

    def norm_transpose(xt, xk, a_fm, b_fm, abk, hT3, hkeys, st, R):
        import os
        NS = int(os.environ.get("KNS", "99"))
        junk, jk = R["junk"].next()
        ssq, sk = R["ss"].next()
        dve(lambda e: e.memset(ssq, 0.0), reads=[], writes=[sk])
        act(junk, xt, AF.Square, reads=[xk, sk], writes=[jk, sk], accum_out=ssq[:, 0:1])
        if NS < 2: return
        rstd_from(ssq[:, 0:1], D, ssq[:, 2:3], [sk], sk, ssq[:, 1:2], sk)
        if NS < 3: return
        xn, xnk = R["xn"].next()
        dve(lambda e: e.tensor_scalar(out=xn, in0=xt, scalar1=ssq[:, 2:3], scalar2=None, op0=ALU.mult),
            reads=[xk, sk], writes=[xnk])
        if NS < 4: return
        b0 = R["tb"][R["tbi"][0] % 2]
        R["tbi"][0] += 1
        pT = psflat[:, b0 * 512:(b0 + 2) * 512].bitcast(BF16).rearrange("p (k t) -> p k t", k=KC)
        for k in range(KC):
            S.add("pe", lambda e, k=k: e.transpose(pT[:, k, :], xn[:, k * 128:(k + 1) * 128], ident),
                  reads=[xnk, KCONST], writes=[PSK(b0 + k // 8)])
        if NS < 5: return
        for k in range(KC):
            o = hT3[:, k, st * 128:(st + 1) * 128]
            EV = os.environ.get("KEV", "mix")
            if (k % 2 == 0 and EV == "mix") or EV == "act":
                act(o, pT[:, k, :], AF.Identity, reads=[PSK(b0 + k // 8), abk], writes=[hkeys[st][0]],
                    scale=a_fm[:, k:k + 1], bias=b_fm[:, k:k + 1])
            else:
                dve(lambda e, o=o, k=k: e.tensor_scalar(out=o, in0=pT[:, k, :], scalar1=a_fm[:, k:k + 1], scalar2=b_fm[:, k:k + 1],
                                                         op0=ALU.mult, op1=ALU.add),
                    reads=[PSK(b0 + k // 8), abk], writes=[hkeys[st][1]])

    def load_mod_fm(l, s, idx_shift, idx_scale, gcol, a_fm, b_fm, abk):
        mf3 = modfm[l].rearrange("p (c s) -> p c s", s=nseq)
        v = vecs[l]
        dve(lambda e: e.scalar_tensor_tensor(out=a_fm, in0=mf3[:, idx_scale * 16:(idx_scale + 1) * 16, s], scalar=1.0,
                                             in1=v[:, gcol:gcol + KC], op0=ALU.add, op1=ALU.mult),
            reads=[("vec", l)], writes=[abk])
        dve(lambda e: e.tensor_copy(out=b_fm, in_=mf3[:, idx_shift * 16:(idx_shift + 1) * 16, s]), reads=[], writes=[abk])

    def gate_bcast(l, s, idx, gbc, kg, tmp, tmpk):
        dma_ld(tmp[0:nseq, :], mod_d[l, :, idx * D:(idx + 1) * D], reads=[], writes=[tmpk])
        for cbk in range(4):
            b = next_bank()
            pe_mm(ps[:, b, :], sel[0:nseq, s * 128:(s + 1) * 128], tmp[0:nseq, cbk * 512:(cbk + 1) * 512], True, True,
                  reads=[tmpk, ("sel", 0)], writes=[PSK(b)])
            dve(lambda e, b=b, cbk=cbk: e.tensor_copy(out=gbc[:, cbk * 512:(cbk + 1) * 512], in_=ps[:, b, :]), reads=[PSK(b)], writes=[kg])

    QK_MAP = {}
    for j in range(48):
        if j < 4: QK_MAP[j] = ("A", V_QNA, j)
        elif j < 8: QK_MAP[j] = ("A", V_KNA, j)
        elif j < 12: QK_MAP[j] = None
        elif j < 18: QK_MAP[j] = ("B", V_QNB, 8 + j - 12)
        elif j < 24: QK_MAP[j] = ("B", V_KNB, 14 + j - 18)
        elif j < 30: QK_MAP[j] = None
        elif j < 36: QK_MAP[j] = ("C", V_QNC, 20 + j - 30)
        elif j < 42: QK_MAP[j] = ("C", V_KNC, 26 + j - 36)
        else: QK_MAP[j] = None
    V_SEGS = {2: [(0, 512, va_d, 0)], 6: [(0, 512, vb_d, 0)], 7: [(0, 256, vb_d, 512)],
              10: [(256, 256, vc_d, 0)], 11: [(0, 512, vc_d, 256)]}

    def phase_A(l, s, row0, Sq, xsrc):
        B.off = persist_mark
        if Sq < SMAX:
            for j in range(14, 20):
                dma_st(qk_d[j, :, PADK + Sq:PADK + Sq + PADK], zt, reads=[("zt", 0)], writes=[("padk2", j)])
            for r0 in range(0, PADK, 128):
                dma_st(vb_d[PADK + Sq + r0:PADK + Sq + r0 + 128, :], zt[:, 0:768], reads=[("zt", 0)], writes=[("padv2", r0)])
        a_fm = B.f32(KC); b_fm = B.f32(KC); abk = B.key("ab1")
        load_mod_fm(l, s, 0, 1, V_N1G, a_fm, b_fm, abk)
        R = {"junk": Ring("junk", 1, D, BF16), "ss": Ring("ss", 3, 4, F32), "xn": Ring("xn", 2, D, BF16),
             "tb": (6, 6), "tbi": [0]}
        xr = Ring("xt", 2, D, F32)
        hr = [(B.bf(KC * T), [[B.key("hT"), B.key("hT")] for _ in range(4)]) for _ in range(2)]
        wr = Ring("wbig", 3, KC * 512 // 2, F32)
        rope = Ring("rope", 2, 4 * T, F32)
        sqr = Ring("sq", 3, T, BF16)
        xsr = Ring("xs", 3, T, BF16)
        rsr = Ring("rstd", 2, 2 * T, F32)
        t1r = Ring("t1", 2, T, F32)
        t2r = Ring("t2", 2, T, F32)
        qor = Ring("qo", 3, T, BF16)
        vor = Ring("vo", 3, T, BF16)
        v = vecs[l]
        ntl = Sq // T

        def prep(tt):
            t0 = tt * T
            hT, hkeys = hr[tt % 2]
            hT3 = hT.rearrange("p (k t) -> p k t", k=KC)
            hall = [k for st in range(4) for k in hkeys[st]]
            for st in range(4):
                xt, xk = xr.next()
                dma_ld(xt, xsrc[row0 + t0 + st * 128:row0 + t0 + (st + 1) * 128, :], reads=[], writes=[xk])
                norm_transpose(xt, xk, a_fm, b_fm, abk, hT3, hkeys, st, R)
            dma_st(hT_d.rearrange("k p s -> p k s")[:, :, t0:t0 + T], hT3, reads=hall, writes=[("hTd", tt)])
            rp, rpk = rope.next()
            rp4 = rp.rearrange("p (f t) -> p f t", f=4)
            dma_ld(rp4[:, 0:2, :], ropeB.rearrange("f p s -> p f s")[:, :, t0:t0 + T], reads=[], writes=[rpk])
            dma_ld(rp4[:, 2:4, :], ropeC.rearrange("f p s -> p f s")[:, :, t0:t0 + T], reads=[], writes=[rpk])
            return hT3, hkeys, hall, rp4, rpk

        cur = prep(0)
        for tt in range(ntl):
            t0 = tt * T
            hT3, hkeys, hall, rp4, rpk = cur
            nxt = None
            pending = None
            for blk in range(12):
                wt, wk = wr.next()
                wtb = wt.bitcast(BF16).rearrange("p (k n) -> p k n", k=KC)
                c0 = blk * 512
                dma_ld(wtb, win_d[l].rearrange("(k p) n -> p k n", p=128)[:, :, c0:c0 + 512],
                       reads=wkeys(l, "win", D), writes=[wk])
                for cc in range(4):
                    j = blk * 4 + cc
                    m = QK_MAP[j]
                    if m is None:
                        continue
                    typ, gcol, dj = m
                    b = next_bank(0, 6)
                    for k in range(KC):
                        pe_mm(ps[:, b, :], wtb[:, k, cc * 128:(cc + 1) * 128], hT3[:, k, :], k == 0, k == KC - 1,
                              reads=[wk] + hall, writes=[PSK(b)])
                    zp = ps[:, b, :]
                    g = v[:, gcol:gcol + 1]
                    sq, sqk = sqr.next()
                    act(sq, zp, AF.Square, reads=[PSK(b)], writes=[sqk])
                    xs = xsk = None
                    if typ != "A":
                        xs, xsk = xsr.next()
                        act(xs, zp, AF.Identity, reads=[PSK(b), ("vec", l)], writes=[xsk], scale=g)
                    if pending is not None:
                        pending()

                    def rest(typ=typ, dj=dj, b=b, zp=zp, g=g, sq=sq, sqk=sqk, xs=xs, xsk=xsk, rp4=rp4, rpk=rpk, t0=t0, tt=tt):
                        b2 = next_bank(0, 6)
                        onesm = ones64 if typ == "C" else ones128
                        nn = 64 if typ == "C" else 128
                        pe_mm(ps[:, b2, :], onesm, sq, True, True, reads=[sqk, KCONST], writes=[PSK(b2)])
                        if typ != "A":
                            b3 = next_bank(0, 6)
                            pm = permB if typ == "B" else permC
                            pe_mm(ps[:, b3, :], pm, xs, True, True, reads=[xsk, KCONST], writes=[PSK(b3)])
                        rs, rsk = rsr.next()
                        rstd_from(ps[:, b2, :], nn, rs[:, T:2 * T], [PSK(b2)], rsk, rs[:, 0:T], rsk)
                        rstd = rs[:, T:2 * T]
                        qo, qok = qor.next()
                        if typ == "A":
                            dve(lambda e: e.scalar_tensor_tensor(out=qo, in0=zp, scalar=g, in1=rstd, op0=ALU.mult, op1=ALU.mult),
                                reads=[PSK(b), rsk, ("vec", l)], writes=[qok])
                        else:
                            cosT = rp4[:, 0, :] if typ == "B" else rp4[:, 2, :]
                            sinT = rp4[:, 1, :] if typ == "B" else rp4[:, 3, :]
                            t1, t1k = t1r.next()
                            t2, t2k = t2r.next()
                            dve(lambda e: e.scalar_tensor_tensor(out=t1, in0=zp, scalar=g, in1=cosT, op0=ALU.mult, op1=ALU.mult),
                                reads=[PSK(b), rpk, ("vec", l)], writes=[t1k])
                            dve(lambda e: e.tensor_tensor(out=t2, in0=ps[:, b3, :], in1=sinT, op=ALU.mult),
                                reads=[PSK(b3), rpk], writes=[t2k])
                            dve(lambda e: e.tensor_tensor(out=t1, in0=t1, in1=t2, op=ALU.add), reads=[t1k, t2k], writes=[t1k])
                            dve(lambda e: e.tensor_tensor(out=qo, in0=t1, in1=rstd, op=ALU.mult), reads=[t1k, rsk], writes=[qok])
                        dma_st(qk_d[dj, :, PADK + t0:PADK + t0 + T], qo, reads=[qok], writes=[("qkd", dj, tt)])
                    pending = rest
                for (cin, ncol, dst, dcol) in V_SEGS.get(blk, []):
                    for st in range(4):
                        b = next_bank(0, 6)
                        for k in range(KC):
                            pe_mm(ps[:, b, 0:ncol], hT3[:, k, st * 128:(st + 1) * 128], wtb[:, k, cin:cin + ncol], k == 0, k == KC - 1,
                                  reads=[wk] + hkeys[st], writes=[PSK(b)])
                        if pending is not None:
                            pending()
                            pending = None
                        vo, vok = vor.next()
                        act(vo[:, 0:ncol], ps[:, b, 0:ncol], AF.Identity, reads=[PSK(b)], writes=[vok])
                        roff = PADK if dst is vb_d else 0
                        dma_st(dst[roff + t0 + st * 128:roff + t0 + (st + 1) * 128, dcol:dcol + ncol], vo[:, 0:ncol],
                               reads=[vok], writes=[("vd", blk, dcol, tt, st)])
                if blk == 6 and tt + 1 < ntl:
                    nxt = prep(tt + 1)
            if pending is not None:
                pending()
                pending = None
            cur = nxt
        S.barrier()

    def phase_B(l, Sq):
        B.off = persist_mark
        v = vecs[l]
        lt = lamt[l]
        sc128 = 128.0 ** -0.5
        sc64 = 64.0 ** -0.5
        mark = B.off
        R_rows = Sq // 64
        plan = na_plan(R_rows)
        asets = []
        for _ in range(2):
            QT = B.bf(Sq); KT = B.bf(Sq); Vt = B.bf(Sq); OT = B.bf(Sq)
            tab = B.f32(NT * 128)
            asets.append((QT, KT, Vt, OT, tab, tuple(B.key(n) for n in ("aQ", "aK", "aV", "aO", "aT"))))
        sbr = Ring("asb", 3, 640, F32)
        ptr = Ring("apt", 3, 640, BF16)
        rzr = Ring("arz", 2, 512, F32)
        npairs = R_rows // 2

        def a_load(h):
            QT, KT, Vt, OT, tab, (kQ, kK, kV, kO, kT) = asets[h % 2]
            Vt3 = Vt.rearrange("p (j c) -> p j c", c=128)
            dma_ld(QT, qk_d[h, :, PADK:PADK + Sq], reads=[], writes=[kQ])
            dma_ld(KT, qk_d[4 + h, :, PADK:PADK + Sq], reads=[], writes=[kK])
            dma_ld(Vt3, va_d[0:Sq, h * 128:(h + 1) * 128].rearrange("(j p) c -> p j c", p=128), reads=[], writes=[kV])
            dma_ld(tab, rpbx[l, h], reads=[], writes=[kT])

        a_load(0)
        for h in range(4):
            if h + 1 < 4:
                a_load(h + 1)
            QT, KT, Vt, OT, tab, (kQ, kK, kV, kO, kT) = asets[h % 2]
            Vt3 = Vt.rearrange("p (j c) -> p j c", c=128)
            stash = {}
            for nn_ in range(npairs + 1):
                if nn_ < npairs:
                    i = nn_
                    js, sig = plan[i]
                    off, n = cases[sig]
                    bS = 2 * (i % 2)
                    for jj, j in enumerate(js):
                        pe_mm(ps[:, bS + jj // 4, (jj % 4) * 128:(jj % 4 + 1) * 128], KT[:, j * 128:(j + 1) * 128], QT[:, i * 128:(i + 1) * 128],
                              True, True, reads=[kK, kQ], writes=[PSK(bS + jj // 4)])
                    sb, sbk = sbr.next()
                    pt, ptk = ptr.next()
                    dve(lambda e, sb=sb, bS=bS, n=n, off=off, tab=tab: e.scalar_tensor_tensor(
                        out=sb[:, 0:n * 128], in0=psflat[:, bS * 512:bS * 512 + n * 128], scalar=sc128,
                        in1=tab[:, off * 128:(off + n) * 128], op0=ALU.mult, op1=ALU.add),
                        reads=[PSK(bS), PSK(bS + 1), kT], writes=[sbk])
                    act(pt[:, 0:n * 128], sb[:, 0:n * 128], AF.Exp, reads=[sbk], writes=[ptk])
                    stash[i] = (pt, ptk, js, n)
                m = nn_ - 1
                if m >= 0:
                    pt, ptk, js, n = stash.pop(m)
                    i0_ = (m // 4) * 4
                    bO = 4 + (m // 4) % 2
                    bZ = 6 + (m // 4) % 2
                    sl = m - i0_
                    for jj, j in enumerate(js):
                        pe_mm(ps[:, bO, sl * 128:(sl + 1) * 128], Vt3[:, j, :], pt[:, jj * 128:(jj + 1) * 128], jj == 0, jj == n - 1,
                              reads=[kV, ptk], writes=[PSK(bO)])
                    for jj, j in enumerate(js):
                        pe_mm(ps[:, bZ, sl * 128:(sl + 1) * 128], ones128, pt[:, jj * 128:(jj + 1) * 128], jj == 0, jj == n - 1,
                              reads=[KCONST, ptk], writes=[PSK(bZ)])
                    if sl == 3:
                        rz, rzk = rzr.next()
                        dve(lambda e, rz=rz, bZ=bZ: e.reciprocal(out=rz, in_=ps[:, bZ, :]), reads=[PSK(bZ)], writes=[rzk])
                        dve(lambda e, rz=rz, bO=bO, i0_=i0_, OT=OT: e.tensor_tensor(out=OT[:, i0_ * 128:(i0_ + 4) * 128], in0=ps[:, bO, :], in1=rz, op=ALU.mult),
                            reads=[PSK(bO), rzk], writes=[kO])
            dma_st(oT_d[h, :, 0:Sq], OT, reads=[kO], writes=[("oTd", h)])
        S.barrier()
        B.off = mark
        bsets = []
        for _ in range(2):
            QT = B.bf(Sq); KT = B.bf(Sq + 2 * PADK)
            bsets.append((QT, KT, B.key("bQ"), B.key("bK")))
        OT = B.bf(Sq)
        QTr = B.bf(Sq); KTr = B.bf(Sq + 2 * PADK)
        accO = B.f32(Sq); accZ = B.f32(Sq)
        kO, kaO, kaZ = (B.key(n) for n in ("bO", "baO", "baZ"))
        kQr, kKr = B.key("bQr"), B.key("bKr")
        vtr = Ring("bvt", 3, 33 * 128, BF16)
        er = Ring("be", 4, 256, BF16)
        ptr = Ring("bpt", 4, 256, BF16)
        combos = [(hg, g, d) for hg in range(2) for g, d in enumerate((1, 4, 16))]

        def b_load(ci):
            hg, g, d = combos[ci]
            cj = g * 2 + hg
            QT, KT, kQ, kK = bsets[ci % 2]
            dma_ld(QT, qk_d[8 + cj, :, PADK:PADK + Sq], reads=[], writes=[kQ])
            dma_ld(KT, qk_d[14 + cj, :, 0:Sq + 2 * PADK], reads=[], writes=[kK])

        b_load(0)
        gid = [0]
        PREB = 2
        for ci, (hg, g, d) in enumerate(combos):
            if ci + 1 < len(combos):
                b_load(ci + 1)
            cj = g * 2 + hg
            QT, KT, kQ, kK = bsets[ci % 2]
            L = Sq // d
            nq = L // 128
            if d == 1:
                Qv = QT.rearrange("p (d m) -> p d m", d=1)
                Kv = KT[:, PADK - 64:PADK - 64 + L + 128].rearrange("p (d m) -> p d m", d=1)
                kQu, kKu = kQ, kK
            else:
                Qv = QTr.rearrange("p (d m) -> p d m", d=d)
                Kv = KTr[:, 0:(L + 128) * d].rearrange("p (d m) -> p d m", d=d)
                dve(lambda e, Qv=Qv, QT=QT, d=d: e.tensor_copy(out=Qv, in_=QT.rearrange("p (m d) -> p d m", d=d)),
                    reads=[kQ], writes=[kQr])
                act(Kv, KT[:, PADK - 64 * d:PADK - 64 * d + (L + 128) * d].rearrange("p (m d) -> p d m", d=d), AF.Identity,
                    reads=[kK], writes=[kKr])
                kQu, kKu = kQr, kKr
            items = [(r, i) for r in range(d) for i in range(nq)]
            vts = {}
            stash = {}
            grp = {}
            for nn_ in range(len(items) + PREB):
                if nn_ < len(items):
                    r, i = items[nn_]
                    if i == 0:
                        vt, vtk = vtr.next()
                        vt3 = vt[:, 0:(nq + 1) * 128].rearrange("p (j c) -> p j c", c=128)
                        base = PADK - 64 * d + r
                        nrow = (nq + 1) * 128
                        vsrc = vb_d[base:base + (nrow - 1) * d + 1:d, cj * 128:(cj + 1) * 128].rearrange("(j p) c -> p j c", p=128)
                        dma_ld(vt3, vsrc, reads=[], writes=[vtk])
                        vts[r] = (vt3, vtk)
                    bS = next_bank(0, 4)
                    rhs = Qv[:, r, 128 * i:128 * (i + 1)]
                    for t in range(2):
                        pe_mm(ps[:, bS, t * 128:(t + 1) * 128], Kv[:, r, 128 * (i + t):128 * (i + t + 1)], rhs, True, True,
                              reads=[kKu, kQu], writes=[PSK(bS)])
                    ee, ek = er.next()
                    pt, ptk = ptr.next()
                    act(ee, ps[:, bS, 0:256], AF.Exp, reads=[PSK(bS)], writes=[ek], scale=sc128)
                    dve(lambda e, pt=pt, ee=ee: e.tensor_tensor(out=pt, in0=ee, in1=band, op=ALU.mult),
                        reads=[ek, KCONST], writes=[ptk])
                    stash[nn_] = (pt, ptk)
                m = nn_ - PREB
                if m >= 0:
                    r, i = items[m]
                    pt, ptk = stash.pop(m)
                    vt3, vtk = vts[r]
                    i0_ = (i // 4) * 4
                    w = min(4, nq - i0_)
                    sl = i - i0_
                    if sl == 0:
                        grp[(r, i0_)] = gid[0]
                        gid[0] += 1
                    gq = grp[(r, i0_)]
                    bO = 4 + gq % 2
                    bZ = 6 + gq % 2
                    for t in range(2):
                        pe_mm(ps[:, bO, sl * 128:(sl + 1) * 128], vt3[:, i + t, :], pt[:, t * 128:(t + 1) * 128], t == 0, t == 1,
                              reads=[vtk, ptk], writes=[PSK(bO)])
                    for t in range(2):
                        jp = i + t
                        om = ones_hi if jp == 0 else (ones_lo if jp == nq else ones128)
                        pe_mm(ps[:, bZ, sl * 128:(sl + 1) * 128], om, pt[:, t * 128:(t + 1) * 128], t == 0, t == 1,
                              reads=[KCONST, ptk], writes=[PSK(bZ)])
                    if sl == w - 1:
                        p0 = r + 128 * i0_ * d
                        dO = accO[:, p0:p0 + (w * 128 - 1) * d + 1:d]
                        dZ = accZ[:, p0:p0 + (w * 128 - 1) * d + 1:d]
                        if g == 0:
                            act(dO, ps[:, bO, 0:w * 128], AF.Identity, reads=[PSK(bO)], writes=[kaO])
                            act(dZ, ps[:, bZ, 0:w * 128], AF.Identity, reads=[PSK(bZ)], writes=[kaZ])
                        else:
                            dve(lambda e, dO=dO, bO=bO, w=w: e.tensor_tensor(out=dO, in0=dO, in1=ps[:, bO, 0:w * 128], op=ALU.add),
                                reads=[PSK(bO), kaO], writes=[kaO])
                            dve(lambda e, dZ=dZ, bZ=bZ, w=w: e.tensor_tensor(out=dZ, in0=dZ, in1=ps[:, bZ, 0:w * 128], op=ALU.add),
                                reads=[PSK(bZ), kaZ], writes=[kaZ])
            if g == 2:
                dve(lambda e, accZ=accZ: e.reciprocal(out=accZ, in_=accZ), reads=[kaZ], writes=[kaZ])
                dve(lambda e, OT=OT, accO=accO, accZ=accZ: e.tensor_tensor(out=OT, in0=accO, in1=accZ, op=ALU.mult), reads=[kaO, kaZ], writes=[kO])
                dma_st(oT_d[4 + hg, :, 0:Sq], OT, reads=[kO], writes=[("oTd", 4 + hg)])
        S.barrier()
        B.off = mark
        csets = []
        for _ in range(2):
            Qx = B.bf(2 * Sq); KT = B.bf(Sq); Vt = B.bf(Sq); OT = B.bf(Sq)
            ks = tuple(B.key(n) for n in ("cQ", "cK", "cV", "cO"))
            dve(lambda e, Qx=Qx: e.memset(Qx, 0.0), reads=[], writes=[ks[0]])
            csets.append((Qx, KT, Vt, OT, ks))
        ptr = Ring("cpt", 5, 512, BF16)
        rzr = Ring("crz", 2, 512, F32)
        tr = Ring("ct", 2, 512, F32)
        ar = Ring("ca", 3, 256, F32)
        sqr = Ring("csq", 3, 256, BF16)
        rsr = Ring("crs", 2, 512, F32)
        PRE = 2
        nkt = Sq // 128
        nqb = Sq // 256

        def c_load(hc):
            Qx, KT, Vt, OT, (kQ, kK, kV, kO) = csets[hc % 2]
            Qx3 = Qx.rearrange("p (c s) -> p c s", c=2)
            Vt3 = Vt.rearrange("p (j c) -> p j c", c=128)
            dma_ld(Qx3[0:64, 0, :], qk_d[20 + hc, 0:64, PADK:PADK + Sq], reads=[], writes=[kQ])
            dma_ld(Qx3[64:128, 1, :], qk_d[20 + hc, 64:128, PADK:PADK + Sq], reads=[], writes=[kQ])
            dma_ld(KT, qk_d[26 + hc, :, PADK:PADK + Sq], reads=[], writes=[kK])
            dma_ld(Vt3, vc_d[0:Sq, hc * 128:(hc + 1) * 128].rearrange("(j p) c -> p j c", p=128), reads=[], writes=[kV])

        c_load(0)
        for hc in range(6):
            if hc + 1 < 6:
                c_load(hc + 1)
            Qx, KT, Vt, OT, (kQ, kK, kV, kO) = csets[hc % 2]
            Qx3 = Qx.rearrange("p (c s) -> p c s", c=2)
            Vt3 = Vt.rearrange("p (j c) -> p j c", c=128)
            items = [(qb, kt) for qb in range(nqb) for kt in range(nkt)]
            stash = {}
            pend = []
            for n in range(len(items) + PRE):
                if n < len(items):
                    qb, kt = items[n]
                    bS = next_bank(0, 3)
                    pe_mm(ps[:, bS, :].rearrange("p (c q) -> p c q", c=2), KT[:, kt * 128:(kt + 1) * 128], Qx3[:, :, qb * 256:(qb + 1) * 256],
                          True, True, reads=[kK, kQ], writes=[PSK(bS)])
                    pt, ptk = ptr.next()
                    act(pt, ps[:, bS, :], AF.Exp, reads=[PSK(bS)], writes=[ptk], scale=sc64)
                    stash[n] = (pt, ptk)
                m = n - PRE
                if m >= 0:
                    qb, kt = items[m]
                    pt, ptk = stash.pop(m)
                    bO = 3 + qb % 2
                    bZ = 5 + qb % 2
                    pe_mm(ps[:, bO, :], Vt3[:, kt, :], pt, kt == 0, kt == nkt - 1, reads=[kV, ptk], writes=[PSK(bO)])
                    pe_mm(ps[:, bZ, :], ones128, pt, kt == 0, kt == nkt - 1, reads=[KCONST, ptk], writes=[PSK(bZ)])
                    if kt == nkt - 1:
                        rz, rzk = rzr.next()
                        tt_, ttk = tr.next()
                        aa, aak = ar.next()
                        sq, sqk = sqr.next()
                        dve(lambda e, rz=rz, bZ=bZ: e.reciprocal(out=rz, in_=ps[:, bZ, :]), reads=[PSK(bZ)], writes=[rzk])
                        dve(lambda e, tt_=tt_, rz=rz, bO=bO: e.tensor_tensor(out=tt_, in0=ps[:, bO, :], in1=rz, op=ALU.mult),
                            reads=[PSK(bO), rzk], writes=[ttk])
                        dve(lambda e, aa=aa, tt_=tt_: e.scalar_tensor_tensor(out=aa, in0=tt_[:, 256:512], scalar=lt[:, 0:1], in1=tt_[:, 0:256],
                                                                              op0=ALU.mult, op1=ALU.add),
                            reads=[ttk, ("lam", l, 0)], writes=[aak])
                        act(sq, aa, AF.Square, reads=[aak], writes=[sqk])

                        def part2(aa=aa, aak=aak, sq=sq, sqk=sqk, qb=qb, OT=OT, kO=kO):
                            rs, rsk = rsr.next()
                            pe_mm(ps[:, 7, 0:256], ones128, sq, True, True, reads=[KCONST, sqk], writes=[PSK(7)])
                            rstd_from(ps[:, 7, 0:256], 128, rs[:, 256:512], [PSK(7)], rsk, rs[:, 0:256], rsk)
                            dve(lambda e: e.scalar_tensor_tensor(out=OT[:, qb * 256:(qb + 1) * 256], in0=aa, scalar=lt[:, 1:2],
                                                                 in1=rs[:, 256:512], op0=ALU.mult, op1=ALU.mult),
                                reads=[aak, rsk, ("lam", l, 1)], writes=[kO])
                        pend.append((n + 6, part2))
                while pend and pend[0][0] <= n:
                    pend.pop(0)[1]()
            for _, fn in pend:
                fn()
            dma_st(oT_d[6 + hc, :, 0:Sq], OT, reads=[kO], writes=[("oTd", 6 + hc)])
        S.barrier()

    def phase_C1(l, s, row0, Sq, xsrc):
        B.off = persist_mark
        v = vecs[l]
        g1bc = B.f32(D); kg1 = B.key("g1bc")
        GATE1 = True
        OTt = B.bf(12 * T); kOT = B.key("OTt")
        OT3 = OTt.rearrange("p (c t) -> p c t", c=12)
        hT = B.bf(KC * T); khT = B.key("hTt")
        hT3 = hT.rearrange("p (k t) -> p k t", k=KC)
        mT = B.bf(KC * T)
        mT3 = mT.rearrange("p (k t) -> p k t", k=KC)
        kmT = [B.key("mT") for _ in range(KC)]
        xts = [(B.f32(D), B.key("xt")) for _ in range(4)]
        gate_bcast(l, s, 2, g1bc, kg1, xts[0][0], xts[0][1])
        wsr = Ring("wsm", 3, (48 + 12) * 128 // 2, F32)
        wbr = Ring("wbig", 2, KC * 512 // 2, F32)
        sgr = Ring("sg", 4, T, F32)
        mr = Ring("m", 3, T, F32)
        tr = Ring("tmp", 2, T, F32)
        for tt in range(Sq // T):
            t0 = tt * T
            dma_ld(OT3, oT_d.rearrange("c p s -> p c s")[:, :, t0:t0 + T], reads=[], writes=[kOT])
            dma_ld(hT3, hT_d.rearrange("k p s -> p k s")[:, :, t0:t0 + T], reads=[], writes=[khT])
            for st in range(4):
                dma_ld(xts[st][0], xsrc[row0 + t0 + st * 128:row0 + t0 + (st + 1) * 128, :], reads=[], writes=[xts[st][1]])
            for dc in range(KC):
                wt, wk = wsr.next()
                wtb = wt.bitcast(BF16)
                wg = wtb[:, 0:48 * 128].rearrange("p (g k n) -> p g k n", g=3, k=KC)
                wb = wtb[:, 48 * 128:60 * 128].rearrange("p (k n) -> p k n", k=12)
                dma_ld(wg, wgate_d[l].rearrange("(k p) (g n) -> p g k n", p=128, g=3)[:, :, :, dc * 128:(dc + 1) * 128],
                       reads=wkeys(l, "wgate", D), writes=[wk])
                dma_ld(wb, wbr_d[l].rearrange("(k p) n -> p k n", p=128)[:, :, dc * 128:(dc + 1) * 128],
                       reads=wkeys(l, "wbra", 512) + wkeys(l, "wbrb", 256) + wkeys(l, "wbrc", 768), writes=[wk])
                kcs = [(0, 4), (4, 6), (6, 12)]
                mcur = None
                for gi in range(3):
                    bg = next_bank()
                    for k in range(KC):
                        pe_mm(ps[:, bg, :], wg[:, gi, k, :], hT3[:, k, :], k == 0, k == KC - 1, reads=[wk, khT], writes=[PSK(bg)])
                    by = next_bank()
                    k0, k1 = kcs[gi]
                    for k in range(k0, k1):
                        pe_mm(ps[:, by, :], wb[:, k, :], OT3[:, k, :], k == k0, k == k1 - 1, reads=[wk, kOT], writes=[PSK(by)])
                    sg, sgk = sgr.next()
                    act(sg, ps[:, bg, :], AF.Sigmoid, reads=[PSK(bg), ("vec", l)], writes=[sgk],
                        bias=v[:, V_BG + gi * 16 + dc:V_BG + gi * 16 + dc + 1])
                    mm_, mk = mr.next()
                    dve(lambda e, mm_=mm_, sg=sg, by=by: e.tensor_tensor(out=mm_, in0=ps[:, by, :], in1=sg, op=ALU.mult),
                        reads=[PSK(by), sgk], writes=[mk])
                    if gi == 0:
                        mcur = (mm_, mk)
                    elif gi == 1:
                        dve(lambda e, a=mcur[0], b_=mm_: e.tensor_tensor(out=a, in0=a, in1=b_, op=ALU.add),
                            reads=[mcur[1], mk], writes=[mcur[1]])
                    else:
                        dve(lambda e, a=mcur[0], b_=mm_, dc=dc: e.tensor_tensor(out=mT3[:, dc, :], in0=a, in1=b_, op=ALU.add),
                            reads=[mcur[1], mk], writes=[kmT[dc]])
            for cbk in range(4):
                wt, wk = wbr.next()
                wtb = wt.bitcast(BF16).rearrange("p (k n) -> p k n", k=KC)
                dma_ld(wtb, wout_d[l].rearrange("(k p) n -> p k n", p=128)[:, :, cbk * 512:(cbk + 1) * 512],
                       reads=wkeys(l, "wout", D), writes=[wk])
                for st in range(4):
                    b = next_bank()
                    for k in range(KC):
                        pe_mm(ps[:, b, :], mT3[:, k, st * 128:(st + 1) * 128], wtb[:, k, :], k == 0, k == KC - 1,
                              reads=[wk, kmT[k]], writes=[PSK(b)])
                    tm, tk = tr.next()
                    xt, xk = xts[st]
                    dve(lambda e, tm=tm, b=b, cbk=cbk: e.tensor_tensor(out=tm, in0=ps[:, b, :], in1=g1bc[:, cbk * 512:(cbk + 1) * 512], op=ALU.mult),
                        reads=[PSK(b), kg1], writes=[tk])
                    dve(lambda e, tm=tm, xt=xt, cbk=cbk: e.tensor_tensor(out=xt[:, cbk * 512:(cbk + 1) * 512], in0=xt[:, cbk * 512:(cbk + 1) * 512],
                                                                          in1=tm, op=ALU.add),
                        reads=[tk, xk], writes=[xk])
            for st in range(4):
                dma_st(xmid_d[t0 + st * 128:t0 + (st + 1) * 128, :], xts[st][0], reads=[xts[st][1]], writes=[("xmid", tt, st)])
        S.barrier()

    def phase_C2(l, s, row0, Sq, xdst):
        B.off = persist_mark
        a_fm = B.f32(KC); b_fm = B.f32(KC); abk = B.key("ab2")
        load_mod_fm(l, s, 3, 4, V_N2G, a_fm, b_fm, abk)
        g2bc = B.f32(D); kg2 = B.key("g2bc")
        GATE2 = True
        R = {"junk": Ring("junk", 1, D, BF16), "ss": Ring("ss", 3, 4, F32), "xn": Ring("xn", 2, D, BF16),
             "tb": (0, 2), "tbi": [0]}
        xts = [(B.f32(D), B.key("xt")) for _ in range(4)]
        gate_bcast(l, s, 5, g2bc, kg2, xts[0][0], xts[0][1])
        hT = B.bf(KC * T)
        hT3 = hT.rearrange("p (k t) -> p k t", k=KC)
        hkeys = [[B.key("h2T"), B.key("h2T")] for _ in range(4)]
        hall = [k for st in range(4) for k in hkeys[st]]
        f1 = B.bf(64 * T)
        f13 = f1.rearrange("p (c t) -> p c t", c=64)
        kf1 = [B.key("f1") for _ in range(64)]
        wbr = Ring("wbig", 3, KC * 512 // 2, F32)
        rr = Ring("relu", 3, T, BF16)
        tr = Ring("tmp", 2, T, F32)
        for tt in range(Sq // T):
            t0 = tt * T
            for st in range(4):
                xt, xk = xts[st]
                dma_ld(xt, xmid_d[t0 + st * 128:t0 + (st + 1) * 128, :], reads=[], writes=[xk])
                norm_transpose(xt, xk, a_fm, b_fm, abk, hT3, hkeys, st, R)
            for blk in range(16):
                wt, wk = wbr.next()
                wtb = wt.bitcast(BF16).rearrange("p (k n) -> p k n", k=KC)
                dma_ld(wtb, wff1_d[l].rearrange("(k p) n -> p k n", p=128)[:, :, blk * 512:(blk + 1) * 512],
                       reads=wkeys(l, "wff1", D), writes=[wk])
                for cc in range(4):
                    c = blk * 4 + cc
                    b = next_bank(0, 4)
                    for k in range(KC):
                        pe_mm(ps[:, b, :], wtb[:, k, cc * 128:(cc + 1) * 128], hT3[:, k, :], k == 0, k == KC - 1,
                              reads=[wk] + hall, writes=[PSK(b)])
                    rl, rk = rr.next()
                    act(rl, ps[:, b, :], AF.Relu, reads=[PSK(b)], writes=[rk])
                    dve(lambda e, rl=rl, c=c: e.tensor_tensor(out=f13[:, c, :], in0=rl, in1=rl, op=ALU.mult),
                        reads=[rk], writes=[kf1[c]])
            for cbk in range(4):
                for kg in range(4):
                    wt, wk = wbr.next()
                    wtb = wt.bitcast(BF16).rearrange("p (k n) -> p k n", k=KC)
                    dma_ld(wtb, wff2_d[l, kg * 2048:(kg + 1) * 2048, :].rearrange("(k p) n -> p k n", p=128)[:, :, cbk * 512:(cbk + 1) * 512],
                           reads=wkeys(l, "wff2", DFF), writes=[wk])
                    for st in range(4):
                        b = 4 + st
                        for k in range(KC):
                            c = kg * 16 + k
                            pe_mm(ps[:, b, :], f13[:, c, st * 128:(st + 1) * 128], wtb[:, k, :], c == 0, c == 63,
                                  reads=[wk, kf1[c]], writes=[PSK(b)])
                for st in range(4):
                    b = 4 + st
                    tm, tk = tr.next()
                    xt, xk = xts[st]
                    dve(lambda e, tm=tm, b=b, cbk=cbk: e.tensor_tensor(out=tm, in0=ps[:, b, :], in1=g2bc[:, cbk * 512:(cbk + 1) * 512], op=ALU.mult),
                        reads=[PSK(b), kg2], writes=[tk])
                    dve(lambda e, tm=tm, xt=xt, cbk=cbk: e.tensor_tensor(out=xt[:, cbk * 512:(cbk + 1) * 512], in0=xt[:, cbk * 512:(cbk + 1) * 512],
                                                                          in1=tm, op=ALU.add),
                        reads=[tk, xk], writes=[xk])
            for st in range(4):
                dma_st(xdst[row0 + t0 + st * 128:row0 + t0 + (st + 1) * 128, :], xts[st][0], reads=[xts[st][1]], writes=[("xout", tt, st)])
        S.barrier()

    issue_casts(0, True)
    if stop >= 1:
        prologue()
    if stop >= 2:
        issue_casts(0, False)
    for l in range(n_layers if stop >= 2 else 0):
        xsrc = xin if l == 0 else x1_d
        xdst = yout if l == n_layers - 1 else x1_d
        row0 = 0
        for s, Sq in enumerate(seqs):
            phase_A(l, s, row0, Sq, xsrc)
            if l == 0 and s == 0 and n_layers > 1:
                issue_casts(1, True)
                issue_casts(1, False)
            if stop >= 3:
                phase_B(l, Sq)
            if stop >= 4:
                phase_C1(l, s, row0, Sq, xsrc)
            if stop >= 5:
                phase_C2(l, s, row0, Sq, xdst)
            row0 += Sq
    S.barrier(include_cast=True)
    S.finalize()

    keys = S.sem_keys()
    sems = {}
    for i, k in enumerate(keys):
        sems[k] = stack.enter_context(nc.semaphore("s%d" % i))
    block = stack.enter_context(nc.Block())

    @block.tensor
    def _(e):
        S.run("pe", e, sems)

    @block.scalar
    def _(e):
        S.run("act", e, sems)

    @block.vector
    def _(e):
        S.run("dve", e, sems)

    @block.gpsimd
    def _(e):
        S.run("pool", e, sems)

    @block.sync
    def _(e):
        S.run("sp", e, sems)

    stack.close()
    return nc


def host_consts(smax):
    bf = ml_dtypes.bfloat16
    cb = np.zeros((128, 8 * 128 + 256), np.float32)
    cb[:, 0:128] = np.eye(128)
    cb[:, 128:256] = 1.0
    cb[0:64, 256:320] = 1.0
    cb[64:128, 320:384] = 1.0
    cb[0:64, 384:512] = 1.0
    cb[64:128, 512:640] = 1.0
    pB = np.zeros((128, 128), np.float32)
    for d in range(16):
        pB[d + 16, d] = 1.0
        pB[d, d + 16] = 1.0
    cb[:, 640:768] = pB
    pC = np.zeros((128, 128), np.float32)
    for blk in range(2):
        for d in range(8):
            pC[blk * 64 + d + 8, blk * 64 + d] = 1.0
            pC[blk * 64 + d, blk * 64 + d + 8] = 1.0
    cb[:, 768:896] = pC
    p = np.arange(128)[:, None]
    c = np.arange(128)[None, :]
    cb[:, 1024:1152] = (p >= c)
    cb[:, 1152:1280] = (p <= c)
    pos = np.arange(smax, dtype=np.float32)

    def rope_tab(dh, nblk):
        rot = dh // 4
        half = rot // 2
        inv = (500000.0 ** (-np.arange(half, dtype=np.float32) / half)).astype(np.float32)
        ang = pos[None, :] * inv[:, None]
        cos, sin = np.cos(ang).astype(np.float32), np.sin(ang).astype(np.float32)
        t = np.zeros((2, 128, smax), np.float32)
        t[0] = 1.0
        for b in range(nblk):
            o = b * dh
            t[0, o:o + half] = cos
            t[0, o + half:o + rot] = cos
            t[1, o:o + half] = -sin
            t[1, o + half:o + rot] = sin
        return t

    return cb.astype(bf), rope_tab(128, 1), rope_tab(64, 2)


def host_inputs(inp, seq_sel, n_layers, smax):
    cbf, ropeB, ropeC = host_consts(smax)
    xs, cs = [], []
    for kind, idx in seq_sel:
        if kind == "p":
            xs.append(np.asarray(inp["x_prompt"][idx]))
            cs.append(np.asarray(inp["c_prompt"][idx]))
        else:
            xs.append(np.asarray(inp["x_sample"][idx]))
            cs.append(np.asarray(inp["c_sample"][idx]))
    xin = np.ascontiguousarray(np.concatenate(xs, axis=0), dtype=np.float32)
    cmat = np.stack(cs, 0)
    cT = np.ascontiguousarray(cmat.reshape(len(cs), KC, 128).transpose(2, 1, 0), dtype=np.float32)
    return xin, cT, cbf, ropeB, ropeC


def shared_inputs(inp, n_layers, nseq):
    L = n_layers
    f = lambda k: np.ascontiguousarray(np.asarray(inp[k])[:L], dtype=np.float32)
    vec = np.zeros((L, 128, NV), np.float32)
    for l in range(L):
        vec[l, :, V_N1G:V_N1G + 16] = np.asarray(inp["norm1_g"][l]).reshape(16, 128).T
        vec[l, :, V_N2G:V_N2G + 16] = np.asarray(inp["norm2_g"][l]).reshape(16, 128).T
        vec[l, :, V_BG:V_BG + 48] = np.asarray(inp["b_gate"][l]).reshape(48, 128).T
        vec[l, :, V_QNA] = np.asarray(inp["qn_a"][l])
        vec[l, :, V_KNA] = np.asarray(inp["kn_a"][l])
        vec[l, :, V_QNB] = np.asarray(inp["qn_b"][l])
        vec[l, :, V_KNB] = np.asarray(inp["kn_b"][l])
        vec[l, :, V_QNC] = np.tile(np.asarray(inp["qn_c"][l]), 2)
        vec[l, :, V_KNC] = np.tile(np.asarray(inp["kn_c"][l]), 2)
        vec[l, :, V_SUB] = np.asarray(inp["subln_c"][l])
        vec[l, :, V_BADA:V_BADA + 96] = np.asarray(inp["b_ada"][l]).reshape(96, 128).T
        for i, k in enumerate(("lam_q1", "lam_k1", "lam_q2", "lam_k2")):
            vec[l, :, V_LAM + 64 * i:V_LAM + 64 * (i + 1)] = np.asarray(inp[k][l])[None, :]
    idx = na_index_table()
    rpb = np.asarray(inp["rpb_a"])[:L].astype(np.float32)
    flat = rpb.reshape(L, 4, 15 * 31)
    rpbx = np.where(idx[None, None] >= 0, flat[:, :, np.clip(idx, 0, None)], np.float32(-1.0e4)).astype(np.float32)
    b_ada3 = np.ascontiguousarray(np.broadcast_to(np.asarray(inp["b_ada"])[:L, None, :], (L, nseq, 6 * D)), dtype=np.float32)
    sh = {"w_ada": f("w_ada"), "b_ada3": b_ada3, "w_in": f("w_in"), "w_gate": f("w_gate"), "w_br_a": f("w_br_a"),
          "w_br_b": f("w_br_b"), "w_br_c": f("w_br_c"), "w_out": f("w_out"), "w_ff1": f("w_ff1"), "w_ff2": f("w_ff2"),
          "vec_fm": vec, "rpbx": np.ascontiguousarray(rpbx),
          "zrow": np.zeros((128, 1024), ml_dtypes.bfloat16),
          "selc": np.ascontiguousarray(np.repeat(np.eye(4, dtype=np.float32), 128, axis=1))}
    return sh


_NC_CACHE = {}


def kernel(**inp):
    n_cores = 8
    seqs = (4096, 2048, 2048)
    key = ("full",)
    if key not in _NC_CACHE:
        _NC_CACHE[key] = build(seqs, 2)
    nc = _NC_CACHE[key]
    sh = shared_inputs(inp, 2, 3)
    in_maps = []
    for c in range(n_cores):
        xin, cT, cbf, ropeB, ropeC = host_inputs(inp, [("p", c), ("s", 2 * c), ("s", 2 * c + 1)], 2, 4096)
        m = dict(sh)
        m.update({"xin": xin, "cT": cT, "cbf": cbf, "ropeB": ropeB, "ropeC": ropeC})
        in_maps.append(m)
    res = run_bass_kernel_spmd(nc, in_maps, core_ids=list(range(n_cores)))
    yp = np.empty((8, 4096, D), np.float32)
    ys = np.empty((16, 2048, D), np.float32)
    for c in range(n_cores):
        y = res.results[c]["yout"]
        yp[c] = y[0:4096]
        ys[2 * c] = y[4096:6144]
        ys[2 * c + 1] = y[6144:8192]
    return (yp, ys)
```

```python
import itertools
import math
import numpy as np
import ml_dtypes
import concourse.bass as bass
import concourse.mybir as mybir
from concourse.bass_utils import run_bass_kernel_spmd

F32 = mybir.dt.float32
BF16 = mybir.dt.bfloat16
AF = mybir.ActivationFunctionType
ALU = mybir.AluOpType
AX = mybir.AxisListType

D = 2048
KC = 16
DFF = 8192
INC = 6144
T = 512
EPS = 1e-6
PADK = 1024
NSEM = {"cast": 40, "sp": 28, "pool": 28}
CSTEP = 512
NCASE_TILES = None

V_N1G, V_N2G, V_BG, V_QNA, V_KNA, V_QNB, V_KNB, V_QNC, V_KNC, V_SUB = 0, 16, 32, 80, 81, 82, 83, 84, 85, 86
V_LAM = 87
V_BADA = V_LAM + 256
NV = V_BADA + 96


class Op:
    __slots__ = ("eng", "fn", "deps", "signal", "semkey", "val", "dma", "psread")

    def __init__(self, eng, fn, dma):
        self.eng = eng
        self.fn = fn
        self.dma = dma
        self.deps = []
        self.signal = False
        self.semkey = None
        self.val = 0
        self.psread = ()


class Sched:
    ENGS = ("pe", "act", "dve", "pool", "sp")

    def __init__(self):
        self.q = {e: [] for e in self.ENGS}
        self.lastw = {}
        self.readers = {}
        self.last_op = {}
        self.dma_rr = {}
        self.dma_last = {}
        self.uid = 0

    def add(self, eng, fn, reads=(), writes=(), dma=False, pool=None):
        o = Op(eng, fn, dma)
        deps = {}
        psr = [r for r in reads if r[0] == "ps"]
        if psr:
            reads = [r for r in reads if r[0] != "ps"]
            writes = list(writes)
            o.psread = set(psr)
            for r in psr:
                w = self.lastw.get(r)
                if w is not None and (not dma) and w.eng == eng and r in w.psread:
                    self.lastw[r] = o
                    continue
                writes.append(r)
        for r in reads:
            w = self.lastw.get(r)
            if w is not None:
                deps[id(w)] = (w, True)
        for r in writes:
            w = self.lastw.get(r)
            if w is not None:
                deps[id(w)] = (w, True)
            rd = self.readers.get(r)
            if rd:
                for x in rd.values():
                    if id(x) not in deps:
                        deps[id(x)] = (x, False)
        for d, is_w in deps.values():
            if (not dma) and (not d.dma) and d.eng == eng:
                if eng == "pe" or not is_w:
                    continue
            o.deps.append(d)
            d.signal = True
        if dma:
            pool = pool or eng
            n = self.dma_rr.get(pool, 0)
            self.dma_rr[pool] = n + 1
            k = (pool, n % NSEM[pool])
            prev = self.dma_last.get(k)
            if prev is not None:
                o.deps.append(prev)
            self.dma_last[k] = o
            o.semkey = k
            o.val = (prev.val if prev is not None else 0) + 16
            o.signal = True
        self.uid += 1
        for r in reads:
            self.readers.setdefault(r, {})[("d", self.uid) if dma else eng] = o
        for r in writes:
            self.lastw[r] = o
            self.readers[r] = {}
        self.q[eng].append(o)
        if not dma:
            self.last_op[eng] = o
        return o

    def barrier(self, include_cast=False):
        lasts = list(self.last_op.values())
        for k, o in self.dma_last.items():
            if k[0] == "cast" and not include_cast:
                continue
            lasts.append(o)
        for o in lasts:
            o.signal = True
        for e in self.ENGS:
            b = Op(e, None, False)
            b.deps = [o for o in lasts if o.dma or o.eng != e]
            self.q[e].append(b)
        keepw = {k: v for k, v in self.lastw.items() if k[0] == "W"}
        self.lastw = keepw
        self.readers = {}
        self.last_op = {}

    def sem_keys(self):
        keys = ["pe", "act", "dve"]
        for pool, n in self.dma_rr.items():
            for i in range(min(n, NSEM[pool])):
                keys.append((pool, i))
        return keys

    def finalize(self):
        for e in ("pe", "act", "dve"):
            cnt = 0
            for o in self.q[e]:
                if o.dma or o.fn is None:
                    continue
                if o.signal:
                    cnt += 1
                    o.semkey = e
                    o.val = cnt

    def run(self, e, eng, sems):
        seen = {}
        for o in self.q[e]:
            for d in o.deps:
                if seen.get(d.semkey, 0) >= d.val:
                    continue
                eng.wait_ge(sems[d.semkey], d.val)
                seen[d.semkey] = d.val
            if o.fn is None:
                continue
            ins = o.fn(eng)
            if o.signal:
                ins.then_inc(sems[o.semkey], 16 if o.dma else 1)


def _rs(r, R):
    return min(max(r - 4, 0), R - 8)


def na_plan(R):
    plan = []
    for i in range(R // 2):
        lo = _rs(2 * i, R)
        hi = _rs(2 * i + 1, R) + 7
        js = list(range(lo // 2, hi // 2 + 1))
        sig = []
        for j in js:
            s = []
            for kr in range(2):
                for qr in range(2):
                    Rk, Rq = 2 * j + kr, 2 * i + qr
                    valid = _rs(Rq, R) <= Rk <= _rs(Rq, R) + 7
                    s.append((Rk - Rq, valid))
            sig.append(tuple(s))
        plan.append((js, tuple(sig)))
    return plan


def na_cases():
    cases = {}
    off = 0
    for R in (64, 32):
        for js, sig in na_plan(R):
            if sig not in cases:
                cases[sig] = (off, len(js))
                off += len(js)
    return cases, off


def na_index_table():
    cases, nt = na_cases()
    tab = -np.ones((128, nt * 128), dtype=np.int64)
    kc = np.arange(64)
    qc = np.arange(64)
    cs = np.clip(qc - 8, 0, 48)
    colvalid = (kc[:, None] >= cs[None, :]) & (kc[:, None] < cs[None, :] + 16)
    colidx = kc[:, None] - qc[None, :] + 15
    for sig, (off, n) in cases.items():
        for t in range(n):
            s = sig[t]
            for kr in range(2):
                for qr in range(2):
                    dr, valid = s[kr * 2 + qr]
                    if not valid:
                        continue
                    blk = np.where(colvalid, (dr + 7) * 31 + colidx, -1)
                    tab[kr * 64:(kr + 1) * 64, (off + t) * 128 + qr * 64:(off + t) * 128 + (qr + 1) * 64] = blk
    return tab


def build(seqs=(4096, 2048, 2048), n_layers=2, debug=False, stop=99):
    nseq = len(seqs)
    TOK = sum(seqs)
    SMAX = max(seqs)
    SP = SMAX + 2 * PADK
    cases, NT = na_cases()
    nc = bass.Bass("TRN2", target_bir_lowering=False)

    def din(name, shape, dt=F32):
        return nc.dram_tensor(name, list(shape), dt, kind="ExternalInput").ap()

    DBG = ("mod_d", "qk_d", "va_d", "vb_d", "vc_d", "hT_d", "oT_d", "xmid_d")

    def dint(name, shape, dt=F32):
        kind = "ExternalOutput" if (debug and name in DBG) else "Internal"
        return nc.dram_tensor(name, list(shape), dt, kind=kind).ap()

    xin = din("xin", [TOK, D])
    cT = din("cT", [128, KC, nseq])
    w_ada = din("w_ada", [n_layers, D, 6 * D])
    b_ada3 = din("b_ada3", [n_layers, nseq, 6 * D])
    w_in = din("w_in", [n_layers, D, INC])
    w_gate = din("w_gate", [n_layers, D, 3 * D])
    w_bra = din("w_br_a", [n_layers, 512, D])
    w_brb = din("w_br_b", [n_layers, 256, D])
    w_brc = din("w_br_c", [n_layers, 768, D])
    w_out = din("w_out", [n_layers, D, D])
    w_ff1 = din("w_ff1", [n_layers, D, DFF])
    w_ff2 = din("w_ff2", [n_layers, DFF, D])
    vec_fm = din("vec_fm", [n_layers, 128, NV])
    cbf = din("cbf", [128, 8 * 128 + 256], BF16)
    ropeB = din("ropeB", [2, 128, SMAX])
    ropeC = din("ropeC", [2, 128, SMAX])
    rpbx = din("rpbx", [n_layers, 4, 128, NT * 128])
    zrow = din("zrow", [128, 1024], BF16)
    selc = din("selc", [4, 4 * 128])
    yout = nc.dram_tensor("yout", [TOK, D], F32, kind="ExternalOutput").ap()

    mod_d = dint("mod_d", [n_layers, nseq, 6 * D])
    win_d = dint("win_d", [n_layers, D, INC], BF16)
    wgate_d = dint("wgate_d", [n_layers, D, 3 * D], BF16)
    wbr_d = dint("wbr_d", [n_layers, 1536, D], BF16)
    wout_d = dint("wout_d", [n_layers, D, D], BF16)
    wff1_d = dint("wff1_d", [n_layers, D, DFF], BF16)
    wff2_d = dint("wff2_d", [n_layers, DFF, D], BF16)
    qk_d = dint("qk_d", [32, 128, SP], BF16)
    va_d = dint("va_d", [SMAX, 512], BF16)
    vb_d = dint("vb_d", [SMAX + 2 * PADK, 768], BF16)
    vc_d = dint("vc_d", [SMAX, 768], BF16)
    hT_d = dint("hT_d", [KC, 128, SMAX], BF16)
    oT_d = dint("oT_d", [12, 128, SMAX], BF16)
    xmid_d = dint("xmid_d", [SMAX, D])
    x1_d = dint("x1_d", [TOK, D])

    S = Sched()
    ARENA = 52000

    ctx = {}

    def emit_all():
        pass

    import contextlib
    stack = contextlib.ExitStack()
    arena = stack.enter_context(nc.sbuf_tensor("arena", [128, ARENA], F32))
    ps = stack.enter_context(nc.psum_tensor("ps", [128, 8, 512], F32))
    psflat = ps.rearrange("p b n -> p (b n)")

    class Bump:
        def __init__(self):
            self.off = 0
            self.n = 0

        def f32(self, cols):
            a = arena[:, self.off:self.off + cols]
            self.off += cols
            assert self.off <= ARENA, ("SBUF arena overflow", self.off)
            return a

        def bf(self, cols):
            w = (cols + 1) // 2
            a = arena[:, self.off:self.off + w].bitcast(BF16)
            self.off += w
            assert self.off <= ARENA, ("SBUF arena overflow", self.off)
            return a[:, 0:cols]

        def key(self, name):
            self.n += 1
            return (name, self.n)

    B = Bump()

    class Ring:
        def __init__(self, name, n, cols, dt):
            self.bufs = [(B.f32(cols) if dt == F32 else B.bf(cols), B.key(name)) for _ in range(n)]
            self.i = 0

        def next(self):
            r = self.bufs[self.i % len(self.bufs)]
            self.i += 1
            return r

    bank_rr = [0]

    def next_bank(lo=0, hi=8):
        b = lo + bank_rr[0] % (hi - lo)
        bank_rr[0] += 1
        return b

    def PSK(b):
        return ("ps", b)

    cb = B.bf(8 * 128 + 256)
    ident = cb[:, 0:128]
    ones128 = cb[:, 128:256]
    ones64 = cb[:, 256:384]
    ones_lo = cb[:, 384:512]
    ones_hi = cb[:, 512:640]
    permB = cb[:, 640:768]
    permC = cb[:, 768:896]
    band = cb[:, 1024:1280]
    vecs = [B.f32(NV) for _ in range(n_layers)]
    lamt = [B.f32(8) for _ in range(n_layers)]
    modfm = [B.f32(96 * nseq) for _ in range(n_layers)]
    sel = B.f32(4 * 128)
    S.add("sp", lambda e: e.dma_start(out=sel[0:4, :], in_=selc), writes=[("sel", 0)], dma=True)
    KCONST = ("const", 0)
    S.add("sp", lambda e: e.dma_start(out=cb, in_=cbf), writes=[KCONST], dma=True)
    for l in range(n_layers):
        S.add("sp", lambda e, l=l: e.dma_start(out=vecs[l], in_=vec_fm[l]), writes=[("vec", l)], dma=True)
    persist_mark = B.off

    def dma_ld(out, in_, reads, writes, slow=False):
        if slow:
            return S.add("sp", lambda e: e.dma_start(out=out, in_=in_, allow_slow_non_contiguous=True), reads=reads, writes=writes, dma=True)
        return S.add("sp", lambda e: e.dma_start(out=out, in_=in_), reads=reads, writes=writes, dma=True)

    def dma_st(out, in_, reads, writes):
        return S.add("pool", lambda e: e.dma_start(out=out, in_=in_), reads=reads, writes=writes, dma=True)

    def dma_cast(out, in_, reads, writes):
        return S.add("pool", lambda e: e.dma_start(out=out, in_=in_), reads=reads, writes=writes, dma=True, pool="cast")

    def pe_mm(out, lhsT, rhs, start, stop, reads, writes):
        return S.add("pe", lambda e: e.matmul(out, lhsT=lhsT, rhs=rhs, start=start, stop=stop), reads=reads, writes=writes)

    def act(out, in_, func, reads, writes, scale=1.0, bias=0.0, accum_out=None):
        if accum_out is None:
            fn = lambda e: e.activation(out=out, in_=in_, func=func, bias=bias, scale=scale)
        else:
            fn = lambda e: e.activation(out=out, in_=in_, func=func, bias=bias, scale=scale, accum_out=accum_out)
        return S.add("act", fn, reads=reads, writes=writes)

    def dve(fn, reads, writes):
        return S.add("dve", fn, reads=reads, writes=writes)

    def rstd_from(ssb_ap, n, out_f32, reads, wkey, tmp, tmpkey):
        act(tmp, ssb_ap, AF.Ln, reads=reads, writes=[tmpkey], scale=1.0 / n, bias=EPS)
        act(out_f32, tmp, AF.Exp, reads=[tmpkey], writes=[wkey], scale=-0.5)

    def cast_weight(src, dst, name, l, rows):
        step = CSTEP
        for r0 in range(0, rows, step):
            dma_cast(dst[r0:min(r0 + step, rows), :], src[r0:min(r0 + step, rows), :], reads=[], writes=[("W", l, name, r0 // step)])

    def wkeys(l, name, rows):
        return [("W", l, name, i) for i in range((rows + CSTEP - 1) // CSTEP)]

    zt = B.bf(1024)
    S.add("sp", lambda e: e.dma_start(out=zt, in_=zrow), writes=[("zt", 0)], dma=True)
    for j in range(14, 20):
        dma_st(qk_d[j, :, 0:PADK], zt, reads=[("zt", 0)], writes=[("padk", j, 0)])
        dma_st(qk_d[j, :, PADK + SMAX:PADK + SMAX + PADK], zt, reads=[("zt", 0)], writes=[("padk", j, 1)])
    for r0 in range(0, PADK, 128):
        dma_st(vb_d[r0:r0 + 128, :], zt[:, 0:768], reads=[("zt", 0)], writes=[("padv", r0, 0)])
        dma_st(vb_d[PADK + SMAX + r0:PADK + SMAX + r0 + 128, :], zt[:, 0:768], reads=[("zt", 0)], writes=[("padv", r0, 1)])

    persist_mark = B.off

    def issue_casts(l, first):
        if first:
            cast_weight(w_in[l], win_d[l], "win", l, D)
            return
        cast_weight(w_gate[l], wgate_d[l], "wgate", l, D)
        cast_weight(w_bra[l], wbr_d[l, 0:512, :], "wbra", l, 512)
        cast_weight(w_brb[l], wbr_d[l, 512:768, :], "wbrb", l, 256)
        cast_weight(w_brc[l], wbr_d[l, 768:1536, :], "wbrc", l, 768)
        cast_weight(w_out[l], wout_d[l], "wout", l, D)
        cast_weight(w_ff1[l], wff1_d[l], "wff1", l, D)
        cast_weight(w_ff2[l], wff2_d[l], "wff2", l, DFF)

    def prologue():
        B.off = persist_mark
        cs = B.f32(KC * nseq)
        csb = B.bf(KC * nseq)
        dma_ld(cs, cT.rearrange("p k s -> p (k s)"), reads=[], writes=[("cs", 0)])
        act(csb, cs, AF.Silu, reads=[("cs", 0)], writes=[("csb", 0)])
        csb3 = csb.rearrange("p (k s) -> p k s", s=nseq)
        wr = Ring("wada", 3, KC * 512 // 2, F32)
        br = Ring("bada", 2, 512, F32)
        orr = Ring("modo", 2, 512, F32)
        for l in range(n_layers):
            for cbk in range(24):
                wt, wk = wr.next()
                wtb = wt.bitcast(BF16).rearrange("p (k n) -> p k n", k=KC)
                c0 = cbk * 512
                S.add("pool", lambda e, wtb=wtb, l=l, c0=c0: e.dma_start(
                    out=wtb, in_=w_ada[l].rearrange("(k p) n -> p k n", p=128)[:, :, c0:c0 + 512]),
                    reads=[], writes=[wk], dma=True)
                bt, bk = br.next()
                dma_ld(bt[0:nseq, :], b_ada3[l, :, c0:c0 + 512], reads=[], writes=[bk])
                b = next_bank()
                for k in range(KC):
                    pe_mm(ps[0:nseq, b, :], csb3[:, k, :], wtb[:, k, :], k == 0, k == KC - 1,
                          reads=[wk, ("csb", 0)], writes=[PSK(b)])
                ot, ok = orr.next()
                dve(lambda e, ot=ot, b=b, bt=bt: e.tensor_tensor(out=ot[0:nseq, :], in0=ps[0:nseq, b, :], in1=bt[0:nseq, :], op=ALU.add),
                    reads=[PSK(b), bk], writes=[ok])
                dma_st(mod_d[l, :, c0:c0 + 512], ot[0:nseq, :], reads=[ok], writes=[("mod", l, cbk)])
                b2 = next_bank()
                mf3 = modfm[l].rearrange("p (c s) -> p c s", s=nseq)
                for cc in range(4):
                    for k in range(KC):
                        pe_mm(ps[:, b2, cc * nseq:(cc + 1) * nseq], wtb[:, k, cc * 128:(cc + 1) * 128], csb3[:, k, :], k == 0, k == KC - 1,
                              reads=[wk, ("csb", 0)], writes=[PSK(b2)])
                for cc in range(4):
                    c = cbk * 4 + cc
                    dve(lambda e, l=l, c=c, cc=cc, b2=b2, mf3=mf3: e.tensor_scalar(
                        out=mf3[:, c, :], in0=ps[:, b2, cc * nseq:(cc + 1) * nseq], scalar1=vecs[l][:, V_BADA + c:V_BADA + c + 1],
                        scalar2=None, op0=ALU.add), reads=[PSK(b2), ("vec", l)], writes=[("modfm", l, c)])
            v = vecs[l]
            lt = lamt[l]
            tmp = B.f32(128)
            tk = B.key("lamtmp")
            lam_init = 0.8 - 0.6 * math.exp(-0.3 * l)
            dve(lambda e, v=v, tmp=tmp: e.tensor_tensor(out=tmp[:, 0:64], in0=v[:, V_LAM:V_LAM + 64], in1=v[:, V_LAM + 64:V_LAM + 128], op=ALU.mult),
                reads=[("vec", l)], writes=[tk])
            dve(lambda e, v=v, tmp=tmp: e.tensor_tensor(out=tmp[:, 64:128], in0=v[:, V_LAM + 128:V_LAM + 192], in1=v[:, V_LAM + 192:V_LAM + 256], op=ALU.mult),
                reads=[("vec", l), tk], writes=[tk])
            dve(lambda e, lt=lt, tmp=tmp: e.reduce_sum(out=lt[:, 2:3], in_=tmp[:, 0:64], axis=AX.X), reads=[tk], writes=[("lam", l, 2)])
            dve(lambda e, lt=lt, tmp=tmp: e.reduce_sum(out=lt[:, 3:4], in_=tmp[:, 64:128], axis=AX.X), reads=[tk], writes=[("lam", l, 3)])
            act(lt[:, 4:6], lt[:, 2:4], AF.Exp, reads=[("lam", l, 2), ("lam", l, 3)], writes=[("lam", l, 4)])
            dve(lambda e, lt=lt: e.tensor_tensor(out=lt[:, 6:7], in0=lt[:, 5:6], in1=lt[:, 4:5], op=ALU.subtract),
                reads=[("lam", l, 4)], writes=[("lam", l, 6)])
            dve(lambda e, lt=lt, li=lam_init: e.tensor_scalar(out=lt[:, 0:1], in0=lt[:, 6:7], scalar1=-li, scalar2=None, op0=ALU.add),
                reads=[("lam", l, 6)], writes=[("lam", l, 0)])
            dve(lambda e, lt=lt, v=v, li=lam_init: e.tensor_scalar(out=lt[:, 1:2], in0=v[:, V_SUB:V_SUB + 1], scalar1=1.0 - li, scalar2=None, op0=ALU.mult),
                reads=[("vec", l)], writes=[("lam", l, 1)])
        S.barrier()

    def norm_transpose(xt, xk, a_fm, b_fm, abk, hT3, hkeys, st, R):
        import os
        NS = int(os.environ.get("KNS", "99"))
        junk, jk = R["junk"].next()
        ssq, sk = R["ss"].next()
        dve(lambda e: e.memset(ssq, 0.0), reads=[], writes=[sk])
        act(junk, xt, AF.Square, reads=[xk, sk], writes=[jk, sk], accum_out=ssq[:, 0:1])
        if NS < 2: return
        rstd_from(ssq[:, 0:1], D, ssq[:, 2:3], [sk], sk, ssq[:, 1:2], sk)
        if NS < 3: return
        xn, xnk = R["xn"].next()
        dve(lambda e: e.tensor_scalar(out=xn, in0=xt, scalar1=ssq[:, 2:3], scalar2=None, op0=ALU.mult),
            reads=[xk, sk], writes=[xnk])
        if NS < 4: return
        b0 = R["tb"][R["tbi"][0] % 2]
        R["tbi"][0] += 1
        pT = psflat[:, b0 * 512:(b0 + 2) * 512].bitcast(BF16).rearrange("p (k t) -> p k t", k=KC)
        for k in range(KC):
            S.add("pe", lambda e, k=k: e.transpose(pT[:, k, :], xn[:, k * 128:(k + 1) * 128], ident),
                  reads=[xnk, KCONST], writes=[PSK(b0 + k // 8)])
        if NS < 5: return
        for k in range(KC):
            o = hT3[:, k, st * 128:(st + 1) * 128]
            EV = os.environ.get("KEV", "mix")
            if (k % 2 == 0 and EV == "mix") or EV == "act":
                act(o, pT[:, k, :], AF.Identity, reads=[PSK(b0 + k // 8), abk], writes=[hkeys[st][0]],
                    scale=a_fm[:, k:k + 1], bias=b_fm[:, k:k + 1])
            else:
                dve(lambda e, o=o, k=k: e.tensor_scalar(out=o, in0=pT[:, k, :], scalar1=a_fm[:, k:k + 1], scalar2=b_fm[:, k:k + 1],
                                                         op0=ALU.mult, op1=ALU.add),
                    reads=[PSK(b0 + k // 8), abk], writes=[hkeys[st][1]])

    def load_mod_fm(l, s, idx_shift, idx_scale, gcol, a_fm, b_fm, abk):
        mf3 = modfm[l].rearrange("p (c s) -> p c s", s=nseq)
        v = vecs[l]
        dve(lambda e: e.scalar_tensor_tensor(out=a_fm, in0=mf3[:, idx_scale * 16:(idx_scale + 1) * 16, s], scalar=1.0,
                                             in1=v[:, gcol:gcol + KC], op0=ALU.add, op1=ALU.mult),
            reads=[("vec", l)], writes=[abk])
        dve(lambda e: e.tensor_copy(out=b_fm, in_=mf3[:, idx_shift * 16:(idx_shift + 1) * 16, s]), reads=[], writes=[abk])

    def gate_bcast(l, s, idx, gbc, kg, tmp, tmpk):
        dma_ld(tmp[0:nseq, :], mod_d[l, :, idx * D:(idx + 1) * D], reads=[], writes=[tmpk])
        for cbk in range(4):
            b = next_bank()
            pe_mm(ps[:, b, :], sel[0:nseq, s * 128:(s + 1) * 128], tmp[0:nseq, cbk * 512:(cbk + 1) * 512], True, True,
                  reads=[tmpk, ("sel", 0)], writes=[PSK(b)])
            dve(lambda e, b=b, cbk=cbk: e.tensor_copy(out=gbc[:, cbk * 512:(cbk + 1) * 512], in_=ps[:, b, :]), reads=[PSK(b)], writes=[kg])

    QK_MAP = {}
    for j in range(48):
        if j < 4: QK_MAP[j] = ("A", V_QNA, j)
        elif j < 8: QK_MAP[j] = ("A", V_KNA, j)
        elif j < 12: QK_MAP[j] = None
        elif j < 18: QK_MAP[j] = ("B", V_QNB, 8 + j - 12)
        elif j < 24: QK_MAP[j] = ("B", V_KNB, 14 + j - 18)
        elif j < 30: QK_MAP[j] = None
        elif j < 36: QK_MAP[j] = ("C", V_QNC, 20 + j - 30)
        elif j < 42: QK_MAP[j] = ("C", V_KNC, 26 + j - 36)
        else: QK_MAP[j] = None
    V_SEGS = {2: [(0, 512, va_d, 0)], 6: [(0, 512, vb_d, 0)], 7: [(0, 256, vb_d, 512)],
              10: [(256, 256, vc_d, 0)], 11: [(0, 512, vc_d, 256)]}

    def phase_A(l, s, row0, Sq, xsrc):
        B.off = persist_mark
        if Sq < SMAX:
            for j in range(14, 20):
                dma_st(qk_d[j, :, PADK + Sq:PADK + Sq + PADK], zt, reads=[("zt", 0)], writes=[("padk2", j)])
            for r0 in range(0, PADK, 128):
                dma_st(vb_d[PADK + Sq + r0:PADK + Sq + r0 + 128, :], zt[:, 0:768], reads=[("zt", 0)], writes=[("padv2", r0)])
        a_fm = B.f32(KC); b_fm = B.f32(KC); abk = B.key("ab1")
        load_mod_fm(l, s, 0, 1, V_N1G, a_fm, b_fm, abk)
        R = {"junk": Ring("junk", 1, D, BF16), "ss": Ring("ss", 3, 4, F32), "xn": Ring("xn", 2, D, BF16),
             "tb": (6, 6), "tbi": [0]}
        xr = Ring("xt", 2, D, F32)
        hr = [(B.bf(KC * T), [[B.key("hT"), B.key("hT")] for _ in range(4)]) for _ in range(2)]
        wr = Ring("wbig", 3, KC * 512 // 2, F32)
        rope = Ring("rope", 2, 4 * T, F32)
        sqr = Ring("sq", 3, T, BF16)
        xsr = Ring("xs", 3, T, BF16)
        rsr = Ring("rstd", 2, 2 * T, F32)
        t1r = Ring("t1", 2, T, F32)
        t2r = Ring("t2", 2, T, F32)
        qor = Ring("qo", 3, T, BF16)
        vor = Ring("vo", 3, T, BF16)
        v = vecs[l]
        ntl = Sq // T

        def prep(tt):
            t0 = tt * T
            hT, hkeys = hr[tt % 2]
            hT3 = hT.rearrange("p (k t) -> p k t", k=KC)
            hall = [k for st in range(4) for k in hkeys[st]]
            for st in range(4):
                xt, xk = xr.next()
                dma_ld(xt, xsrc[row0 + t0 + st * 128:row0 + t0 + (st + 1) * 128, :], reads=[], writes=[xk])
                norm_transpose(xt, xk, a_fm, b_fm, abk, hT3, hkeys, st, R)
            dma_st(hT_d.rearrange("k p s -> p k s")[:, :, t0:t0 + T], hT3, reads=hall, writes=[("hTd", tt)])
            rp, rpk = rope.next()
            rp4 = rp.rearrange("p (f t) -> p f t", f=4)
            dma_ld(rp4[:, 0:2, :], ropeB.rearrange("f p s -> p f s")[:, :, t0:t0 + T], reads=[], writes=[rpk])
            dma_ld(rp4[:, 2:4, :], ropeC.rearrange("f p s -> p f s")[:, :, t0:t0 + T], reads=[], writes=[rpk])
            return hT3, hkeys, hall, rp4, rpk

        cur = prep(0)
        for tt in range(ntl):
            t0 = tt * T
            hT3, hkeys, hall, rp4, rpk = cur
            nxt = None
            pending = None
            for blk in range(12):
                wt, wk = wr.next()
                wtb = wt.bitcast(BF16).rearrange("p (k n) -> p k n", k=KC)
                c0 = blk * 512
                dma_ld(wtb, win_d[l].rearrange("(k p) n -> p k n", p=128)[:, :, c0:c0 + 512],
                       reads=wkeys(l, "win", D), writes=[wk])
                for cc in range(4):
                    j = blk * 4 + cc
                    m = QK_MAP[j]
                    if m is None:
                        continue
                    typ, gcol, dj = m
                    b = next_bank(0, 6)
                    for k in range(KC):
                        pe_mm(ps[:, b, :], wtb[:, k, cc * 128:(cc + 1) * 128], hT3[:, k, :], k == 0, k == KC - 1,
                              reads=[wk] + hall, writes=[PSK(b)])
                    zp = ps[:, b, :]
                    g = v[:, gcol:gcol + 1]
                    sq, sqk = sqr.next()
                    act(sq, zp, AF.Square, reads=[PSK(b)], writes=[sqk])
                    xs = xsk = None
                    if typ != "A":
                        xs, xsk = xsr.next()
                        act(xs, zp, AF.Identity, reads=[PSK(b), ("vec", l)], writes=[xsk], scale=g)
                    if pending is not None:
                        pending()

                    def rest(typ=typ, dj=dj, b=b, zp=zp, g=g, sq=sq, sqk=sqk, xs=xs, xsk=xsk, rp4=rp4, rpk=rpk, t0=t0, tt=tt):
                        b2 = next_bank(0, 6)
                        onesm = ones64 if typ == "C" else ones128
                        nn = 64 if typ == "C" else 128
                        pe_mm(ps[:, b2, :], onesm, sq, True, True, reads=[sqk, KCONST], writes=[PSK(b2)])
                        if typ != "A":
                            b3 = next_bank(0, 6)
                            pm = permB if typ == "B" else permC
                            pe_mm(ps[:, b3, :], pm, xs, True, True, reads=[xsk, KCONST], writes=[PSK(b3)])
                        rs, rsk = rsr.next()
                        rstd_from(ps[:, b2, :], nn, rs[:, T:2 * T], [PSK(b2)], rsk, rs[:, 0:T], rsk)
                        rstd = rs[:, T:2 * T]
                        qo, qok = qor.next()
                        if typ == "A":
                            dve(lambda e: e.scalar_tensor_tensor(out=qo, in0=zp, scalar=g, in1=rstd, op0=ALU.mult, op1=ALU.mult),
                                reads=[PSK(b), rsk, ("vec", l)], writes=[qok])
                        else:
                            cosT = rp4[:, 0, :] if typ == "B" else rp4[:, 2, :]
                            sinT = rp4[:, 1, :] if typ == "B" else rp4[:, 3, :]
                            t1, t1k = t1r.next()
                            t2, t2k = t2r.next()
                            dve(lambda e: e.scalar_tensor_tensor(out=t1, in0=zp, scalar=g, in1=cosT, op0=ALU.mult, op1=ALU.mult),
                                reads=[PSK(b), rpk, ("vec", l)], writes=[t1k])
                            dve(lambda e: e.tensor_tensor(out=t2, in0=ps[:, b3, :], in1=sinT, op=ALU.mult),
                                reads=[PSK(b3), rpk], writes=[t2k])
                            dve(lambda e: e.tensor_tensor(out=t1, in0=t1, in1=t2, op=ALU.add), reads=[t1k, t2k], writes=[t1k])
                            dve(lambda e: e.tensor_tensor(out=qo, in0=t1, in1=rstd, op=ALU.mult), reads=[t1k, rsk], writes=[qok])
                        dma_st(qk_d[dj, :, PADK + t0:PADK + t0 + T], qo, reads=[qok], writes=[("qkd", dj, tt)])
                    pending = rest
                for (cin, ncol, dst, dcol) in V_SEGS.get(blk, []):
                    for st in range(4):
                        b = next_bank(0, 6)
                        for k in range(KC):
                            pe_mm(ps[:, b, 0:ncol], hT3[:, k, st * 128:(st + 1) * 128], wtb[:, k, cin:cin + ncol], k == 0, k == KC - 1,
                                  reads=[wk] + hkeys[st], writes=[PSK(b)])
                        if pending is not None:
                            pending()
                            pending = None
                        vo, vok = vor.next()
                        act(vo[:, 0:ncol], ps[:, b, 0:ncol], AF.Identity, reads=[PSK(b)], writes=[vok])
                        roff = PADK if dst is vb_d else 0
                        dma_st(dst[roff + t0 + st * 128:roff + t0 + (st + 1) * 128, dcol:dcol + ncol], vo[:, 0:ncol],
                               reads=[vok], writes=[("vd", blk, dcol, tt, st)])
                if blk == 6 and tt + 1 < ntl:
                    nxt = prep(tt + 1)
            if pending is not None:
                pending()
                pending = None
            cur = nxt
        S.barrier()

    def phase_B(l, Sq):
        B.off = persist_mark
        v = vecs[l]
        lt = lamt[l]
        sc128 = 128.0 ** -0.5
        sc64 = 64.0 ** -0.5
        mark = B.off
        R_rows = Sq // 64
        plan = na_plan(R_rows)
        QT = B.bf(Sq); KT = B.bf(Sq); Vt = B.bf(Sq); OT = B.bf(Sq)
        tab = B.f32(NT * 128)
        kQ, kK, kV, kO, kT = (B.key(n) for n in ("aQ", "aK", "aV", "aO", "aT"))
        Vt3 = Vt.rearrange("p (j c) -> p j c", c=128)
        sbr = Ring("asb", 2, 640, F32)
        ptr = Ring("apt", 2, 640, BF16)
        rzr = Ring("arz", 2, 512, F32)
        for h in range(4):
            dma_ld(QT, qk_d[h, :, PADK:PADK + Sq], reads=[], writes=[kQ])
            dma_ld(KT, qk_d[4 + h, :, PADK:PADK + Sq], reads=[], writes=[kK])
            dma_ld(Vt3, va_d[0:Sq, h * 128:(h + 1) * 128].rearrange("(j p) c -> p j c", p=128), reads=[], writes=[kV])
            dma_ld(tab, rpbx[l, h], reads=[], writes=[kT])
            npairs = R_rows // 2
            for i0 in range(0, npairs, 4):
                bO = 4 + (i0 // 4) % 2
                bZ = 6 + (i0 // 4) % 2
                for i in range(i0, i0 + 4):
                    js, sig = plan[i]
                    off, n = cases[sig]
                    bS = 2 * ((i) % 2)
                    for jj, j in enumerate(js):
                        pe_mm(ps[:, bS + jj // 4, (jj % 4) * 128:(jj % 4 + 1) * 128], KT[:, j * 128:(j + 1) * 128], QT[:, i * 128:(i + 1) * 128],
                              True, True, reads=[kK, kQ], writes=[PSK(bS + jj // 4)])
                    sb, sbk = sbr.next()
                    pt, ptk = ptr.next()
                    dve(lambda e, sb=sb, bS=bS, n=n, off=off: e.scalar_tensor_tensor(
                        out=sb[:, 0:n * 128], in0=psflat[:, bS * 512:bS * 512 + n * 128], scalar=sc128,
                        in1=tab[:, off * 128:(off + n) * 128], op0=ALU.mult, op1=ALU.add),
                        reads=[PSK(bS), PSK(bS + 1), kT], writes=[sbk])
                    act(pt[:, 0:n * 128], sb[:, 0:n * 128], AF.Exp, reads=[sbk], writes=[ptk])
                    sl = i - i0
                    for jj, j in enumerate(js):
                        pe_mm(ps[:, bO, sl * 128:(sl + 1) * 128], Vt3[:, j, :], pt[:, jj * 128:(jj + 1) * 128], jj == 0, jj == n - 1,
                              reads=[kV, ptk], writes=[PSK(bO)])
                    for jj, j in enumerate(js):
                        pe_mm(ps[:, bZ, sl * 128:(sl + 1) * 128], ones128, pt[:, jj * 128:(jj + 1) * 128], jj == 0, jj == n - 1,
                              reads=[KCONST, ptk], writes=[PSK(bZ)])
                rz, rzk = rzr.next()
                dve(lambda e, rz=rz, bZ=bZ: e.reciprocal(out=rz, in_=ps[:, bZ, :]), reads=[PSK(bZ)], writes=[rzk])
                dve(lambda e, rz=rz, bO=bO, i0=i0, OT=OT: e.tensor_tensor(out=OT[:, i0 * 128:(i0 + 4) * 128], in0=ps[:, bO, :], in1=rz, op=ALU.mult),
                    reads=[PSK(bO), rzk], writes=[kO])
            dma_st(oT_d[h, :, 0:Sq], OT, reads=[kO], writes=[("oTd", h)])
        S.barrier()
        B.off = mark
        QT = B.bf(Sq); KT = B.bf(Sq + 2 * PADK); OT = B.bf(Sq)
        QTr = B.bf(Sq); KTr = B.bf(Sq + 2 * PADK)
        accO = B.f32(Sq); accZ = B.f32(Sq)
        kQ, kK, kO, kaO, kaZ = (B.key(n) for n in ("bQ", "bK", "bO", "baO", "baZ"))
        kQr, kKr = B.key("bQr"), B.key("bKr")
        vtr = Ring("bvt", 2, 33 * 128, BF16)
        er = Ring("be", 2, 256, BF16)
        ptr = Ring("bpt", 2, 256, BF16)
        for hg in range(2):
            for g, d in enumerate((1, 4, 16)):
                cj = g * 2 + hg
                dma_ld(QT, qk_d[8 + cj, :, PADK:PADK + Sq], reads=[], writes=[kQ])
                dma_ld(KT, qk_d[14 + cj, :, 0:Sq + 2 * PADK], reads=[], writes=[kK])
                L = Sq // d
                nq = L // 128
                if d == 1:
                    Qv = QT.rearrange("p (d m) -> p d m", d=1)
                    Kv = KT[:, PADK - 64:PADK - 64 + L + 128].rearrange("p (d m) -> p d m", d=1)
                    kQu, kKu = kQ, kK
                else:
                    Qv = QTr.rearrange("p (d m) -> p d m", d=d)
                    Kv = KTr[:, 0:(L + 128) * d].rearrange("p (d m) -> p d m", d=d)
                    dve(lambda e, Qv=Qv, QT=QT, d=d: e.tensor_copy(out=Qv, in_=QT.rearrange("p (m d) -> p d m", d=d)),
                        reads=[kQ], writes=[kQr])
                    act(Kv, KT[:, PADK - 64 * d:PADK - 64 * d + (L + 128) * d].rearrange("p (m d) -> p d m", d=d), AF.Identity,
                        reads=[kK], writes=[kKr])
                    kQu, kKu = kQr, kKr
                for r in range(d):
                    vt, vtk = vtr.next()
                    vt3 = vt[:, 0:(nq + 1) * 128].rearrange("p (j c) -> p j c", c=128)
                    base = PADK - 64 * d + r
                    nrow = (nq + 1) * 128
                    vsrc = vb_d[base:base + (nrow - 1) * d + 1:d, cj * 128:(cj + 1) * 128].rearrange("(j p) c -> p j c", p=128)
                    dma_ld(vt3, vsrc, reads=[], writes=[vtk])
                    for i0 in range(0, nq, 4):
                        w = min(4, nq - i0)
                        bO = 4 + (i0 // 4) % 2
                        bZ = 6 + (i0 // 4) % 2
                        for i in range(i0, i0 + w):
                            bS = next_bank(0, 4)
                            rhs = Qv[:, r, 128 * i:128 * (i + 1)]
                            for t in range(2):
                                pe_mm(ps[:, bS, t * 128:(t + 1) * 128], Kv[:, r, 128 * (i + t):128 * (i + t + 1)], rhs, True, True,
                                      reads=[kKu, kQu], writes=[PSK(bS)])
                            ee, ek = er.next()
                            pt, ptk = ptr.next()
                            act(ee, ps[:, bS, 0:256], AF.Exp, reads=[PSK(bS)], writes=[ek], scale=sc128)
                            dve(lambda e, pt=pt, ee=ee: e.tensor_tensor(out=pt, in0=ee, in1=band, op=ALU.mult),
                                reads=[ek, KCONST], writes=[ptk])
                            sl = i - i0
                            for t in range(2):
                                pe_mm(ps[:, bO, sl * 128:(sl + 1) * 128], vt3[:, i + t, :], pt[:, t * 128:(t + 1) * 128], t == 0, t == 1,
                                      reads=[vtk, ptk], writes=[PSK(bO)])
                            for t in range(2):
                                jp = i + t
                                om = ones_hi if jp == 0 else (ones_lo if jp == nq else ones128)
                                pe_mm(ps[:, bZ, sl * 128:(sl + 1) * 128], om, pt[:, t * 128:(t + 1) * 128], t == 0, t == 1,
                                      reads=[KCONST, ptk], writes=[PSK(bZ)])
                        p0 = r + 128 * i0 * d
                        dO = accO[:, p0:p0 + (w * 128 - 1) * d + 1:d]
                        dZ = accZ[:, p0:p0 + (w * 128 - 1) * d + 1:d]
                        if g == 0:
                            act(dO, ps[:, bO, 0:w * 128], AF.Identity, reads=[PSK(bO)], writes=[kaO])
                            act(dZ, ps[:, bZ, 0:w * 128], AF.Identity, reads=[PSK(bZ)], writes=[kaZ])
                        else:
                            dve(lambda e, dO=dO, bO=bO, w=w: e.tensor_tensor(out=dO, in0=dO, in1=ps[:, bO, 0:w * 128], op=ALU.add),
                                reads=[PSK(bO), kaO], writes=[kaO])
                            dve(lambda e, dZ=dZ, bZ=bZ, w=w: e.tensor_tensor(out=dZ, in0=dZ, in1=ps[:, bZ, 0:w * 128], op=ALU.add),
                                reads=[PSK(bZ), kaZ], writes=[kaZ])
            dve(lambda e, accZ=accZ: e.reciprocal(out=accZ, in_=accZ), reads=[kaZ], writes=[kaZ])
            dve(lambda e, OT=OT, accO=accO, accZ=accZ: e.tensor_tensor(out=OT, in0=accO, in1=accZ, op=ALU.mult), reads=[kaO, kaZ], writes=[kO])
            dma_st(oT_d[4 + hg, :, 0:Sq], OT, reads=[kO], writes=[("oTd", 4 + hg)])
        S.barrier()
        B.off = mark
        csets = []
        for _ in range(2):
            Qx = B.bf(2 * Sq); KT = B.bf(Sq); Vt = B.bf(Sq); OT = B.bf(Sq)
            ks = tuple(B.key(n) for n in ("cQ", "cK", "cV", "cO"))
            dve(lambda e, Qx=Qx: e.memset(Qx, 0.0), reads=[], writes=[ks[0]])
            csets.append((Qx, KT, Vt, OT, ks))
        ptr = Ring("cpt", 5, 512, BF16)
        rzr = Ring("crz", 2, 512, F32)
        tr = Ring("ct", 2, 512, F32)
        ar = Ring("ca", 3, 256, F32)
        sqr = Ring("csq", 3, 256, BF16)
        rsr = Ring("crs", 2, 512, F32)
        PRE = 2
        nkt = Sq // 128
        nqb = Sq // 256

        def c_load(hc):
            Qx, KT, Vt, OT, (kQ, kK, kV, kO) = csets[hc % 2]
            Qx3 = Qx.rearrange("p (c s) -> p c s", c=2)
            Vt3 = Vt.rearrange("p (j c) -> p j c", c=128)
            dma_ld(Qx3[0:64, 0, :], qk_d[20 + hc, 0:64, PADK:PADK + Sq], reads=[], writes=[kQ])
            dma_ld(Qx3[64:128, 1, :], qk_d[20 + hc, 64:128, PADK:PADK + Sq], reads=[], writes=[kQ])
            dma_ld(KT, qk_d[26 + hc, :, PADK:PADK + Sq], reads=[], writes=[kK])
            dma_ld(Vt3, vc_d[0:Sq, hc * 128:(hc + 1) * 128].rearrange("(j p) c -> p j c", p=128), reads=[], writes=[kV])

        c_load(0)
        for hc in range(6):
            if hc + 1 < 6:
                c_load(hc + 1)
            Qx, KT, Vt, OT, (kQ, kK, kV, kO) = csets[hc % 2]
            Qx3 = Qx.rearrange("p (c s) -> p c s", c=2)
            Vt3 = Vt.rearrange("p (j c) -> p j c", c=128)
            items = [(qb, kt) for qb in range(nqb) for kt in range(nkt)]
            stash = {}
            pend = []
            for n in range(len(items) + PRE):
                if n < len(items):
                    qb, kt = items[n]
                    bS = next_bank(0, 3)
                    pe_mm(ps[:, bS, :].rearrange("p (c q) -> p c q", c=2), KT[:, kt * 128:(kt + 1) * 128], Qx3[:, :, qb * 256:(qb + 1) * 256],
                          True, True, reads=[kK, kQ], writes=[PSK(bS)])
                    pt, ptk = ptr.next()
                    act(pt, ps[:, bS, :], AF.Exp, reads=[PSK(bS)], writes=[ptk], scale=sc64)
                    stash[n] = (pt, ptk)
                m = n - PRE
                if m >= 0:
                    qb, kt = items[m]
                    pt, ptk = stash.pop(m)
                    bO = 3 + qb % 2
                    bZ = 5 + qb % 2
                    pe_mm(ps[:, bO, :], Vt3[:, kt, :], pt, kt == 0, kt == nkt - 1, reads=[kV, ptk], writes=[PSK(bO)])
                    pe_mm(ps[:, bZ, :], ones128, pt, kt == 0, kt == nkt - 1, reads=[KCONST, ptk], writes=[PSK(bZ)])
                    if kt == nkt - 1:
                        rz, rzk = rzr.next()
                        tt_, ttk = tr.next()
                        aa, aak = ar.next()
                        sq, sqk = sqr.next()
                        dve(lambda e, rz=rz, bZ=bZ: e.reciprocal(out=rz, in_=ps[:, bZ, :]), reads=[PSK(bZ)], writes=[rzk])
                        dve(lambda e, tt_=tt_, rz=rz, bO=bO: e.tensor_tensor(out=tt_, in0=ps[:, bO, :], in1=rz, op=ALU.mult),
                            reads=[PSK(bO), rzk], writes=[ttk])
                        dve(lambda e, aa=aa, tt_=tt_: e.scalar_tensor_tensor(out=aa, in0=tt_[:, 256:512], scalar=lt[:, 0:1], in1=tt_[:, 0:256],
                                                                              op0=ALU.mult, op1=ALU.add),
                            reads=[ttk, ("lam", l, 0)], writes=[aak])
                        act(sq, aa, AF.Square, reads=[aak], writes=[sqk])

                        def part2(aa=aa, aak=aak, sq=sq, sqk=sqk, qb=qb, OT=OT, kO=kO):
                            rs, rsk = rsr.next()
                            pe_mm(ps[:, 7, 0:256], ones128, sq, True, True, reads=[KCONST, sqk], writes=[PSK(7)])
                            rstd_from(ps[:, 7, 0:256], 128, rs[:, 256:512], [PSK(7)], rsk, rs[:, 0:256], rsk)
                            dve(lambda e: e.scalar_tensor_tensor(out=OT[:, qb * 256:(qb + 1) * 256], in0=aa, scalar=lt[:, 1:2],
                                                                 in1=rs[:, 256:512], op0=ALU.mult, op1=ALU.mult),
                                reads=[aak, rsk, ("lam", l, 1)], writes=[kO])
                        pend.append((n + 6, part2))
                while pend and pend[0][0] <= n:
                    pend.pop(0)[1]()
            for _, fn in pend:
                fn()
            dma_st(oT_d[6 + hc, :, 0:Sq], OT, reads=[kO], writes=[("oTd", 6 + hc)])
        S.barrier()

    def phase_C1(l, s, row0, Sq, xsrc):
        B.off = persist_mark
        v = vecs[l]
        g1bc = B.f32(D); kg1 = B.key("g1bc")
        GATE1 = True
        OTt = B.bf(12 * T); kOT = B.key("OTt")
        OT3 = OTt.rearrange("p (c t) -> p c t", c=12)
        hT = B.bf(KC * T); khT = B.key("hTt")
        hT3 = hT.rearrange("p (k t) -> p k t", k=KC)
        mT = B.bf(KC * T)
        mT3 = mT.rearrange("p (k t) -> p k t", k=KC)
        kmT = [B.key("mT") for _ in range(KC)]
        xts = [(B.f32(D), B.key("xt")) for _ in range(4)]
        gate_bcast(l, s, 2, g1bc, kg1, xts[0][0], xts[0][1])
        wsr = Ring("wsm", 3, (48 + 12) * 128 // 2, F32)
        wbr = Ring("wbig", 2, KC * 512 // 2, F32)
        sgr = Ring("sg", 4, T, F32)
        mr = Ring("m", 3, T, F32)
        tr = Ring("tmp", 2, T, F32)
        for tt in range(Sq // T):
            t0 = tt * T
            dma_ld(OT3, oT_d.rearrange("c p s -> p c s")[:, :, t0:t0 + T], reads=[], writes=[kOT])
            dma_ld(hT3, hT_d.rearrange("k p s -> p k s")[:, :, t0:t0 + T], reads=[], writes=[khT])
            for st in range(4):
                dma_ld(xts[st][0], xsrc[row0 + t0 + st * 128:row0 + t0 + (st + 1) * 128, :], reads=[], writes=[xts[st][1]])
            for dc in range(KC):
                wt, wk = wsr.next()
                wtb = wt.bitcast(BF16)
                wg = wtb[:, 0:48 * 128].rearrange("p (g k n) -> p g k n", g=3, k=KC)
                wb = wtb[:, 48 * 128:60 * 128].rearrange("p (k n) -> p k n", k=12)
                dma_ld(wg, wgate_d[l].rearrange("(k p) (g n) -> p g k n", p=128, g=3)[:, :, :, dc * 128:(dc + 1) * 128],
                       reads=wkeys(l, "wgate", D), writes=[wk])
                dma_ld(wb, wbr_d[l].rearrange("(k p) n -> p k n", p=128)[:, :, dc * 128:(dc + 1) * 128],
                       reads=wkeys(l, "wbra", 512) + wkeys(l, "wbrb", 256) + wkeys(l, "wbrc", 768), writes=[wk])
                kcs = [(0, 4), (4, 6), (6, 12)]
                mcur = None
                for gi in range(3):
                    bg = next_bank()
                    for k in range(KC):
                        pe_mm(ps[:, bg, :], wg[:, gi, k, :], hT3[:, k, :], k == 0, k == KC - 1, reads=[wk, khT], writes=[PSK(bg)])
                    by = next_bank()
                    k0, k1 = kcs[gi]
                    for k in range(k0, k1):
                        pe_mm(ps[:, by, :], wb[:, k, :], OT3[:, k, :], k == k0, k == k1 - 1, reads=[wk, kOT], writes=[PSK(by)])
                    sg, sgk = sgr.next()
                    act(sg, ps[:, bg, :], AF.Sigmoid, reads=[PSK(bg), ("vec", l)], writes=[sgk],
                        bias=v[:, V_BG + gi * 16 + dc:V_BG + gi * 16 + dc + 1])
                    mm_, mk = mr.next()
                    dve(lambda e, mm_=mm_, sg=sg, by=by: e.tensor_tensor(out=mm_, in0=ps[:, by, :], in1=sg, op=ALU.mult),
                        reads=[PSK(by), sgk], writes=[mk])
                    if gi == 0:
                        mcur = (mm_, mk)
                    elif gi == 1:
                        dve(lambda e, a=mcur[0], b_=mm_: e.tensor_tensor(out=a, in0=a, in1=b_, op=ALU.add),
                            reads=[mcur[1], mk], writes=[mcur[1]])
                    else:
                        dve(lambda e, a=mcur[0], b_=mm_, dc=dc: e.tensor_tensor(out=mT3[:, dc, :], in0=a, in1=b_, op=ALU.add),
                            reads=[mcur[1], mk], writes=[kmT[dc]])
            for cbk in range(4):
                wt, wk = wbr.next()
                wtb = wt.bitcast(BF16).rearrange("p (k n) -> p k n", k=KC)
                dma_ld(wtb, wout_d[l].rearrange("(k p) n -> p k n", p=128)[:, :, cbk * 512:(cbk + 1) * 512],
                       reads=wkeys(l, "wout", D), writes=[wk])
                for st in range(4):
                    b = next_bank()
                    for k in range(KC):
                        pe_mm(ps[:, b, :], mT3[:, k, st * 128:(st + 1) * 128], wtb[:, k, :], k == 0, k == KC - 1,
                              reads=[wk, kmT[k]], writes=[PSK(b)])
                    tm, tk = tr.next()
                    xt, xk = xts[st]
                    dve(lambda e, tm=tm, b=b, cbk=cbk: e.tensor_tensor(out=tm, in0=ps[:, b, :], in1=g1bc[:, cbk * 512:(cbk + 1) * 512], op=ALU.mult),
                        reads=[PSK(b), kg1], writes=[tk])
                    dve(lambda e, tm=tm, xt=xt, cbk=cbk: e.tensor_tensor(out=xt[:, cbk * 512:(cbk + 1) * 512], in0=xt[:, cbk * 512:(cbk + 1) * 512],
                                                                          in1=tm, op=ALU.add),
                        reads=[tk, xk], writes=[xk])
            for st in range(4):
                dma_st(xmid_d[t0 + st * 128:t0 + (st + 1) * 128, :], xts[st][0], reads=[xts[st][1]], writes=[("xmid", tt, st)])
        S.barrier()

    def phase_C2(l, s, row0, Sq, xdst):
        B.off = persist_mark
        a_fm = B.f32(KC); b_fm = B.f32(KC); abk = B.key("ab2")
        load_mod_fm(l, s, 3, 4, V_N2G, a_fm, b_fm, abk)
        g2bc = B.f32(D); kg2 = B.key("g2bc")
        GATE2 = True
        R = {"junk": Ring("junk", 1, D, BF16), "ss": Ring("ss", 3, 4, F32), "xn": Ring("xn", 2, D, BF16),
             "tb": (0, 2), "tbi": [0]}
        xts = [(B.f32(D), B.key("xt")) for _ in range(4)]
        gate_bcast(l, s, 5, g2bc, kg2, xts[0][0], xts[0][1])
        hT = B.bf(KC * T)
        hT3 = hT.rearrange("p (k t) -> p k t", k=KC)
        hkeys = [[B.key("h2T"), B.key("h2T")] for _ in range(4)]
        hall = [k for st in range(4) for k in hkeys[st]]
        f1 = B.bf(64 * T)
        f13 = f1.rearrange("p (c t) -> p c t", c=64)
        kf1 = [B.key("f1") for _ in range(64)]
        wbr = Ring("wbig", 3, KC * 512 // 2, F32)
        rr = Ring("relu", 3, T, BF16)
        tr = Ring("tmp", 2, T, F32)
        for tt in range(Sq // T):
            t0 = tt * T
            for st in range(4):
                xt, xk = xts[st]
                dma_ld(xt, xmid_d[t0 + st * 128:t0 + (st + 1) * 128, :], reads=[], writes=[xk])
                norm_transpose(xt, xk, a_fm, b_fm, abk, hT3, hkeys, st, R)
            for blk in range(16):
                wt, wk = wbr.next()
                wtb = wt.bitcast(BF16).rearrange("p (k n) -> p k n", k=KC)
                dma_ld(wtb, wff1_d[l].rearrange("(k p) n -> p k n", p=128)[:, :, blk * 512:(blk + 1) * 512],
                       reads=wkeys(l, "wff1", D), writes=[wk])
                for cc in range(4):
                    c = blk * 4 + cc
                    b = next_bank(0, 4)
                    for k in range(KC):
                        pe_mm(ps[:, b, :], wtb[:, k, cc * 128:(cc + 1) * 128], hT3[:, k, :], k == 0, k == KC - 1,
                              reads=[wk] + hall, writes=[PSK(b)])
                    rl, rk = rr.next()
                    act(rl, ps[:, b, :], AF.Relu, reads=[PSK(b)], writes=[rk])
                    dve(lambda e, rl=rl, c=c: e.tensor_tensor(out=f13[:, c, :], in0=rl, in1=rl, op=ALU.mult),
                        reads=[rk], writes=[kf1[c]])
            for cbk in range(4):
                for kg in range(4):
                    wt, wk = wbr.next()
                    wtb = wt.bitcast(BF16).rearrange("p (k n) -> p k n", k=KC)
                    dma_ld(wtb, wff2_d[l, kg * 2048:(kg + 1) * 2048, :].rearrange("(k p) n -> p k n", p=128)[:, :, cbk * 512:(cbk + 1) * 512],
                           reads=wkeys(l, "wff2", DFF), writes=[wk])
                    for st in range(4):
                        b = 4 + st
                        for k in range(KC):
                            c = kg * 16 + k
                            pe_mm(ps[:, b, :], f13[:, c, st * 128:(st + 1) * 128], wtb[:, k, :], c == 0, c == 63,
                                  reads=[wk, kf1[c]], writes=[PSK(b)])
                for st in range(4):
                    b = 4 + st
                    tm, tk = tr.next()
                    xt, xk = xts[st]
                    dve(lambda e, tm=tm, b=b, cbk=cbk: e.tensor_tensor(out=tm, in0=ps[:, b, :], in1=g2bc[:, cbk * 512:(cbk + 1) * 512], op=ALU.mult),
                        reads=[PSK(b), kg2], writes=[tk])
                    dve(lambda e, tm=tm, xt=xt, cbk=cbk: e.tensor_tensor(out=xt[:, cbk * 512:(cbk + 1) * 512], in0=xt[:, cbk * 512:(cbk + 1) * 512],
                                                                          in1=tm, op=ALU.add),
                        reads=[tk, xk], writes=[xk])
            for st in range(4):
                dma_st(xdst[row0 + t0 + st * 128:row0 + t0 + (st + 1) * 128, :], xts[st][0], reads=[xts[st][1]], writes=[("xout", tt, st)])
        S.barrier()

    issue_casts(0, True)
    if stop >= 1:
        prologue()
    if stop >= 2:
        issue_casts(0, False)
    for l in range(n_layers if stop >= 2 else 0):
        xsrc = xin if l == 0 else x1_d
        xdst = yout if l == n_layers - 1 else x1_d
        row0 = 0
        for s, Sq in enumerate(seqs):
            phase_A(l, s, row0, Sq, xsrc)
            if l == 0 and s == 0 and n_layers > 1:
                issue_casts(1, True)
                issue_casts(1, False)
            if stop >= 3:
                phase_B(l, Sq)
            if stop >= 4:
                phase_C1(l, s, row0, Sq, xsrc)
            if stop >= 5:
                phase_C2(l, s, row0, Sq, xdst)
            row0 += Sq
    S.barrier(include_cast=True)
    S.finalize()

    keys = S.sem_keys()
    sems = {}
    for i, k in enumerate(keys):
        sems[k] = stack.enter_context(nc.semaphore("s%d" % i))
    block = stack.enter_context(nc.Block())

    @block.tensor
    def _(e):
        S.run("pe", e, sems)

    @block.scalar
    def _(e):
        S.run("act", e, sems)

    @block.vector
    def _(e):
        S.run("dve", e, sems)

    @block.gpsimd
    def _(e):
        S.run("pool", e, sems)

    @block.sync
    def _(e):
        S.run("sp", e, sems)

    stack.close()
    return nc


def host_consts(smax):
    bf = ml_dtypes.bfloat16
    cb = np.zeros((128, 8 * 128 + 256), np.float32)
    cb[:, 0:128] = np.eye(128)
    cb[:, 128:256] = 1.0
    cb[0:64, 256:320] = 1.0
    cb[64:128, 320:384] = 1.0
    cb[0:64, 384:512] = 1.0
    cb[64:128, 512:640] = 1.0
    pB = np.zeros((128, 128), np.float32)
    for d in range(16):
        pB[d + 16, d] = 1.0
        pB[d, d + 16] = 1.0
    cb[:, 640:768] = pB
    pC = np.zeros((128, 128), np.float32)
    for blk in range(2):
        for d in range(8):
            pC[blk * 64 + d + 8, blk * 64 + d] = 1.0
            pC[blk * 64 + d, blk * 64 + d + 8] = 1.0
    cb[:, 768:896] = pC
    p = np.arange(128)[:, None]
    c = np.arange(128)[None, :]
    cb[:, 1024:1152] = (p >= c)
    cb[:, 1152:1280] = (p <= c)
    pos = np.arange(smax, dtype=np.float32)

    def rope_tab(dh, nblk):
        rot = dh // 4
        half = rot // 2
        inv = (500000.0 ** (-np.arange(half, dtype=np.float32) / half)).astype(np.float32)
        ang = pos[None, :] * inv[:, None]
        cos, sin = np.cos(ang).astype(np.float32), np.sin(ang).astype(np.float32)
        t = np.zeros((2, 128, smax), np.float32)
        t[0] = 1.0
        for b in range(nblk):
            o = b * dh
            t[0, o:o + half] = cos
            t[0, o + half:o + rot] = cos
            t[1, o:o + half] = -sin
            t[1, o + half:o + rot] = sin
        return t

    return cb.astype(bf), rope_tab(128, 1), rope_tab(64, 2)


def host_inputs(inp, seq_sel, n_layers, smax):
    cbf, ropeB, ropeC = host_consts(smax)
    xs, cs = [], []
    for kind, idx in seq_sel:
        if kind == "p":
            xs.append(np.asarray(inp["x_prompt"][idx]))
            cs.append(np.asarray(inp["c_prompt"][idx]))
        else:
            xs.append(np.asarray(inp["x_sample"][idx]))
            cs.append(np.asarray(inp["c_sample"][idx]))
    xin = np.ascontiguousarray(np.concatenate(xs, axis=0), dtype=np.float32)
    cmat = np.stack(cs, 0)
    cT = np.ascontiguousarray(cmat.reshape(len(cs), KC, 128).transpose(2, 1, 0), dtype=np.float32)
    return xin, cT, cbf, ropeB, ropeC


def shared_inputs(inp, n_layers, nseq):
    L = n_layers
    f = lambda k: np.ascontiguousarray(np.asarray(inp[k])[:L], dtype=np.float32)
    vec = np.zeros((L, 128, NV), np.float32)
    for l in range(L):
        vec[l, :, V_N1G:V_N1G + 16] = np.asarray(inp["norm1_g"][l]).reshape(16, 128).T
        vec[l, :, V_N2G:V_N2G + 16] = np.asarray(inp["norm2_g"][l]).reshape(16, 128).T
        vec[l, :, V_BG:V_BG + 48] = np.asarray(inp["b_gate"][l]).reshape(48, 128).T
        vec[l, :, V_QNA] = np.asarray(inp["qn_a"][l])
        vec[l, :, V_KNA] = np.asarray(inp["kn_a"][l])
        vec[l, :, V_QNB] = np.asarray(inp["qn_b"][l])
        vec[l, :, V_KNB] = np.asarray(inp["kn_b"][l])
        vec[l, :, V_QNC] = np.tile(np.asarray(inp["qn_c"][l]), 2)
        vec[l, :, V_KNC] = np.tile(np.asarray(inp["kn_c"][l]), 2)
        vec[l, :, V_SUB] = np.asarray(inp["subln_c"][l])
        vec[l, :, V_BADA:V_BADA + 96] = np.asarray(inp["b_ada"][l]).reshape(96, 128).T
        for i, k in enumerate(("lam_q1", "lam_k1", "lam_q2", "lam_k2")):
            vec[l, :, V_LAM + 64 * i:V_LAM + 64 * (i + 1)] = np.asarray(inp[k][l])[None, :]
    idx = na_index_table()
    rpb = np.asarray(inp["rpb_a"])[:L].astype(np.float32)
    flat = rpb.reshape(L, 4, 15 * 31)
    rpbx = np.where(idx[None, None] >= 0, flat[:, :, np.clip(idx, 0, None)], np.float32(-1.0e4)).astype(np.float32)
    b_ada3 = np.ascontiguousarray(np.broadcast_to(np.asarray(inp["b_ada"])[:L, None, :], (L, nseq, 6 * D)), dtype=np.float32)
    sh = {"w_ada": f("w_ada"), "b_ada3": b_ada3, "w_in": f("w_in"), "w_gate": f("w_gate"), "w_br_a": f("w_br_a"),
          "w_br_b": f("w_br_b"), "w_br_c": f("w_br_c"), "w_out": f("w_out"), "w_ff1": f("w_ff1"), "w_ff2": f("w_ff2"),
          "vec_fm": vec, "rpbx": np.ascontiguousarray(rpbx),
          "zrow": np.zeros((128, 1024), ml_dtypes.bfloat16),
          "selc": np.ascontiguousarray(np.repeat(np.eye(4, dtype=np.float32), 128, axis=1))}
    return sh


_NC_CACHE = {}


def kernel(**inp):
    n_cores = 8
    seqs = (4096, 2048, 2048)
    key = ("full",)
    if key not in _NC_CACHE:
        _NC_CACHE[key] = build(seqs, 2)
    nc = _NC_CACHE[key]
    sh = shared_inputs(inp, 2, 3)
    in_maps = []
    for c in range(n_cores):
        xin, cT, cbf, ropeB, ropeC = host_inputs(inp, [("p", c), ("s", 2 * c), ("s", 2 * c + 1)], 2, 4096)
        m = dict(sh)
        m.update({"xin": xin, "cT": cT, "cbf": cbf, "ropeB": ropeB, "ropeC": ropeC})
        in_maps.append(m)
    res = run_bass_kernel_spmd(nc, in_maps, core_ids=list(range(n_cores)))
    yp = np.empty((8, 4096, D), np.float32)
    ys = np.empty((16, 2048, D), np.float32)
    for c in range(n_cores):
        y = res.results[c]["yout"]
        yp[c] = y[0:4096]
        ys[2 * c] = y[4096:6144]
        ys[2 * c + 1] = y[6144:8192]
    return (yp, ys)
```

```python
import itertools
import math
import numpy as np
import ml_dtypes
import concourse.bass as bass
import concourse.mybir as mybir
from concourse.bass_utils import run_bass_kernel_spmd

F32 = mybir.dt.float32
BF16 = mybir.dt.bfloat16
AF = mybir.ActivationFunctionType
ALU = mybir.AluOpType
AX = mybir.AxisListType

D = 2048
KC = 16
DFF = 8192
INC = 6144
T = 512
EPS = 1e-6
PADK = 1024
NSEM = {"cast": 40, "sp": 28, "pool": 28}
CSTEP = 512
NCASE_TILES = None

V_N1G, V_N2G, V_BG, V_QNA, V_KNA, V_QNB, V_KNB, V_QNC, V_KNC, V_SUB = 0, 16, 32, 80, 81, 82, 83, 84, 85, 86
V_LAM = 87
V_BADA = V_LAM + 256
NV = V_BADA + 96


class Op:
    __slots__ = ("eng", "fn", "deps", "signal", "semkey", "val", "dma")

    def __init__(self, eng, fn, dma):
        self.eng = eng
        self.fn = fn
        self.dma = dma
        self.deps = []
        self.signal = False
        self.semkey = None
        self.val = 0


class Sched:
    ENGS = ("pe", "act", "dve", "pool", "sp")

    def __init__(self):
        self.q = {e: [] for e in self.ENGS}
        self.lastw = {}
        self.readers = {}
        self.last_op = {}
        self.dma_rr = {}
        self.dma_last = {}
        self.uid = 0

    def add(self, eng, fn, reads=(), writes=(), dma=False, pool=None):
        o = Op(eng, fn, dma)
        deps = {}
        psr = [r for r in reads if r[0] == "ps"]
        if psr:
            reads = [r for r in reads if r[0] != "ps"]
            writes = list(writes) + psr
        for r in reads:
            w = self.lastw.get(r)
            if w is not None:
                deps[id(w)] = (w, True)
        for r in writes:
            w = self.lastw.get(r)
            if w is not None:
                deps[id(w)] = (w, True)
            rd = self.readers.get(r)
            if rd:
                for x in rd.values():
                    if id(x) not in deps:
                        deps[id(x)] = (x, False)
        for d, is_w in deps.values():
            if (not dma) and (not d.dma) and d.eng == eng:
                if eng == "pe" or not is_w:
                    continue
            o.deps.append(d)
            d.signal = True
        if dma:
            pool = pool or eng
            n = self.dma_rr.get(pool, 0)
            self.dma_rr[pool] = n + 1
            k = (pool, n % NSEM[pool])
            prev = self.dma_last.get(k)
            if prev is not None:
                o.deps.append(prev)
            self.dma_last[k] = o
            o.semkey = k
            o.val = (prev.val if prev is not None else 0) + 16
            o.signal = True
        self.uid += 1
        for r in reads:
            self.readers.setdefault(r, {})[("d", self.uid) if dma else eng] = o
        for r in writes:
            self.lastw[r] = o
            self.readers[r] = {}
        self.q[eng].append(o)
        if not dma:
            self.last_op[eng] = o
        return o

    def barrier(self, include_cast=False):
        lasts = list(self.last_op.values())
        for k, o in self.dma_last.items():
            if k[0] == "cast" and not include_cast:
                continue
            lasts.append(o)
        for o in lasts:
            o.signal = True
        for e in self.ENGS:
            b = Op(e, None, False)
            b.deps = [o for o in lasts if o.dma or o.eng != e]
            self.q[e].append(b)
        keepw = {k: v for k, v in self.lastw.items() if k[0] == "W"}
        self.lastw = keepw
        self.readers = {}
        self.last_op = {}

    def sem_keys(self):
        keys = ["pe", "act", "dve"]
        for pool, n in self.dma_rr.items():
            for i in range(min(n, NSEM[pool])):
                keys.append((pool, i))
        return keys

    def finalize(self):
        for e in ("pe", "act", "dve"):
            cnt = 0
            for o in self.q[e]:
                if o.dma or o.fn is None:
                    continue
                if o.signal:
                    cnt += 1
                    o.semkey = e
                    o.val = cnt

    def run(self, e, eng, sems):
        seen = {}
        for o in self.q[e]:
            for d in o.deps:
                if seen.get(d.semkey, 0) >= d.val:
                    continue
                eng.wait_ge(sems[d.semkey], d.val)
                seen[d.semkey] = d.val
            if o.fn is None:
                continue
            ins = o.fn(eng)
            if o.signal:
                ins.then_inc(sems[o.semkey], 16 if o.dma else 1)


def _rs(r, R):
    return min(max(r - 4, 0), R - 8)


def na_plan(R):
    plan = []
    for i in range(R // 2):
        lo = _rs(2 * i, R)
        hi = _rs(2 * i + 1, R) + 7
        js = list(range(lo // 2, hi // 2 + 1))
        sig = []
        for j in js:
            s = []
            for kr in range(2):
                for qr in range(2):
                    Rk, Rq = 2 * j + kr, 2 * i + qr
                    valid = _rs(Rq, R) <= Rk <= _rs(Rq, R) + 7
                    s.append((Rk - Rq, valid))
            sig.append(tuple(s))
        plan.append((js, tuple(sig)))
    return plan


def na_cases():
    cases = {}
    off = 0
    for R in (64, 32):
        for js, sig in na_plan(R):
            if sig not in cases:
                cases[sig] = (off, len(js))
                off += len(js)
    return cases, off


def na_index_table():
    cases, nt = na_cases()
    tab = -np.ones((128, nt * 128), dtype=np.int64)
    kc = np.arange(64)
    qc = np.arange(64)
    cs = np.clip(qc - 8, 0, 48)
    colvalid = (kc[:, None] >= cs[None, :]) & (kc[:, None] < cs[None, :] + 16)
    colidx = kc[:, None] - qc[None, :] + 15
    for sig, (off, n) in cases.items():
        for t in range(n):
            s = sig[t]
            for kr in range(2):
                for qr in range(2):
                    dr, valid = s[kr * 2 + qr]
                    if not valid:
                        continue
                    blk = np.where(colvalid, (dr + 7) * 31 + colidx, -1)
                    tab[kr * 64:(kr + 1) * 64, (off + t) * 128 + qr * 64:(off + t) * 128 + (qr + 1) * 64] = blk
    return tab


def build(seqs=(4096, 2048, 2048), n_layers=2, debug=False, stop=99):
    nseq = len(seqs)
    TOK = sum(seqs)
    SMAX = max(seqs)
    SP = SMAX + 2 * PADK
    cases, NT = na_cases()
    nc = bass.Bass("TRN2", target_bir_lowering=False)

    def din(name, shape, dt=F32):
        return nc.dram_tensor(name, list(shape), dt, kind="ExternalInput").ap()

    DBG = ("mod_d", "qk_d", "va_d", "vb_d", "vc_d", "hT_d", "oT_d", "xmid_d")

    def dint(name, shape, dt=F32):
        kind = "ExternalOutput" if (debug and name in DBG) else "Internal"
        return nc.dram_tensor(name, list(shape), dt, kind=kind).ap()

    xin = din("xin", [TOK, D])
    cT = din("cT", [128, KC, nseq])
    w_ada = din("w_ada", [n_layers, D, 6 * D])
    b_ada3 = din("b_ada3", [n_layers, nseq, 6 * D])
    w_in = din("w_in", [n_layers, D, INC])
    w_gate = din("w_gate", [n_layers, D, 3 * D])
    w_bra = din("w_br_a", [n_layers, 512, D])
    w_brb = din("w_br_b", [n_layers, 256, D])
    w_brc = din("w_br_c", [n_layers, 768, D])
    w_out = din("w_out", [n_layers, D, D])
    w_ff1 = din("w_ff1", [n_layers, D, DFF])
    w_ff2 = din("w_ff2", [n_layers, DFF, D])
    vec_fm = din("vec_fm", [n_layers, 128, NV])
    cbf = din("cbf", [128, 8 * 128 + 256], BF16)
    ropeB = din("ropeB", [2, 128, SMAX])
    ropeC = din("ropeC", [2, 128, SMAX])
    rpbx = din("rpbx", [n_layers, 4, 128, NT * 128])
    zrow = din("zrow", [128, 1024], BF16)
    selc = din("selc", [4, 4 * 128])
    yout = nc.dram_tensor("yout", [TOK, D], F32, kind="ExternalOutput").ap()

    mod_d = dint("mod_d", [n_layers, nseq, 6 * D])
    win_d = dint("win_d", [n_layers, D, INC], BF16)
    wgate_d = dint("wgate_d", [n_layers, D, 3 * D], BF16)
    wbr_d = dint("wbr_d", [n_layers, 1536, D], BF16)
    wout_d = dint("wout_d", [n_layers, D, D], BF16)
    wff1_d = dint("wff1_d", [n_layers, D, DFF], BF16)
    wff2_d = dint("wff2_d", [n_layers, DFF, D], BF16)
    qk_d = dint("qk_d", [32, 128, SP], BF16)
    va_d = dint("va_d", [SMAX, 512], BF16)
    vb_d = dint("vb_d", [SMAX + 2 * PADK, 768], BF16)
    vc_d = dint("vc_d", [SMAX, 768], BF16)
    hT_d = dint("hT_d", [KC, 128, SMAX], BF16)
    oT_d = dint("oT_d", [12, 128, SMAX], BF16)
    xmid_d = dint("xmid_d", [SMAX, D])
    x1_d = dint("x1_d", [TOK, D])

    S = Sched()
    ARENA = 52000

    ctx = {}

    def emit_all():
        pass

    import contextlib
    stack = contextlib.ExitStack()
    arena = stack.enter_context(nc.sbuf_tensor("arena", [128, ARENA], F32))
    ps = stack.enter_context(nc.psum_tensor("ps", [128, 8, 512], F32))
    psflat = ps.rearrange("p b n -> p (b n)")

    class Bump:
        def __init__(self):
            self.off = 0
            self.n = 0

        def f32(self, cols):
            a = arena[:, self.off:self.off + cols]
            self.off += cols
            assert self.off <= ARENA, ("SBUF arena overflow", self.off)
            return a

        def bf(self, cols):
            w = (cols + 1) // 2
            a = arena[:, self.off:self.off + w].bitcast(BF16)
            self.off += w
            assert self.off <= ARENA, ("SBUF arena overflow", self.off)
            return a[:, 0:cols]

        def key(self, name):
            self.n += 1
            return (name, self.n)

    B = Bump()

    class Ring:
        def __init__(self, name, n, cols, dt):
            self.bufs = [(B.f32(cols) if dt == F32 else B.bf(cols), B.key(name)) for _ in range(n)]
            self.i = 0

        def next(self):
            r = self.bufs[self.i % len(self.bufs)]
            self.i += 1
            return r

    bank_rr = [0]

    def next_bank(lo=0, hi=8):
        b = lo + bank_rr[0] % (hi - lo)
        bank_rr[0] += 1
        return b

    def PSK(b):
        return ("ps", b)

    cb = B.bf(8 * 128 + 256)
    ident = cb[:, 0:128]
    ones128 = cb[:, 128:256]
    ones64 = cb[:, 256:384]
    ones_lo = cb[:, 384:512]
    ones_hi = cb[:, 512:640]
    permB = cb[:, 640:768]
    permC = cb[:, 768:896]
    band = cb[:, 1024:1280]
    vecs = [B.f32(NV) for _ in range(n_layers)]
    lamt = [B.f32(8) for _ in range(n_layers)]
    modfm = [B.f32(96 * nseq) for _ in range(n_layers)]
    sel = B.f32(4 * 128)
    S.add("sp", lambda e: e.dma_start(out=sel[0:4, :], in_=selc), writes=[("sel", 0)], dma=True)
    KCONST = ("const", 0)
    S.add("sp", lambda e: e.dma_start(out=cb, in_=cbf), writes=[KCONST], dma=True)
    for l in range(n_layers):
        S.add("sp", lambda e, l=l: e.dma_start(out=vecs[l], in_=vec_fm[l]), writes=[("vec", l)], dma=True)
    persist_mark = B.off

    def dma_ld(out, in_, reads, writes, slow=False):
        if slow:
            return S.add("sp", lambda e: e.dma_start(out=out, in_=in_, allow_slow_non_contiguous=True), reads=reads, writes=writes, dma=True)
        return S.add("sp", lambda e: e.dma_start(out=out, in_=in_), reads=reads, writes=writes, dma=True)

    def dma_st(out, in_, reads, writes):
        return S.add("pool", lambda e: e.dma_start(out=out, in_=in_), reads=reads, writes=writes, dma=True)

    def dma_cast(out, in_, reads, writes):
        return S.add("pool", lambda e: e.dma_start(out=out, in_=in_), reads=reads, writes=writes, dma=True, pool="cast")

    def pe_mm(out, lhsT, rhs, start, stop, reads, writes):
        return S.add("pe", lambda e: e.matmul(out, lhsT=lhsT, rhs=rhs, start=start, stop=stop), reads=reads, writes=writes)

    def act(out, in_, func, reads, writes, scale=1.0, bias=0.0, accum_out=None):
        if accum_out is None:
            fn = lambda e: e.activation(out=out, in_=in_, func=func, bias=bias, scale=scale)
        else:
            fn = lambda e: e.activation(out=out, in_=in_, func=func, bias=bias, scale=scale, accum_out=accum_out)
        return S.add("act", fn, reads=reads, writes=writes)

    def dve(fn, reads, writes):
        return S.add("dve", fn, reads=reads, writes=writes)

    def rstd_from(ssb_ap, n, out_f32, reads, wkey, tmp, tmpkey):
        act(tmp, ssb_ap, AF.Ln, reads=reads, writes=[tmpkey], scale=1.0 / n, bias=EPS)
        act(out_f32, tmp, AF.Exp, reads=[tmpkey], writes=[wkey], scale=-0.5)

    def cast_weight(src, dst, name, l, rows):
        step = CSTEP
        for r0 in range(0, rows, step):
            dma_cast(dst[r0:min(r0 + step, rows), :], src[r0:min(r0 + step, rows), :], reads=[], writes=[("W", l, name, r0 // step)])

    def wkeys(l, name, rows):
        return [("W", l, name, i) for i in range((rows + CSTEP - 1) // CSTEP)]

    zt = B.bf(1024)
    S.add("sp", lambda e: e.dma_start(out=zt, in_=zrow), writes=[("zt", 0)], dma=True)
    for j in range(14, 20):
        dma_st(qk_d[j, :, 0:PADK], zt, reads=[("zt", 0)], writes=[("padk", j, 0)])
        dma_st(qk_d[j, :, PADK + SMAX:PADK + SMAX + PADK], zt, reads=[("zt", 0)], writes=[("padk", j, 1)])
    for r0 in range(0, PADK, 128):
        dma_st(vb_d[r0:r0 + 128, :], zt[:, 0:768], reads=[("zt", 0)], writes=[("padv", r0, 0)])
        dma_st(vb_d[PADK + SMAX + r0:PADK + SMAX + r0 + 128, :], zt[:, 0:768], reads=[("zt", 0)], writes=[("padv", r0, 1)])

    persist_mark = B.off

    def issue_casts(l, first):
        if first:
            cast_weight(w_in[l], win_d[l], "win", l, D)
            return
        cast_weight(w_gate[l], wgate_d[l], "wgate", l, D)
        cast_weight(w_bra[l], wbr_d[l, 0:512, :], "wbra", l, 512)
        cast_weight(w_brb[l], wbr_d[l, 512:768, :], "wbrb", l, 256)
        cast_weight(w_brc[l], wbr_d[l, 768:1536, :], "wbrc", l, 768)
        cast_weight(w_out[l], wout_d[l], "wout", l, D)
        cast_weight(w_ff1[l], wff1_d[l], "wff1", l, D)
        cast_weight(w_ff2[l], wff2_d[l], "wff2", l, DFF)

    def prologue():
        B.off = persist_mark
        cs = B.f32(KC * nseq)
        csb = B.bf(KC * nseq)
        dma_ld(cs, cT.rearrange("p k s -> p (k s)"), reads=[], writes=[("cs", 0)])
        act(csb, cs, AF.Silu, reads=[("cs", 0)], writes=[("csb", 0)])
        csb3 = csb.rearrange("p (k s) -> p k s", s=nseq)
        wr = Ring("wada", 3, KC * 512 // 2, F32)
        br = Ring("bada", 2, 512, F32)
        orr = Ring("modo", 2, 512, F32)
        for l in range(n_layers):
            for cbk in range(24):
                wt, wk = wr.next()
                wtb = wt.bitcast(BF16).rearrange("p (k n) -> p k n", k=KC)
                c0 = cbk * 512
                S.add("pool", lambda e, wtb=wtb, l=l, c0=c0: e.dma_start(
                    out=wtb, in_=w_ada[l].rearrange("(k p) n -> p k n", p=128)[:, :, c0:c0 + 512]),
                    reads=[], writes=[wk], dma=True)
                bt, bk = br.next()
                dma_ld(bt[0:nseq, :], b_ada3[l, :, c0:c0 + 512], reads=[], writes=[bk])
                b = next_bank()
                for k in range(KC):
                    pe_mm(ps[0:nseq, b, :], csb3[:, k, :], wtb[:, k, :], k == 0, k == KC - 1,
                          reads=[wk, ("csb", 0)], writes=[PSK(b)])
                ot, ok = orr.next()
                dve(lambda e, ot=ot, b=b, bt=bt: e.tensor_tensor(out=ot[0:nseq, :], in0=ps[0:nseq, b, :], in1=bt[0:nseq, :], op=ALU.add),
                    reads=[PSK(b), bk], writes=[ok])
                dma_st(mod_d[l, :, c0:c0 + 512], ot[0:nseq, :], reads=[ok], writes=[("mod", l, cbk)])
                b2 = next_bank()
                mf3 = modfm[l].rearrange("p (c s) -> p c s", s=nseq)
                for cc in range(4):
                    for k in range(KC):
                        pe_mm(ps[:, b2, cc * nseq:(cc + 1) * nseq], wtb[:, k, cc * 128:(cc + 1) * 128], csb3[:, k, :], k == 0, k == KC - 1,
                              reads=[wk, ("csb", 0)], writes=[PSK(b2)])
                for cc in range(4):
                    c = cbk * 4 + cc
                    dve(lambda e, l=l, c=c, cc=cc, b2=b2, mf3=mf3: e.tensor_scalar(
                        out=mf3[:, c, :], in0=ps[:, b2, cc * nseq:(cc + 1) * nseq], scalar1=vecs[l][:, V_BADA + c:V_BADA + c + 1],
                        scalar2=None, op0=ALU.add), reads=[PSK(b2), ("vec", l)], writes=[("modfm", l, c)])
            v = vecs[l]
            lt = lamt[l]
            tmp = B.f32(128)
            tk = B.key("lamtmp")
            lam_init = 0.8 - 0.6 * math.exp(-0.3 * l)
            dve(lambda e, v=v, tmp=tmp: e.tensor_tensor(out=tmp[:, 0:64], in0=v[:, V_LAM:V_LAM + 64], in1=v[:, V_LAM + 64:V_LAM + 128], op=ALU.mult),
                reads=[("vec", l)], writes=[tk])
            dve(lambda e, v=v, tmp=tmp: e.tensor_tensor(out=tmp[:, 64:128], in0=v[:, V_LAM + 128:V_LAM + 192], in1=v[:, V_LAM + 192:V_LAM + 256], op=ALU.mult),
                reads=[("vec", l), tk], writes=[tk])
            dve(lambda e, lt=lt, tmp=tmp: e.reduce_sum(out=lt[:, 2:3], in_=tmp[:, 0:64], axis=AX.X), reads=[tk], writes=[("lam", l, 2)])
            dve(lambda e, lt=lt, tmp=tmp: e.reduce_sum(out=lt[:, 3:4], in_=tmp[:, 64:128], axis=AX.X), reads=[tk], writes=[("lam", l, 3)])
            act(lt[:, 4:6], lt[:, 2:4], AF.Exp, reads=[("lam", l, 2), ("lam", l, 3)], writes=[("lam", l, 4)])
            dve(lambda e, lt=lt: e.tensor_tensor(out=lt[:, 6:7], in0=lt[:, 5:6], in1=lt[:, 4:5], op=ALU.subtract),
                reads=[("lam", l, 4)], writes=[("lam", l, 6)])
            dve(lambda e, lt=lt, li=lam_init: e.tensor_scalar(out=lt[:, 0:1], in0=lt[:, 6:7], scalar1=-li, scalar2=None, op0=ALU.add),
                reads=[("lam", l, 6)], writes=[("lam", l, 0)])
            dve(lambda e, lt=lt, v=v, li=lam_init: e.tensor_scalar(out=lt[:, 1:2], in0=v[:, V_SUB:V_SUB + 1], scalar1=1.0 - li, scalar2=None, op0=ALU.mult),
                reads=[("vec", l)], writes=[("lam", l, 1)])
        S.barrier()

    def norm_transpose(xt, xk, a_fm, b_fm, abk, hT3, hkeys, st, R):
        import os
        NS = int(os.environ.get("KNS", "99"))
        junk, jk = R["junk"].next()
        ssq, sk = R["ss"].next()
        dve(lambda e: e.memset(ssq, 0.0), reads=[], writes=[sk])
        act(junk, xt, AF.Square, reads=[xk, sk], writes=[jk, sk], accum_out=ssq[:, 0:1])
        if NS < 2: return
        rstd_from(ssq[:, 0:1], D, ssq[:, 2:3], [sk], sk, ssq[:, 1:2], sk)
        if NS < 3: return
        xn, xnk = R["xn"].next()
        dve(lambda e: e.tensor_scalar(out=xn, in0=xt, scalar1=ssq[:, 2:3], scalar2=None, op0=ALU.mult),
            reads=[xk, sk], writes=[xnk])
        if NS < 4: return
        b0 = R["tb"][R["tbi"][0] % 2]
        R["tbi"][0] += 1
        pT = psflat[:, b0 * 512:(b0 + 2) * 512].bitcast(BF16).rearrange("p (k t) -> p k t", k=KC)
        for k in range(KC):
            S.add("pe", lambda e, k=k: e.transpose(pT[:, k, :], xn[:, k * 128:(k + 1) * 128], ident),
                  reads=[xnk, KCONST], writes=[PSK(b0 + k // 8)])
        if NS < 5: return
        for k in range(KC):
            o = hT3[:, k, st * 128:(st + 1) * 128]
            EV = os.environ.get("KEV", "mix")
            if (k % 2 == 0 and EV == "mix") or EV == "act":
                act(o, pT[:, k, :], AF.Identity, reads=[PSK(b0 + k // 8), abk], writes=[hkeys[st][0]],
                    scale=a_fm[:, k:k + 1], bias=b_fm[:, k:k + 1])
            else:
                dve(lambda e, o=o, k=k: e.tensor_scalar(out=o, in0=pT[:, k, :], scalar1=a_fm[:, k:k + 1], scalar2=b_fm[:, k:k + 1],
                                                         op0=ALU.mult, op1=ALU.add),
                    reads=[PSK(b0 + k // 8), abk], writes=[hkeys[st][1]])

    def load_mod_fm(l, s, idx_shift, idx_scale, gcol, a_fm, b_fm, abk):
        mf3 = modfm[l].rearrange("p (c s) -> p c s", s=nseq)
        v = vecs[l]
        dve(lambda e: e.scalar_tensor_tensor(out=a_fm, in0=mf3[:, idx_scale * 16:(idx_scale + 1) * 16, s], scalar=1.0,
                                             in1=v[:, gcol:gcol + KC], op0=ALU.add, op1=ALU.mult),
            reads=[("vec", l)], writes=[abk])
        dve(lambda e: e.tensor_copy(out=b_fm, in_=mf3[:, idx_shift * 16:(idx_shift + 1) * 16, s]), reads=[], writes=[abk])

    def gate_bcast(l, s, idx, gbc, kg, tmp, tmpk):
        dma_ld(tmp[0:nseq, :], mod_d[l, :, idx * D:(idx + 1) * D], reads=[], writes=[tmpk])
        for cbk in range(4):
            b = next_bank()
            pe_mm(ps[:, b, :], sel[0:nseq, s * 128:(s + 1) * 128], tmp[0:nseq, cbk * 512:(cbk + 1) * 512], True, True,
                  reads=[tmpk, ("sel", 0)], writes=[PSK(b)])
            dve(lambda e, b=b, cbk=cbk: e.tensor_copy(out=gbc[:, cbk * 512:(cbk + 1) * 512], in_=ps[:, b, :]), reads=[PSK(b)], writes=[kg])

    QK_MAP = {}
    for j in range(48):
        if j < 4: QK_MAP[j] = ("A", V_QNA, j)
        elif j < 8: QK_MAP[j] = ("A", V_KNA, j)
        elif j < 12: QK_MAP[j] = None
        elif j < 18: QK_MAP[j] = ("B", V_QNB, 8 + j - 12)
        elif j < 24: QK_MAP[j] = ("B", V_KNB, 14 + j - 18)
        elif j < 30: QK_MAP[j] = None
        elif j < 36: QK_MAP[j] = ("C", V_QNC, 20 + j - 30)
        elif j < 42: QK_MAP[j] = ("C", V_KNC, 26 + j - 36)
        else: QK_MAP[j] = None
    V_SEGS = {2: [(0, 512, va_d, 0)], 6: [(0, 512, vb_d, 0)], 7: [(0, 256, vb_d, 512)],
              10: [(256, 256, vc_d, 0)], 11: [(0, 512, vc_d, 256)]}

    def phase_A(l, s, row0, Sq, xsrc):
        B.off = persist_mark
        if Sq < SMAX:
            for j in range(14, 20):
                dma_st(qk_d[j, :, PADK + Sq:PADK + Sq + PADK], zt, reads=[("zt", 0)], writes=[("padk2", j)])
            for r0 in range(0, PADK, 128):
                dma_st(vb_d[PADK + Sq + r0:PADK + Sq + r0 + 128, :], zt[:, 0:768], reads=[("zt", 0)], writes=[("padv2", r0)])
        a_fm = B.f32(KC); b_fm = B.f32(KC); abk = B.key("ab1")
        load_mod_fm(l, s, 0, 1, V_N1G, a_fm, b_fm, abk)
        R = {"junk": Ring("junk", 1, D, BF16), "ss": Ring("ss", 3, 4, F32), "xn": Ring("xn", 2, D, BF16),
             "tb": (6, 6), "tbi": [0]}
        xr = Ring("xt", 2, D, F32)
        hr = [(B.bf(KC * T), [[B.key("hT"), B.key("hT")] for _ in range(4)]) for _ in range(2)]
        wr = Ring("wbig", 3, KC * 512 // 2, F32)
        rope = Ring("rope", 2, 4 * T, F32)
        sqr = Ring("sq", 3, T, BF16)
        xsr = Ring("xs", 3, T, BF16)
        rsr = Ring("rstd", 2, 2 * T, F32)
        t1r = Ring("t1", 2, T, F32)
        t2r = Ring("t2", 2, T, F32)
        qor = Ring("qo", 3, T, BF16)
        vor = Ring("vo", 3, T, BF16)
        v = vecs[l]
        ntl = Sq // T

        def prep(tt):
            t0 = tt * T
            hT, hkeys = hr[tt % 2]
            hT3 = hT.rearrange("p (k t) -> p k t", k=KC)
            hall = [k for st in range(4) for k in hkeys[st]]
            for st in range(4):
                xt, xk = xr.next()
                dma_ld(xt, xsrc[row0 + t0 + st * 128:row0 + t0 + (st + 1) * 128, :], reads=[], writes=[xk])
                norm_transpose(xt, xk, a_fm, b_fm, abk, hT3, hkeys, st, R)
            dma_st(hT_d.rearrange("k p s -> p k s")[:, :, t0:t0 + T], hT3, reads=hall, writes=[("hTd", tt)])
            rp, rpk = rope.next()
            rp4 = rp.rearrange("p (f t) -> p f t", f=4)
            dma_ld(rp4[:, 0:2, :], ropeB.rearrange("f p s -> p f s")[:, :, t0:t0 + T], reads=[], writes=[rpk])
            dma_ld(rp4[:, 2:4, :], ropeC.rearrange("f p s -> p f s")[:, :, t0:t0 + T], reads=[], writes=[rpk])
            return hT3, hkeys, hall, rp4, rpk

        cur = prep(0)
        for tt in range(ntl):
            t0 = tt * T
            hT3, hkeys, hall, rp4, rpk = cur
            nxt = None
            pending = None
            for blk in range(12):
                wt, wk = wr.next()
                wtb = wt.bitcast(BF16).rearrange("p (k n) -> p k n", k=KC)
                c0 = blk * 512
                dma_ld(wtb, win_d[l].rearrange("(k p) n -> p k n", p=128)[:, :, c0:c0 + 512],
                       reads=wkeys(l, "win", D), writes=[wk])
                for cc in range(4):
                    j = blk * 4 + cc
                    m = QK_MAP[j]
                    if m is None:
                        continue
                    typ, gcol, dj = m
                    b = next_bank(0, 6)
                    for k in range(KC):
                        pe_mm(ps[:, b, :], wtb[:, k, cc * 128:(cc + 1) * 128], hT3[:, k, :], k == 0, k == KC - 1,
                              reads=[wk] + hall, writes=[PSK(b)])
                    zp = ps[:, b, :]
                    g = v[:, gcol:gcol + 1]
                    sq, sqk = sqr.next()
                    act(sq, zp, AF.Square, reads=[PSK(b)], writes=[sqk])
                    xs = xsk = None
                    if typ != "A":
                        xs, xsk = xsr.next()
                        act(xs, zp, AF.Identity, reads=[PSK(b), ("vec", l)], writes=[xsk], scale=g)
                    if pending is not None:
                        pending()

                    def rest(typ=typ, dj=dj, b=b, zp=zp, g=g, sq=sq, sqk=sqk, xs=xs, xsk=xsk, rp4=rp4, rpk=rpk, t0=t0, tt=tt):
                        b2 = next_bank(0, 6)
                        onesm = ones64 if typ == "C" else ones128
                        nn = 64 if typ == "C" else 128
                        pe_mm(ps[:, b2, :], onesm, sq, True, True, reads=[sqk, KCONST], writes=[PSK(b2)])
                        if typ != "A":
                            b3 = next_bank(0, 6)
                            pm = permB if typ == "B" else permC
                            pe_mm(ps[:, b3, :], pm, xs, True, True, reads=[xsk, KCONST], writes=[PSK(b3)])
                        rs, rsk = rsr.next()
                        rstd_from(ps[:, b2, :], nn, rs[:, T:2 * T], [PSK(b2)], rsk, rs[:, 0:T], rsk)
                        rstd = rs[:, T:2 * T]
                        qo, qok = qor.next()
                        if typ == "A":
                            dve(lambda e: e.scalar_tensor_tensor(out=qo, in0=zp, scalar=g, in1=rstd, op0=ALU.mult, op1=ALU.mult),
                                reads=[PSK(b), rsk, ("vec", l)], writes=[qok])
                        else:
                            cosT = rp4[:, 0, :] if typ == "B" else rp4[:, 2, :]
                            sinT = rp4[:, 1, :] if typ == "B" else rp4[:, 3, :]
                            t1, t1k = t1r.next()
                            t2, t2k = t2r.next()
                            dve(lambda e: e.scalar_tensor_tensor(out=t1, in0=zp, scalar=g, in1=cosT, op0=ALU.mult, op1=ALU.mult),
                                reads=[PSK(b), rpk, ("vec", l)], writes=[t1k])
                            dve(lambda e: e.tensor_tensor(out=t2, in0=ps[:, b3, :], in1=sinT, op=ALU.mult),
                                reads=[PSK(b3), rpk], writes=[t2k])
                            dve(lambda e: e.tensor_tensor(out=t1, in0=t1, in1=t2, op=ALU.add), reads=[t1k, t2k], writes=[t1k])
                            dve(lambda e: e.tensor_tensor(out=qo, in0=t1, in1=rstd, op=ALU.mult), reads=[t1k, rsk], writes=[qok])
                        dma_st(qk_d[dj, :, PADK + t0:PADK + t0 + T], qo, reads=[qok], writes=[("qkd", dj, tt)])
                    pending = rest
                for (cin, ncol, dst, dcol) in V_SEGS.get(blk, []):
                    for st in range(4):
                        b = next_bank(0, 6)
                        for k in range(KC):
                            pe_mm(ps[:, b, 0:ncol], hT3[:, k, st * 128:(st + 1) * 128], wtb[:, k, cin:cin + ncol], k == 0, k == KC - 1,
                                  reads=[wk] + hkeys[st], writes=[PSK(b)])
                        if pending is not None:
                            pending()
                            pending = None
                        vo, vok = vor.next()
                        act(vo[:, 0:ncol], ps[:, b, 0:ncol], AF.Identity, reads=[PSK(b)], writes=[vok])
                        roff = PADK if dst is vb_d else 0
                        dma_st(dst[roff + t0 + st * 128:roff + t0 + (st + 1) * 128, dcol:dcol + ncol], vo[:, 0:ncol],
                               reads=[vok], writes=[("vd", blk, dcol, tt, st)])
                if blk == 6 and tt + 1 < ntl:
                    nxt = prep(tt + 1)
            if pending is not None:
                pending()
                pending = None
            cur = nxt
        S.barrier()

    def phase_B(l, Sq):
        B.off = persist_mark
        v = vecs[l]
        lt = lamt[l]
        sc128 = 128.0 ** -0.5
        sc64 = 64.0 ** -0.5
        mark = B.off
        R_rows = Sq // 64
        plan = na_plan(R_rows)
        QT = B.bf(Sq); KT = B.bf(Sq); Vt = B.bf(Sq); OT = B.bf(Sq)
        tab = B.f32(NT * 128)
        kQ, kK, kV, kO, kT = (B.key(n) for n in ("aQ", "aK", "aV", "aO", "aT"))
        Vt3 = Vt.rearrange("p (j c) -> p j c", c=128)
        sbr = Ring("asb", 2, 640, F32)
        ptr = Ring("apt", 2, 640, BF16)
        rzr = Ring("arz", 2, 512, F32)
        for h in range(4):
            dma_ld(QT, qk_d[h, :, PADK:PADK + Sq], reads=[], writes=[kQ])
            dma_ld(KT, qk_d[4 + h, :, PADK:PADK + Sq], reads=[], writes=[kK])
            dma_ld(Vt3, va_d[0:Sq, h * 128:(h + 1) * 128].rearrange("(j p) c -> p j c", p=128), reads=[], writes=[kV])
            dma_ld(tab, rpbx[l, h], reads=[], writes=[kT])
            npairs = R_rows // 2
            for i0 in range(0, npairs, 4):
                bO = 4 + (i0 // 4) % 2
                bZ = 6 + (i0 // 4) % 2
                for i in range(i0, i0 + 4):
                    js, sig = plan[i]
                    off, n = cases[sig]
                    bS = 2 * ((i) % 2)
                    for jj, j in enumerate(js):
                        pe_mm(ps[:, bS + jj // 4, (jj % 4) * 128:(jj % 4 + 1) * 128], KT[:, j * 128:(j + 1) * 128], QT[:, i * 128:(i + 1) * 128],
                              True, True, reads=[kK, kQ], writes=[PSK(bS + jj // 4)])
                    sb, sbk = sbr.next()
                    pt, ptk = ptr.next()
                    dve(lambda e, sb=sb, bS=bS, n=n, off=off: e.scalar_tensor_tensor(
                        out=sb[:, 0:n * 128], in0=psflat[:, bS * 512:bS * 512 + n * 128], scalar=sc128,
                        in1=tab[:, off * 128:(off + n) * 128], op0=ALU.mult, op1=ALU.add),
                        reads=[PSK(bS), PSK(bS + 1), kT], writes=[sbk])
                    act(pt[:, 0:n * 128], sb[:, 0:n * 128], AF.Exp, reads=[sbk], writes=[ptk])
                    sl = i - i0
                    for jj, j in enumerate(js):
                        pe_mm(ps[:, bO, sl * 128:(sl + 1) * 128], Vt3[:, j, :], pt[:, jj * 128:(jj + 1) * 128], jj == 0, jj == n - 1,
                              reads=[kV, ptk], writes=[PSK(bO)])
                    for jj, j in enumerate(js):
                        pe_mm(ps[:, bZ, sl * 128:(sl + 1) * 128], ones128, pt[:, jj * 128:(jj + 1) * 128], jj == 0, jj == n - 1,
                              reads=[KCONST, ptk], writes=[PSK(bZ)])
                rz, rzk = rzr.next()
                dve(lambda e, rz=rz, bZ=bZ: e.reciprocal(out=rz, in_=ps[:, bZ, :]), reads=[PSK(bZ)], writes=[rzk])
                dve(lambda e, rz=rz, bO=bO, i0=i0, OT=OT: e.tensor_tensor(out=OT[:, i0 * 128:(i0 + 4) * 128], in0=ps[:, bO, :], in1=rz, op=ALU.mult),
                    reads=[PSK(bO), rzk], writes=[kO])
            dma_st(oT_d[h, :, 0:Sq], OT, reads=[kO], writes=[("oTd", h)])
        S.barrier()
        B.off = mark
        QT = B.bf(Sq); KT = B.bf(Sq + 2 * PADK); OT = B.bf(Sq)
        QTr = B.bf(Sq); KTr = B.bf(Sq + 2 * PADK)
        accO = B.f32(Sq); accZ = B.f32(Sq)
        kQ, kK, kO, kaO, kaZ = (B.key(n) for n in ("bQ", "bK", "bO", "baO", "baZ"))
        kQr, kKr = B.key("bQr"), B.key("bKr")
        vtr = Ring("bvt", 2, 33 * 128, BF16)
        er = Ring("be", 2, 256, BF16)
        ptr = Ring("bpt", 2, 256, BF16)
        for hg in range(2):
            for g, d in enumerate((1, 4, 16)):
                cj = g * 2 + hg
                dma_ld(QT, qk_d[8 + cj, :, PADK:PADK + Sq], reads=[], writes=[kQ])
                dma_ld(KT, qk_d[14 + cj, :, 0:Sq + 2 * PADK], reads=[], writes=[kK])
                L = Sq // d
                nq = L // 128
                if d == 1:
                    Qv = QT.rearrange("p (d m) -> p d m", d=1)
                    Kv = KT[:, PADK - 64:PADK - 64 + L + 128].rearrange("p (d m) -> p d m", d=1)
                    kQu, kKu = kQ, kK
                else:
                    Qv = QTr.rearrange("p (d m) -> p d m", d=d)
                    Kv = KTr[:, 0:(L + 128) * d].rearrange("p (d m) -> p d m", d=d)
                    dve(lambda e, Qv=Qv, QT=QT, d=d: e.tensor_copy(out=Qv, in_=QT.rearrange("p (m d) -> p d m", d=d)),
                        reads=[kQ], writes=[kQr])
                    act(Kv, KT[:, PADK - 64 * d:PADK - 64 * d + (L + 128) * d].rearrange("p (m d) -> p d m", d=d), AF.Identity,
                        reads=[kK], writes=[kKr])
                    kQu, kKu = kQr, kKr
                for r in range(d):
                    vt, vtk = vtr.next()
                    vt3 = vt[:, 0:(nq + 1) * 128].rearrange("p (j c) -> p j c", c=128)
                    base = PADK - 64 * d + r
                    nrow = (nq + 1) * 128
                    vsrc = vb_d[base:base + (nrow - 1) * d + 1:d, cj * 128:(cj + 1) * 128].rearrange("(j p) c -> p j c", p=128)
                    dma_ld(vt3, vsrc, reads=[], writes=[vtk])
                    for i0 in range(0, nq, 4):
                        w = min(4, nq - i0)
                        bO = 4 + (i0 // 4) % 2
                        bZ = 6 + (i0 // 4) % 2
                        for i in range(i0, i0 + w):
                            bS = next_bank(0, 4)
                            rhs = Qv[:, r, 128 * i:128 * (i + 1)]
                            for t in range(2):
                                pe_mm(ps[:, bS, t * 128:(t + 1) * 128], Kv[:, r, 128 * (i + t):128 * (i + t + 1)], rhs, True, True,
                                      reads=[kKu, kQu], writes=[PSK(bS)])
                            ee, ek = er.next()
                            pt, ptk = ptr.next()
                            act(ee, ps[:, bS, 0:256], AF.Exp, reads=[PSK(bS)], writes=[ek], scale=sc128)
                            dve(lambda e, pt=pt, ee=ee: e.tensor_tensor(out=pt, in0=ee, in1=band, op=ALU.mult),
                                reads=[ek, KCONST], writes=[ptk])
                            sl = i - i0
                            for t in range(2):
                                pe_mm(ps[:, bO, sl * 128:(sl + 1) * 128], vt3[:, i + t, :], pt[:, t * 128:(t + 1) * 128], t == 0, t == 1,
                                      reads=[vtk, ptk], writes=[PSK(bO)])
                            for t in range(2):
                                jp = i + t
                                om = ones_hi if jp == 0 else (ones_lo if jp == nq else ones128)
                                pe_mm(ps[:, bZ, sl * 128:(sl + 1) * 128], om, pt[:, t * 128:(t + 1) * 128], t == 0, t == 1,
                                      reads=[KCONST, ptk], writes=[PSK(bZ)])
                        p0 = r + 128 * i0 * d
                        dO = accO[:, p0:p0 + (w * 128 - 1) * d + 1:d]
                        dZ = accZ[:, p0:p0 + (w * 128 - 1) * d + 1:d]
                        if g == 0:
                            act(dO, ps[:, bO, 0:w * 128], AF.Identity, reads=[PSK(bO)], writes=[kaO])
                            act(dZ, ps[:, bZ, 0:w * 128], AF.Identity, reads=[PSK(bZ)], writes=[kaZ])
                        else:
                            dve(lambda e, dO=dO, bO=bO, w=w: e.tensor_tensor(out=dO, in0=dO, in1=ps[:, bO, 0:w * 128], op=ALU.add),
                                reads=[PSK(bO), kaO], writes=[kaO])
                            dve(lambda e, dZ=dZ, bZ=bZ, w=w: e.tensor_tensor(out=dZ, in0=dZ, in1=ps[:, bZ, 0:w * 128], op=ALU.add),
                                reads=[PSK(bZ), kaZ], writes=[kaZ])
            dve(lambda e, accZ=accZ: e.reciprocal(out=accZ, in_=accZ), reads=[kaZ], writes=[kaZ])
            dve(lambda e, OT=OT, accO=accO, accZ=accZ: e.tensor_tensor(out=OT, in0=accO, in1=accZ, op=ALU.mult), reads=[kaO, kaZ], writes=[kO])
            dma_st(oT_d[4 + hg, :, 0:Sq], OT, reads=[kO], writes=[("oTd", 4 + hg)])
        S.barrier()
        B.off = mark
        csets = []
        for _ in range(2):
            Qx = B.bf(2 * Sq); KT = B.bf(Sq); Vt = B.bf(Sq); OT = B.bf(Sq)
            ks = tuple(B.key(n) for n in ("cQ", "cK", "cV", "cO"))
            dve(lambda e, Qx=Qx: e.memset(Qx, 0.0), reads=[], writes=[ks[0]])
            csets.append((Qx, KT, Vt, OT, ks))
        ptr = Ring("cpt", 5, 512, BF16)
        rzr = Ring("crz", 2, 512, F32)
        tr = Ring("ct", 2, 512, F32)
        ar = Ring("ca", 3, 256, F32)
        sqr = Ring("csq", 3, 256, BF16)
        rsr = Ring("crs", 2, 512, F32)
        PRE = 2
        nkt = Sq // 128
        nqb = Sq // 256

        def c_load(hc):
            Qx, KT, Vt, OT, (kQ, kK, kV, kO) = csets[hc % 2]
            Qx3 = Qx.rearrange("p (c s) -> p c s", c=2)
            Vt3 = Vt.rearrange("p (j c) -> p j c", c=128)
            dma_ld(Qx3[0:64, 0, :], qk_d[20 + hc, 0:64, PADK:PADK + Sq], reads=[], writes=[kQ])
            dma_ld(Qx3[64:128, 1, :], qk_d[20 + hc, 64:128, PADK:PADK + Sq], reads=[], writes=[kQ])
            dma_ld(KT, qk_d[26 + hc, :, PADK:PADK + Sq], reads=[], writes=[kK])
            dma_ld(Vt3, vc_d[0:Sq, hc * 128:(hc + 1) * 128].rearrange("(j p) c -> p j c", p=128), reads=[], writes=[kV])

        c_load(0)
        for hc in range(6):
            if hc + 1 < 6:
                c_load(hc + 1)
            Qx, KT, Vt, OT, (kQ, kK, kV, kO) = csets[hc % 2]
            Qx3 = Qx.rearrange("p (c s) -> p c s", c=2)
            Vt3 = Vt.rearrange("p (j c) -> p j c", c=128)
            items = [(qb, kt) for qb in range(nqb) for kt in range(nkt)]
            stash = {}
            pend = []
            for n in range(len(items) + PRE):
                if n < len(items):
                    qb, kt = items[n]
                    bS = next_bank(0, 3)
                    pe_mm(ps[:, bS, :].rearrange("p (c q) -> p c q", c=2), KT[:, kt * 128:(kt + 1) * 128], Qx3[:, :, qb * 256:(qb + 1) * 256],
                          True, True, reads=[kK, kQ], writes=[PSK(bS)])
                    pt, ptk = ptr.next()
                    act(pt, ps[:, bS, :], AF.Exp, reads=[PSK(bS)], writes=[ptk], scale=sc64)
                    stash[n] = (pt, ptk)
                m = n - PRE
                if m >= 0:
                    qb, kt = items[m]
                    pt, ptk = stash.pop(m)
                    bO = 3 + qb % 2
                    bZ = 5 + qb % 2
                    pe_mm(ps[:, bO, :], Vt3[:, kt, :], pt, kt == 0, kt == nkt - 1, reads=[kV, ptk], writes=[PSK(bO)])
                    pe_mm(ps[:, bZ, :], ones128, pt, kt == 0, kt == nkt - 1, reads=[KCONST, ptk], writes=[PSK(bZ)])
                    if kt == nkt - 1:
                        rz, rzk = rzr.next()
                        tt_, ttk = tr.next()
                        aa, aak = ar.next()
                        sq, sqk = sqr.next()
                        dve(lambda e, rz=rz, bZ=bZ: e.reciprocal(out=rz, in_=ps[:, bZ, :]), reads=[PSK(bZ)], writes=[rzk])
                        dve(lambda e, tt_=tt_, rz=rz, bO=bO: e.tensor_tensor(out=tt_, in0=ps[:, bO, :], in1=rz, op=ALU.mult),
                            reads=[PSK(bO), rzk], writes=[ttk])
                        dve(lambda e, aa=aa, tt_=tt_: e.scalar_tensor_tensor(out=aa, in0=tt_[:, 256:512], scalar=lt[:, 0:1], in1=tt_[:, 0:256],
                                                                              op0=ALU.mult, op1=ALU.add),
                            reads=[ttk, ("lam", l, 0)], writes=[aak])
                        act(sq, aa, AF.Square, reads=[aak], writes=[sqk])

                        def part2(aa=aa, aak=aak, sq=sq, sqk=sqk, qb=qb, OT=OT, kO=kO):
                            rs, rsk = rsr.next()
                            pe_mm(ps[:, 7, 0:256], ones128, sq, True, True, reads=[KCONST, sqk], writes=[PSK(7)])
                            rstd_from(ps[:, 7, 0:256], 128, rs[:, 256:512], [PSK(7)], rsk, rs[:, 0:256], rsk)
                            dve(lambda e: e.scalar_tensor_tensor(out=OT[:, qb * 256:(qb + 1) * 256], in0=aa, scalar=lt[:, 1:2],
                                                                 in1=rs[:, 256:512], op0=ALU.mult, op1=ALU.mult),
                                reads=[aak, rsk, ("lam", l, 1)], writes=[kO])
                        pend.append((n + 6, part2))
                while pend and pend[0][0] <= n:
                    pend.pop(0)[1]()
            for _, fn in pend:
                fn()
            dma_st(oT_d[6 + hc, :, 0:Sq], OT, reads=[kO], writes=[("oTd", 6 + hc)])
        S.barrier()

    def phase_C1(l, s, row0, Sq, xsrc):
        B.off = persist_mark
        v = vecs[l]
        g1bc = B.f32(D); kg1 = B.key("g1bc")
        GATE1 = True
        OTt = B.bf(12 * T); kOT = B.key("OTt")
        OT3 = OTt.rearrange("p (c t) -> p c t", c=12)
        hT = B.bf(KC * T); khT = B.key("hTt")
        hT3 = hT.rearrange("p (k t) -> p k t", k=KC)
        mT = B.bf(KC * T)
        mT3 = mT.rearrange("p (k t) -> p k t", k=KC)
        kmT = [B.key("mT") for _ in range(KC)]
        xts = [(B.f32(D), B.key("xt")) for _ in range(4)]
        gate_bcast(l, s, 2, g1bc, kg1, xts[0][0], xts[0][1])
        wsr = Ring("wsm", 3, (48 + 12) * 128 // 2, F32)
        wbr = Ring("wbig", 2, KC * 512 // 2, F32)
        sgr = Ring("sg", 4, T, F32)
        mr = Ring("m", 3, T, F32)
        tr = Ring("tmp", 2, T, F32)
        def c1_loads(tt):
            t0 = tt * T
            dma_ld(OT3, oT_d.rearrange("c p s -> p c s")[:, :, t0:t0 + T], reads=[], writes=[kOT])
            dma_ld(hT3, hT_d.rearrange("k p s -> p k s")[:, :, t0:t0 + T], reads=[], writes=[khT])

        c1_loads(0)
        for tt in range(Sq // T):
            t0 = tt * T
            for st in range(4):
                dma_ld(xts[st][0], xsrc[row0 + t0 + st * 128:row0 + t0 + (st + 1) * 128, :], reads=[], writes=[xts[st][1]])
            for dc in range(KC):
                wt, wk = wsr.next()
                wtb = wt.bitcast(BF16)
                wg = wtb[:, 0:48 * 128].rearrange("p (g k n) -> p g k n", g=3, k=KC)
                wb = wtb[:, 48 * 128:60 * 128].rearrange("p (k n) -> p k n", k=12)
                dma_ld(wg, wgate_d[l].rearrange("(k p) (g n) -> p g k n", p=128, g=3)[:, :, :, dc * 128:(dc + 1) * 128],
                       reads=wkeys(l, "wgate", D), writes=[wk])
                dma_ld(wb, wbr_d[l].rearrange("(k p) n -> p k n", p=128)[:, :, dc * 128:(dc + 1) * 128],
                       reads=wkeys(l, "wbra", 512) + wkeys(l, "wbrb", 256) + wkeys(l, "wbrc", 768), writes=[wk])
                kcs = [(0, 4), (4, 6), (6, 12)]
                mcur = None
                for gi in range(3):
                    bg = next_bank()
                    for k in range(KC):
                        pe_mm(ps[:, bg, :], wg[:, gi, k, :], hT3[:, k, :], k == 0, k == KC - 1, reads=[wk, khT], writes=[PSK(bg)])
                    by = next_bank()
                    k0, k1 = kcs[gi]
                    for k in range(k0, k1):
                        pe_mm(ps[:, by, :], wb[:, k, :], OT3[:, k, :], k == k0, k == k1 - 1, reads=[wk, kOT], writes=[PSK(by)])
                    sg, sgk = sgr.next()
                    act(sg, ps[:, bg, :], AF.Sigmoid, reads=[PSK(bg), ("vec", l)], writes=[sgk],
                        bias=v[:, V_BG + gi * 16 + dc:V_BG + gi * 16 + dc + 1])
                    mm_, mk = mr.next()
                    dve(lambda e, mm_=mm_, sg=sg, by=by: e.tensor_tensor(out=mm_, in0=ps[:, by, :], in1=sg, op=ALU.mult),
                        reads=[PSK(by), sgk], writes=[mk])
                    if gi == 0:
                        mcur = (mm_, mk)
                    elif gi == 1:
                        dve(lambda e, a=mcur[0], b_=mm_: e.tensor_tensor(out=a, in0=a, in1=b_, op=ALU.add),
                            reads=[mcur[1], mk], writes=[mcur[1]])
                    else:
                        dve(lambda e, a=mcur[0], b_=mm_, dc=dc: e.tensor_tensor(out=mT3[:, dc, :], in0=a, in1=b_, op=ALU.add),
                            reads=[mcur[1], mk], writes=[kmT[dc]])
            if tt + 1 < Sq // T:
                c1_loads(tt + 1)
            for cbk in range(4):
                wt, wk = wbr.next()
                wtb = wt.bitcast(BF16).rearrange("p (k n) -> p k n", k=KC)
                dma_ld(wtb, wout_d[l].rearrange("(k p) n -> p k n", p=128)[:, :, cbk * 512:(cbk + 1) * 512],
                       reads=wkeys(l, "wout", D), writes=[wk])
                for st in range(4):
                    b = next_bank()
                    for k in range(KC):
                        pe_mm(ps[:, b, :], mT3[:, k, st * 128:(st + 1) * 128], wtb[:, k, :], k == 0, k == KC - 1,
                              reads=[wk, kmT[k]], writes=[PSK(b)])
                    tm, tk = tr.next()
                    xt, xk = xts[st]
                    dve(lambda e, tm=tm, b=b, cbk=cbk: e.tensor_tensor(out=tm, in0=ps[:, b, :], in1=g1bc[:, cbk * 512:(cbk + 1) * 512], op=ALU.mult),
                        reads=[PSK(b), kg1], writes=[tk])
                    dve(lambda e, tm=tm, xt=xt, cbk=cbk: e.tensor_tensor(out=xt[:, cbk * 512:(cbk + 1) * 512], in0=xt[:, cbk * 512:(cbk + 1) * 512],
                                                                          in1=tm, op=ALU.add),
                        reads=[tk, xk], writes=[xk])
            for st in range(4):
                dma_st(xmid_d[t0 + st * 128:t0 + (st + 1) * 128, :], xts[st][0], reads=[xts[st][1]], writes=[("xmid", tt, st)])
        S.barrier()

    def phase_C2(l, s, row0, Sq, xdst):
        B.off = persist_mark
        a_fm = B.f32(KC); b_fm = B.f32(KC); abk = B.key("ab2")
        load_mod_fm(l, s, 3, 4, V_N2G, a_fm, b_fm, abk)
        g2bc = B.f32(D); kg2 = B.key("g2bc")
        GATE2 = True
        R = {"junk": Ring("junk", 1, D, BF16), "ss": Ring("ss", 3, 4, F32), "xn": Ring("xn", 2, D, BF16),
             "tb": (0, 2), "tbi": [0]}
        xts = [(B.f32(D), B.key("xt")) for _ in range(4)]
        gate_bcast(l, s, 5, g2bc, kg2, xts[0][0], xts[0][1])
        hT = B.bf(KC * T)
        hT3 = hT.rearrange("p (k t) -> p k t", k=KC)
        hkeys = [[B.key("h2T"), B.key("h2T")] for _ in range(4)]
        hall = [k for st in range(4) for k in hkeys[st]]
        f1 = B.bf(64 * T)
        f13 = f1.rearrange("p (c t) -> p c t", c=64)
        kf1 = [B.key("f1") for _ in range(64)]
        wbr = Ring("wbig", 3, KC * 512 // 2, F32)
        rr = Ring("relu", 3, T, BF16)
        tr = Ring("tmp", 2, T, F32)
        for tt in range(Sq // T):
            t0 = tt * T
            for st in range(4):
                xt, xk = xts[st]
                dma_ld(xt, xmid_d[t0 + st * 128:t0 + (st + 1) * 128, :], reads=[], writes=[xk])
                norm_transpose(xt, xk, a_fm, b_fm, abk, hT3, hkeys, st, R)
            for blk in range(16):
                wt, wk = wbr.next()
                wtb = wt.bitcast(BF16).rearrange("p (k n) -> p k n", k=KC)
                dma_ld(wtb, wff1_d[l].rearrange("(k p) n -> p k n", p=128)[:, :, blk * 512:(blk + 1) * 512],
                       reads=wkeys(l, "wff1", D), writes=[wk])
                for cc in range(4):
                    c = blk * 4 + cc
                    b = next_bank(0, 4)
                    for k in range(KC):
                        pe_mm(ps[:, b, :], wtb[:, k, cc * 128:(cc + 1) * 128], hT3[:, k, :], k == 0, k == KC - 1,
                              reads=[wk] + hall, writes=[PSK(b)])
                    rl, rk = rr.next()
                    act(rl, ps[:, b, :], AF.Relu, reads=[PSK(b)], writes=[rk])
                    dve(lambda e, rl=rl, c=c: e.tensor_tensor(out=f13[:, c, :], in0=rl, in1=rl, op=ALU.mult),
                        reads=[rk], writes=[kf1[c]])
            for cbk in range(4):
                for kg in range(4):
                    wt, wk = wbr.next()
                    wtb = wt.bitcast(BF16).rearrange("p (k n) -> p k n", k=KC)
                    dma_ld(wtb, wff2_d[l, kg * 2048:(kg + 1) * 2048, :].rearrange("(k p) n -> p k n", p=128)[:, :, cbk * 512:(cbk + 1) * 512],
                           reads=wkeys(l, "wff2", DFF), writes=[wk])
                    for st in range(4):
                        b = 4 + st
                        for k in range(KC):
                            c = kg * 16 + k
                            pe_mm(ps[:, b, :], f13[:, c, st * 128:(st + 1) * 128], wtb[:, k, :], c == 0, c == 63,
                                  reads=[wk, kf1[c]], writes=[PSK(b)])
                for st in range(4):
                    b = 4 + st
                    tm, tk = tr.next()
                    xt, xk = xts[st]
                    dve(lambda e, tm=tm, b=b, cbk=cbk: e.tensor_tensor(out=tm, in0=ps[:, b, :], in1=g2bc[:, cbk * 512:(cbk + 1) * 512], op=ALU.mult),
                        reads=[PSK(b), kg2], writes=[tk])
                    dve(lambda e, tm=tm, xt=xt, cbk=cbk: e.tensor_tensor(out=xt[:, cbk * 512:(cbk + 1) * 512], in0=xt[:, cbk * 512:(cbk + 1) * 512],
                                                                          in1=tm, op=ALU.add),
                        reads=[tk, xk], writes=[xk])
            for st in range(4):
                dma_st(xdst[row0 + t0 + st * 128:row0 + t0 + (st + 1) * 128, :], xts[st][0], reads=[xts[st][1]], writes=[("xout", tt, st)])
        S.barrier()

    issue_casts(0, True)
    if stop >= 1:
        prologue()
    if stop >= 2:
        issue_casts(0, False)
    for l in range(n_layers if stop >= 2 else 0):
        xsrc = xin if l == 0 else x1_d
        xdst = yout if l == n_layers - 1 else x1_d
        row0 = 0
        for s, Sq in enumerate(seqs):
            phase_A(l, s, row0, Sq, xsrc)
            if l == 0 and s == 0 and n_layers > 1:
                issue_casts(1, True)
                issue_casts(1, False)
            if stop >= 3:
                phase_B(l, Sq)
            if stop >= 4:
                phase_C1(l, s, row0, Sq, xsrc)
            if stop >= 5:
                phase_C2(l, s, row0, Sq, xdst)
            row0 += Sq
    S.barrier(include_cast=True)
    S.finalize()

    keys = S.sem_keys()
    sems = {}
    for i, k in enumerate(keys):
        sems[k] = stack.enter_context(nc.semaphore("s%d" % i))
    block = stack.enter_context(nc.Block())

    @block.tensor
    def _(e):
        S.run("pe", e, sems)

    @block.scalar
    def _(e):
        S.run("act", e, sems)

    @block.vector
    def _(e):
        S.run("dve", e, sems)

    @block.gpsimd
    def _(e):
        S.run("pool", e, sems)

    @block.sync
    def _(e):
        S.run("sp", e, sems)

    stack.close()
    return nc


def host_consts(smax):
    bf = ml_dtypes.bfloat16
    cb = np.zeros((128, 8 * 128 + 256), np.float32)
    cb[:, 0:128] = np.eye(128)
    cb[:, 128:256] = 1.0
    cb[0:64, 256:320] = 1.0
    cb[64:128, 320:384] = 1.0
    cb[0:64, 384:512] = 1.0
    cb[64:128, 512:640] = 1.0
    pB = np.zeros((128, 128), np.float32)
    for d in range(16):
        pB[d + 16, d] = 1.0
        pB[d, d + 16] = 1.0
    cb[:, 640:768] = pB
    pC = np.zeros((128, 128), np.float32)
    for blk in range(2):
        for d in range(8):
            pC[blk * 64 + d + 8, blk * 64 + d] = 1.0
            pC[blk * 64 + d, blk * 64 + d + 8] = 1.0
    cb[:, 768:896] = pC
    p = np.arange(128)[:, None]
    c = np.arange(128)[None, :]
    cb[:, 1024:1152] = (p >= c)
    cb[:, 1152:1280] = (p <= c)
    pos = np.arange(smax, dtype=np.float32)

    def rope_tab(dh, nblk):
        rot = dh // 4
        half = rot // 2
        inv = (500000.0 ** (-np.arange(half, dtype=np.float32) / half)).astype(np.float32)
        ang = pos[None, :] * inv[:, None]
        cos, sin = np.cos(ang).astype(np.float32), np.sin(ang).astype(np.float32)
        t = np.zeros((2, 128, smax), np.float32)
        t[0] = 1.0
        for b in range(nblk):
            o = b * dh
            t[0, o:o + half] = cos
            t[0, o + half:o + rot] = cos
            t[1, o:o + half] = -sin
            t[1, o + half:o + rot] = sin
        return t

    return cb.astype(bf), rope_tab(128, 1), rope_tab(64, 2)


def host_inputs(inp, seq_sel, n_layers, smax):
    cbf, ropeB, ropeC = host_consts(smax)
    xs, cs = [], []
    for kind, idx in seq_sel:
        if kind == "p":
            xs.append(np.asarray(inp["x_prompt"][idx]))
            cs.append(np.asarray(inp["c_prompt"][idx]))
        else:
            xs.append(np.asarray(inp["x_sample"][idx]))
            cs.append(np.asarray(inp["c_sample"][idx]))
    xin = np.ascontiguousarray(np.concatenate(xs, axis=0), dtype=np.float32)
    cmat = np.stack(cs, 0)
    cT = np.ascontiguousarray(cmat.reshape(len(cs), KC, 128).transpose(2, 1, 0), dtype=np.float32)
    return xin, cT, cbf, ropeB, ropeC


def shared_inputs(inp, n_layers, nseq):
    L = n_layers
    f = lambda k: np.ascontiguousarray(np.asarray(inp[k])[:L], dtype=np.float32)
    vec = np.zeros((L, 128, NV), np.float32)
    for l in range(L):
        vec[l, :, V_N1G:V_N1G + 16] = np.asarray(inp["norm1_g"][l]).reshape(16, 128).T
        vec[l, :, V_N2G:V_N2G + 16] = np.asarray(inp["norm2_g"][l]).reshape(16, 128).T
        vec[l, :, V_BG:V_BG + 48] = np.asarray(inp["b_gate"][l]).reshape(48, 128).T
        vec[l, :, V_QNA] = np.asarray(inp["qn_a"][l])
        vec[l, :, V_KNA] = np.asarray(inp["kn_a"][l])
        vec[l, :, V_QNB] = np.asarray(inp["qn_b"][l])
        vec[l, :, V_KNB] = np.asarray(inp["kn_b"][l])
        vec[l, :, V_QNC] = np.tile(np.asarray(inp["qn_c"][l]), 2)
        vec[l, :, V_KNC] = np.tile(np.asarray(inp["kn_c"][l]), 2)
        vec[l, :, V_SUB] = np.asarray(inp["subln_c"][l])
        vec[l, :, V_BADA:V_BADA + 96] = np.asarray(inp["b_ada"][l]).reshape(96, 128).T
        for i, k in enumerate(("lam_q1", "lam_k1", "lam_q2", "lam_k2")):
            vec[l, :, V_LAM + 64 * i:V_LAM + 64 * (i + 1)] = np.asarray(inp[k][l])[None, :]
    idx = na_index_table()
    rpb = np.asarray(inp["rpb_a"])[:L].astype(np.float32)
    flat = rpb.reshape(L, 4, 15 * 31)
    rpbx = np.where(idx[None, None] >= 0, flat[:, :, np.clip(idx, 0, None)], np.float32(-1.0e4)).astype(np.float32)
    b_ada3 = np.ascontiguousarray(np.broadcast_to(np.asarray(inp["b_ada"])[:L, None, :], (L, nseq, 6 * D)), dtype=np.float32)
    sh = {"w_ada": f("w_ada"), "b_ada3": b_ada3, "w_in": f("w_in"), "w_gate": f("w_gate"), "w_br_a": f("w_br_a"),
          "w_br_b": f("w_br_b"), "w_br_c": f("w_br_c"), "w_out": f("w_out"), "w_ff1": f("w_ff1"), "w_ff2": f("w_ff2"),
          "vec_fm": vec, "rpbx": np.ascontiguousarray(rpbx),
          "zrow": np.zeros((128, 1024), ml_dtypes.bfloat16),
          "selc": np.ascontiguousarray(np.repeat(np.eye(4, dtype=np.float32), 128, axis=1))}
    return sh


_NC_CACHE = {}


def kernel(**inp):
    n_cores = 8
    seqs = (4096, 2048, 2048)
    key = ("full",)
    if key not in _NC_CACHE:
        _NC_CACHE[key] = build(seqs, 2)
    nc = _NC_CACHE[key]
    sh = shared_inputs(inp, 2, 3)
    in_maps = []
    for c in range(n_cores):
        xin, cT, cbf, ropeB, ropeC = host_inputs(inp, [("p", c), ("s", 2 * c), ("s", 2 * c + 1)], 2, 4096)
        m = dict(sh)
        m.update({"xin": xin, "cT": cT, "cbf": cbf, "ropeB": ropeB, "ropeC": ropeC})
        in_maps.append(m)
    res = run_bass_kernel_spmd(nc, in_maps, core_ids=list(range(n_cores)))
    yp = np.empty((8, 4096, D), np.float32)
    ys = np.empty((16, 2048, D), np.float32)
    for c in range(n_cores):
        y = res.results[c]["yout"]
        yp[c] = y[0:4096]
        ys[2 * c] = y[4096:6144]
        ys[2 * c + 1] = y[6144:8192]
    return (yp, ys)
```
